# Optimizing a Trainium2 kernel written in Bass

```python
import math
import jax
import jax.numpy as jnp
from jax import lax
import numpy as np

D_MODEL = 1024
BATCH = 8
SEQ = 4096
DEPTH = 1

GRID_W = 64
CTX_LEN = 256
F_GROUPS = 4
F_GROUP_DIM = 128
F_WIDTH = F_GROUPS * F_GROUP_DIM
DN_HEADS = 8
DN_HEAD_DIM = 128
DN_WIDTH = DN_HEADS * DN_HEAD_DIM
CONV_K = 3
CHUNK = 64
N_DIR = 2
IN_SPLITS = (F_WIDTH, F_WIDTH, 3 * DN_WIDTH, DN_WIDTH, N_DIR * DN_HEADS, N_DIR * DN_HEADS, D_MODEL, D_MODEL)
IN_COLS = sum(IN_SPLITS)
IN_OFFSETS = tuple(int(o) for o in np.cumsum(IN_SPLITS)[:-1])
DEEPNORM_ALPHA = (2 * DEPTH) ** 0.25
DEEPNORM_BETA = (8 * DEPTH) ** -0.25
EPS = 1e-6

kernel_name = 'hybrid_fourier_deltanet_dit'


def layer_norm(x):
    xf = x.astype(jnp.float32)
    mu = jnp.mean(xf, axis=-1, keepdims=True)
    var = jnp.mean(jnp.square(xf - mu), axis=-1, keepdims=True)
    return ((xf - mu) * lax.rsqrt(var + EPS)).astype(x.dtype)


def l2_normalize(x):
    xf = x.astype(jnp.float32)
    return xf * lax.rsqrt(jnp.sum(xf * xf, axis=-1, keepdims=True) + EPS)


def sincos_2d(rows, cols, dim, dtype):
    quarter = dim // 4
    omega = 1.0 / (10000.0 ** (jnp.arange(quarter, dtype=jnp.float32) / quarter))
    pr = jnp.arange(rows, dtype=jnp.float32)[:, None] * omega
    pc = jnp.arange(cols, dtype=jnp.float32)[:, None] * omega
    er = jnp.concatenate([jnp.sin(pr), jnp.cos(pr)], axis=-1)
    ec = jnp.concatenate([jnp.sin(pc), jnp.cos(pc)], axis=-1)
    pe = jnp.concatenate([jnp.broadcast_to(er[:, None, :], (rows, cols, dim // 2)),
                          jnp.broadcast_to(ec[None, :, :], (rows, cols, dim // 2))], axis=-1)
    return pe.reshape(rows * cols, dim).astype(dtype)


def centred_depthwise_conv(x, w):
    n = x.shape[1]
    pad = w.shape[0] // 2
    xp = jnp.pad(x, ((0, 0), (pad, pad), (0, 0)))
    out = xp[:, 0:n] * w[0]
    for j in range(1, w.shape[0]):
        out = out + xp[:, j:j + n] * w[j]
    return out


def gated_delta_chunked(q, k, v, beta, g, s0):
    f32 = jnp.float32
    bsz, n, h, dk = q.shape
    dv = v.shape[-1]
    nc = n // CHUNK

    def to_chunks(t):
        t = t.astype(f32).reshape((bsz, nc, CHUNK) + t.shape[2:])
        return jnp.moveaxis(t, 3, 1)

    q = to_chunks(q) * (dk ** -0.5)
    k = to_chunks(k)
    v = to_chunks(v)
    beta = to_chunks(beta)
    gc = jnp.cumsum(to_chunks(g), axis=-1)
    idx = jnp.arange(CHUNK)
    tril = idx[:, None] >= idx[None, :]
    strict = idx[:, None] > idx[None, :]
    decay = jnp.exp(jnp.where(tril, gc[..., :, None] - gc[..., None, :], -jnp.inf))
    kb = k * beta[..., None]
    a_mat = jnp.where(strict, jnp.einsum('bhnid,bhnjd->bhnij', kb, k) * decay, 0.0) + jnp.eye(CHUNK, dtype=f32)
    rhs = jnp.concatenate([v * beta[..., None], kb * jnp.exp(gc)[..., None]], axis=-1)
    sol = lax.linalg.triangular_solve(a_mat, rhs, left_side=True, lower=True, unit_diagonal=True)
    u, w = sol[..., :dv], sol[..., dv:]
    qk = jnp.where(tril, jnp.einsum('bhnid,bhnjd->bhnij', q, k) * decay, 0.0)

    def step(s, xs):
        q_c, k_c, u_c, w_c, qk_c, gc_c = xs
        v_new = u_c - jnp.einsum('bhcd,bhde->bhce', w_c, s)
        o_c = (jnp.einsum('bhcd,bhde->bhce', q_c * jnp.exp(gc_c)[..., None], s)
               + jnp.einsum('bhij,bhje->bhie', qk_c, v_new))
        g_last = gc_c[..., -1]
        k_dec = k_c * jnp.exp(g_last[..., None] - gc_c)[..., None]
        s = s * jnp.exp(g_last)[..., None, None] + jnp.einsum('bhcd,bhce->bhde', k_dec, v_new)
        return s, o_c

    xs = tuple(jnp.moveaxis(t, 2, 0) for t in (q, k, u, w, qk, gc))
    s_final, o = lax.scan(step, s0.astype(f32), xs)
    o = jnp.transpose(o, (1, 0, 3, 2, 4)).reshape(bsz, n, h, dv)
    return o, s_final


def deltanet_branch(qkv, dn_gate, beta_raw, decay_raw, conv_w, a_log, dt_bias, norm_w, w_dn_out, s0_f, s0_b):
    bsz, n, _ = qkv.shape
    qkv = jax.nn.silu(centred_depthwise_conv(qkv, conv_w))
    q, k, v = jnp.split(qkv, 3, axis=-1)
    q = l2_normalize(q.reshape(bsz, n, DN_HEADS, DN_HEAD_DIM))
    k = l2_normalize(k.reshape(bsz, n, DN_HEADS, DN_HEAD_DIM))
    v = v.reshape(bsz, n, DN_HEADS, DN_HEAD_DIM)
    beta = jax.nn.sigmoid(beta_raw.astype(jnp.float32)).reshape(bsz, n, N_DIR, DN_HEADS)
    g = -jnp.exp(a_log.astype(jnp.float32)) * jax.nn.softplus(
        decay_raw.astype(jnp.float32).reshape(bsz, n, N_DIR, DN_HEADS) + dt_bias.astype(jnp.float32))
    o_f, s_f = gated_delta_chunked(q, k, v, beta[:, :, 0], g[:, :, 0], s0_f)
    flip = lambda t: jnp.flip(t, axis=1)
    o_b, s_b = gated_delta_chunked(flip(q), flip(k), flip(v), flip(beta[:, :, 1]), flip(g[:, :, 1]), s0_b)
    o = o_f + flip(o_b)
    o = o * lax.rsqrt(jnp.mean(o * o, axis=-1, keepdims=True) + EPS) * norm_w.astype(jnp.float32)
    o = o.astype(qkv.dtype).reshape(bsz, n, DN_WIDTH) * jax.nn.silu(dn_gate)
    return o @ w_dn_out, s_f, s_b


def fourier_branch(f_val, f_gate, w_fmix, w_f_out):
    bsz, n, _ = f_val.shape
    u = f_val.astype(jnp.float32).reshape(bsz, n, F_GROUPS, F_GROUP_DIM)
    mixed = jnp.fft.fft2(u, axes=(1, 3), norm='ortho').real.astype(f_val.dtype)
    mixed = jnp.einsum('bngc,gcd->bngd', mixed, w_fmix).reshape(bsz, n, F_WIDTH)
    return (mixed * jax.nn.silu(f_gate)) @ w_f_out


def merge_branches(r_f, r_d, y_f, y_d, w_out):
    return (jax.nn.sigmoid(r_f) * y_f + jax.nn.sigmoid(r_d) * y_d) @ w_out


def setup_inputs(seed: int = 0) -> dict:
    key = jax.random.key(seed)
    ks = jax.random.split(key, 17)
    f32 = jnp.float32

    def nrm(k, shape, scale):
        return jax.random.normal(k, shape, f32) * scale

    x = nrm(ks[0], (BATCH, SEQ, D_MODEL), 1.0)
    c = nrm(ks[1], (BATCH, D_MODEL), 1.0)
    ctx = nrm(ks[2], (BATCH, CTX_LEN, D_MODEL), 1.0)
    c_ctx = nrm(ks[3], (D_MODEL,), 1.0)
    w_mod = nrm(ks[4], (DEPTH, D_MODEL, 3 * D_MODEL), 0.5 * D_MODEL ** -0.5)
    b_mod = nrm(ks[5], (DEPTH, 3 * D_MODEL), 0.02)
    w_in = nrm(ks[6], (DEPTH, D_MODEL, IN_COLS), D_MODEL ** -0.5)
    conv_w = nrm(ks[7], (DEPTH, CONV_K, 3 * DN_WIDTH), CONV_K ** -0.5)
    a_log = jnp.log(jax.random.uniform(ks[8], (DEPTH, N_DIR, DN_HEADS), f32, 1.0, 16.0))
    dt = jnp.exp(jax.random.uniform(ks[9], (DEPTH, N_DIR, DN_HEADS), f32, math.log(1e-3), math.log(1e-1)))
    dt_bias = dt + jnp.log(-jnp.expm1(-dt))
    dn_norm_w = 1.0 + nrm(ks[10], (DEPTH, DN_HEAD_DIM), 0.02)
    w_dn_out = nrm(ks[11], (DEPTH, DN_WIDTH, D_MODEL), DEEPNORM_BETA * DN_WIDTH ** -0.5)
    w_fmix = nrm(ks[12], (DEPTH, F_GROUPS, F_GROUP_DIM, F_GROUP_DIM), F_GROUP_DIM ** -0.5)
    w_f_out = nrm(ks[13], (DEPTH, F_WIDTH, D_MODEL), DEEPNORM_BETA * F_WIDTH ** -0.5)
    w_out = nrm(ks[14], (DEPTH, D_MODEL, D_MODEL), DEEPNORM_BETA * D_MODEL ** -0.5)
    ln_g = 1.0 + nrm(ks[15], (DEPTH, D_MODEL), 0.02)
    ln_b = nrm(ks[16], (DEPTH, D_MODEL), 0.02)
    return {'x': x, 'c': c, 'ctx': ctx, 'c_ctx': c_ctx, 'w_mod': w_mod, 'b_mod': b_mod,
            'w_in': w_in, 'conv_w': conv_w, 'a_log': a_log, 'dt_bias': dt_bias,
            'dn_norm_w': dn_norm_w, 'w_dn_out': w_dn_out, 'w_fmix': w_fmix, 'w_f_out': w_f_out,
            'w_out': w_out, 'ln_g': ln_g, 'ln_b': ln_b}


def reference(x, c, ctx, c_ctx, w_mod, b_mod, w_in, conv_w, a_log, dt_bias, dn_norm_w,
              w_dn_out, w_fmix, w_f_out, w_out, ln_g, ln_b):
    n_lat = x.shape[1]
    rows = n_lat // GRID_W
    x = x + sincos_2d(rows, GRID_W, D_MODEL, x.dtype)[None]
    for l in range(DEPTH):
        mod_x = jax.nn.silu(c) @ w_mod[l] + b_mod[l]
        mod_c = jax.nn.silu(c_ctx) @ w_mod[l] + b_mod[l]
        shift_x, scale_x, gate_x = jnp.split(mod_x[:, None, :], 3, axis=-1)
        shift_c, scale_c, gate_c = jnp.split(mod_c, 3, axis=-1)
        h_x = layer_norm(x) * (1.0 + scale_x) + shift_x
        h_c = layer_norm(ctx) * (1.0 + scale_c) + shift_c
        p_x = jnp.split(h_x @ w_in[l], IN_OFFSETS, axis=-1)
        p_c = jnp.split(h_c @ w_in[l], IN_OFFSETS, axis=-1)
        zero_state = jnp.zeros((ctx.shape[0], DN_HEADS, DN_HEAD_DIM, DN_HEAD_DIM), jnp.float32)
        y_dn_c, s_f, s_b = deltanet_branch(p_c[2], p_c[3], p_c[4], p_c[5], conv_w[l], a_log[l], dt_bias[l],
                                           dn_norm_w[l], w_dn_out[l], zero_state, zero_state)
        y_dn_x, _, _ = deltanet_branch(p_x[2], p_x[3], p_x[4], p_x[5], conv_w[l], a_log[l], dt_bias[l],
                                       dn_norm_w[l], w_dn_out[l], s_f, s_b)
        y_f_x = fourier_branch(p_x[0], p_x[1], w_fmix[l], w_f_out[l])
        out_x = merge_branches(p_x[6], p_x[7], y_f_x, y_dn_x, w_out[l])
        x_next = layer_norm(DEEPNORM_ALPHA * x + gate_x * out_x) * ln_g[l] + ln_b[l]
        if l + 1 < DEPTH:
            y_f_c = fourier_branch(p_c[0], p_c[1], w_fmix[l], w_f_out[l])
            out_c = merge_branches(p_c[6], p_c[7], y_f_c, y_dn_c, w_out[l])
            ctx = layer_norm(DEEPNORM_ALPHA * ctx + gate_c * out_c) * ln_g[l] + ln_b[l]
        x = x_next
    return x
```

```python
from contextlib import ExitStack
import numpy as np
import ml_dtypes
import concourse.bass as bass
import concourse.mybir as mybir
from concourse.bass_utils import run_bass_kernel_spmd

F32 = mybir.dt.float32
BF16 = mybir.dt.bfloat16
AF = mybir.ActivationFunctionType
ALU = mybir.AluOpType
AX = mybir.AxisListType

D = 1024
NLAT = 4096
NCTX = 256
NTOK = NCTX + NLAT
H = 8
IN_SPLITS = (512, 512, 3072, 1024, 16, 16, 1024, 1024)
OFF = [0]
for _s in IN_SPLITS:
    OFF.append(OFF[-1] + _s)
INCOLS = OFF[-1]
EPS = 1e-6
ALPHA = 2.0 ** 0.25
NEG = -30000.0
import os
NOEXCL = bool(os.environ.get('NOEXCL'))


class Sched:
    def __init__(self, nc):
        self.nc = nc
        self.ops = []
        self.barriers = []

    def barrier(self):
        self.barriers.append(len(self.ops))

    def add(self, eng, fn, reads=(), writes=(), dma=False, semkey=None, waitall=False):
        self.ops.append(dict(eng=eng, fn=fn, reads=tuple(reads), writes=tuple(writes), dma=dma,
                             semkey=semkey, waitall=waitall))

    def pe(self, fn, r, w):
        self.add('pe', fn, r, w)

    def act(self, fn, r, w):
        self.add('act', fn, r, w)

    def dve(self, fn, r, w):
        self.add('dve', fn, r, w)

    def pool(self, fn, r, w):
        self.add('pool', fn, r, w)

    def dma(self, out, in_, r, w, eng='sp', semkey=None, waitall=False):
        self.add(eng, lambda e: e.dma_start(out=out, in_=in_), r, w, dma=True, semkey=semkey, waitall=waitall)

    def emit(self):
        nc = self.nc
        ops = self.ops
        last_writer = {}
        readers = {}
        last_on_eng = {}
        last_dma = {}
        pending = {}
        bank_last = {}
        bars = set(self.barriers)
        for i, op in enumerate(ops):
            if i in bars:
                bd = set(last_on_eng.values()) | set(last_dma.values())
                for en in ('pe', 'act', 'dve', 'pool', 'sp'):
                    pending[en] = set(bd) | pending.get(en, set())
            deps = set()
            if pending.get(op['eng']):
                deps |= pending.pop(op['eng'])
            for k in op['reads']:
                if k in last_writer:
                    deps.add(last_writer[k])
            for k in op['writes']:
                if k in last_writer:
                    deps.add(last_writer[k])
                deps.update(readers.get(k, ()))
            deps.discard(i)
            for k in op['reads'] + op['writes']:
                if isinstance(k, tuple) and k and k[0] == 'ps':
                    b = k[1]
                    la = bank_last.get(b)
                    if la is not None and ops[la]['eng'] != op['eng'] and not NOEXCL:
                        deps.add(la)
                    bank_last[b] = i
            if op['eng'] == 'pe':
                deps = {d for d in deps if not (ops[d]['eng'] == 'pe' and not ops[d]['dma'])}
            op['deps'] = deps
            if op['dma']:
                last_dma[op['writes'][0]] = i
            else:
                last_on_eng[op['eng']] = i
            for k in op['reads']:
                readers.setdefault(k, []).append(i)
            for k in op['writes']:
                last_writer[k] = i
                readers[k] = []
        needed = set()
        for op in ops:
            needed.update(op['deps'])
        with ExitStack() as es:
            engsem = {}
            for en in ('pe', 'act', 'dve', 'pool'):
                engsem[en] = es.enter_context(nc.semaphore('s_' + en))
            dmasem = {}
            dmacnt = {}
            cnt = {en: 0 for en in engsem}
            for i, op in enumerate(ops):
                if op['dma']:
                    key = op['semkey'] if op['semkey'] is not None else op['writes'][0]
                    if key not in dmasem:
                        dmasem[key] = es.enter_context(nc.semaphore('d%d' % len(dmasem)))
                        dmacnt[key] = 0
                    dmacnt[key] += 16
                    op['sig'] = (dmasem[key], dmacnt[key], 16)
                    op['sigkey'] = key
                elif i in needed:
                    cnt[op['eng']] += 1
                    op['sig'] = (engsem[op['eng']], cnt[op['eng']], 1)
                else:
                    op['sig'] = None
            grp_last = {}
            for i, op in enumerate(ops):
                if op['dma']:
                    grp_last[op['sigkey']] = i
            if os.environ.get('KDEBUG'):
                print('nsem_dma', len(dmasem), 'nops', len(ops), 'cnt', cnt, 'sems', [str(v) for v in list(dmasem.values())[:3]], [str(v) for v in engsem.values()])
            block = es.enter_context(nc.Block())
            by_eng = {}
            for i, op in enumerate(ops):
                by_eng.setdefault(op['eng'], []).append(i)

            def run(e, idxs, final=False):
                waited = {}
                for i in idxs:
                    op = ops[i]
                    need = {}
                    for d in op['deps']:
                        s = ops[d]['sig']
                        if ops[d]['dma'] and ops[d]['waitall'] and i > grp_last[ops[d]['sigkey']]:
                            s = (s[0], dmacnt[ops[d]['sigkey']], 16)
                        sid = id(s[0])
                        if s[1] > need.get(sid, (None, 0))[1]:
                            need[sid] = (s[0], s[1])
                    for sid, (sem, val) in need.items():
                        if waited.get(sid, 0) < val:
                            e.wait_ge(sem, val)
                            waited[sid] = val
                    ins = op['fn'](e)
                    if op['sig'] is not None:
                        ins.then_inc(op['sig'][0], op['sig'][2])
                if final:
                    for key, sem in dmasem.items():
                        if waited.get(id(sem), 0) < dmacnt[key]:
                            e.wait_ge(sem, dmacnt[key])

            @block.sync
            def _(e):
                run(e, by_eng.get('sp', []), final=True)

            @block.tensor
            def _(e):
                run(e, by_eng.get('pe', []))

            @block.scalar
            def _(e):
                run(e, by_eng.get('act', []))

            @block.vector
            def _(e):
                run(e, by_eng.get('dve', []))

            @block.gpsimd
            def _(e):
                run(e, by_eng.get('pool', []))


def _bf(a):
    return np.ascontiguousarray(a).astype(ml_dtypes.bfloat16)


def host_consts():
    c = {}
    c['ident_bf'] = _bf(np.eye(128))
    c['ident_f'] = np.eye(128, dtype=np.float32)
    c['ones_bf'] = _bf(np.ones((128, 128)))
    selx = np.zeros((2, 128), np.float32)
    selx[0] = 1.0
    c['selx'] = selx
    q = D // 4
    om = 1.0 / (10000.0 ** (np.arange(q, dtype=np.float32) / q))
    pr = np.arange(64, dtype=np.float32)[:, None] * om
    er = np.concatenate([np.sin(pr), np.cos(pr)], -1)
    pe = np.concatenate([np.broadcast_to(er[:, None, :], (64, 64, 512)),
                         np.broadcast_to(er[None, :, :], (64, 64, 512))], -1)
    c['pe'] = np.ascontiguousarray(pe.reshape(4096, D)).astype(np.float32)
    idx = np.arange(128)
    same = (idx[:, None] // 64) == (idx[None, :] // 64)
    c['Uf'] = (same & (idx[:, None] <= idx[None, :])).astype(np.float32)
    c['Ub'] = (same & (idx[:, None] >= idx[None, :])).astype(np.float32)
    c['SCm'] = same.astype(np.float32)
    vf = same & (idx[None, :] >= idx[:, None])
    vb = same & (idx[None, :] <= idx[:, None])
    c['mnegF'] = np.where(vf, 0.0, NEG).astype(np.float32)
    c['mnegB'] = np.where(vb, 0.0, NEG).astype(np.float32)
    sub = (idx[:, None] // 16) == (idx[None, :] // 16)
    c['mdF'] = _bf(sub & (idx[None, :] > idx[:, None]))
    c['mdB'] = _bf(sub & (idx[None, :] < idx[:, None]))
    c['moF'] = _bf(same & ~sub & (idx[None, :] > idx[:, None]))
    c['moB'] = _bf(same & ~sub & (idx[None, :] < idx[:, None]))
    a64 = 2 * np.pi * np.outer(np.arange(64), np.arange(64)) / 64
    c['c64'] = _bf(np.concatenate([np.cos(a64), -np.sin(a64)], 1))
    kk = np.arange(4096)
    th = 2 * np.pi * np.outer(np.arange(64), kk) / 4096
    Mc = (np.cos(th) / 64.0).reshape(64, 64, 64)
    Ms = (np.sin(th) / 64.0).reshape(64, 64, 64)
    Mc = Mc.transpose(0, 2, 1)
    Ms = Ms.transpose(0, 2, 1)
    rr = np.zeros((64, 64, 2, 2, 64))
    rr[:, :, 0, 0] = Mc
    rr[:, :, 0, 1] = -Ms
    rr[:, :, 1, 0] = Ms
    rr[:, :, 1, 1] = Mc
    c['rr'] = _bf(rr.reshape(64, 64, 2, 128))
    ac = 2 * np.pi * np.outer(np.arange(128), np.arange(128)) / 128
    c['ccsc'] = _bf(np.concatenate([np.cos(ac), np.sin(ac)], 1) / np.sqrt(128.0))
    return c


def build(dbg=(), phases=('dn', 'fo', 'fin'), heads=range(H)):
    nc = bass.Bass("TRN2", target_bir_lowering=False)
    consts = host_consts()
    dr = {}

    def din(name, shape, dt=F32):
        dr[name] = nc.dram_tensor(name, list(shape), dt, kind="ExternalInput").ap()
        return dr[name]

    x_d = din('x', [NLAT, D])
    ctx_d = din('ctx', [NCTX, D])
    cc_d = din('cc', [128, 16])
    wmod_d = din('w_mod', [128, 8, 3 * D])
    bmod_d = din('b_mod2', [2, 3 * D])
    win_d = din('w_in_r', [128, 8, INCOLS])
    wdn_d = din('w_dn_r', [128, 8, D])
    wout_d = din('w_out_r', [128, 8, D])
    wfo_d = din('w_fo_r', [128, 4, D])
    wfm_d = din('w_fmix_r', [128, 4, 128])
    convw_d = din('convw_r', [128, 24, 3])
    alog_d = din('alog_t', [128, 16])
    dtb_d = din('dtb_t', [128, 16])
    normw_d = din('normw_t', [128, 128])
    lng_d = din('lng_t', [128, D])
    lnb_d = din('lnb_t', [128, D])
    for k, v in consts.items():
        din(k, v.shape, BF16 if v.dtype == ml_dtypes.bfloat16 else F32)
    out_d = nc.dram_tensor('out', [NLAT, D], F32, kind="ExternalOutput").ap()
    og_s = nc.dram_tensor('og_s', [H, 128, NLAT], BF16).ap()
    fm_s = nc.dram_tensor('fm_s', [4, 128, NLAT], BF16).ap()
    mg_s = nc.dram_tensor('mg_s', [8, 128, NLAT], BF16).ap()
    dbg_d = {}
    for name, shape, dt in dbg:
        dbg_d[name] = nc.dram_tensor('dbg_' + name, list(shape), dt, kind="ExternalOutput").ap()

    S = Sched(nc)

    def MM(out, lhsT, rhs, r, w, start=True, stop=True):
        S.pe(lambda e: e.matmul(out, lhsT=lhsT, rhs=rhs, start=start, stop=stop), r, w)

    def TR(out, in_, ident, r, w):
        S.pe(lambda e: e.transpose(out=out, in_=in_, identity=ident), r, w)

    def TT(eng, out, in0, in1, op, r, w):
        S.add(eng, lambda e: e.tensor_tensor(out=out, in0=in0, in1=in1, op=op), r, w)

    def TS(eng, out, in0, s1, s2, op0, op1, r, w):
        if s2 is None:
            S.add(eng, lambda e: e.tensor_scalar(out=out, in0=in0, scalar1=s1, scalar2=None, op0=op0), r, w)
        else:
            S.add(eng, lambda e: e.tensor_scalar(out=out, in0=in0, scalar1=s1, scalar2=s2, op0=op0, op1=op1), r, w)

    def STT(eng, out, in0, scalar, in1, op0, op1, r, w):
        S.add(eng, lambda e: e.scalar_tensor_tensor(out=out, in0=in0, scalar=scalar, in1=in1, op0=op0, op1=op1),
              r, w)

    def ACT(out, in_, func, r, w, bias=None, scale=None, accum=None):
        kw = {}
        if bias is not None:
            kw['bias'] = bias
        if scale is not None:
            kw['scale'] = scale
        if accum is not None:
            kw['accum_out'] = accum
        S.act(lambda e: e.activation(out=out, in_=in_, func=func, **kw), r, w)

    def CP(eng, out, in_, r, w):
        if eng == 'act':
            S.act(lambda e: e.activation(out=out, in_=in_, func=AF.Copy), r, w)
        else:
            S.add(eng, lambda e: e.tensor_copy(out=out, in_=in_), r, w)

    def RED(out, in_, r, w):
        S.dve(lambda e: e.tensor_reduce(out=out, in_=in_, axis=AX.X, op=ALU.add), r, w)

    def MEMSET(eng, ap, val, w):
        S.add(eng, lambda e: e.memset(ap, val), [], w)

    with ExitStack() as es:
        def sb(name, shape, dt=F32):
            return es.enter_context(nc.sbuf_tensor('sb_' + name, list(shape), dt))

        ps = [es.enter_context(nc.psum_tensor('ps%d' % i, [128, 512], F32)) for i in range(8)]

        def pk(b, q0=0, nq=4):
            return [('ps', b, q) for q in range(q0, q0 + nq)]

        def psq(b, q, nq=1):
            return ps[b][:, q * 128:(q + nq) * 128]

        cst = {}

        def load_consts(names, alloc, grp):
            for name in names:
                v = consts[name]
                cst[name] = alloc(name, v.shape, BF16 if v.dtype == ml_dtypes.bfloat16 else F32)
                S.dma(cst[name][:], dr[name], [], [name], semkey=grp, waitall=True)

        load_consts(('ident_bf', 'ident_f', 'ones_bf', 'selx'), sb, 'consts')
        ident_bf, ident_f, selx = cst['ident_bf'], cst['ident_f'], cst['selx']
        hT = sb('hT', [128, 8, NTOK], BF16)
        mhalf = sb('mhalf', [128, 32])
        MEMSET('pool', mhalf[:], -0.5, ['mhalf'])
        modT = sb('modT', [128, 48])
        gxb = sb('gxb', [128, D])

        def load_cast(dst, src, a, b, wkey):
            bc = max(1, 2048 // a)
            for b0 in range(0, b, bc):
                bn = min(bc, b - b0)
                S.dma(dst[:, :, b0:b0 + bn], src[:, :, b0:b0 + bn], [], [wkey], eng='pool')

        def hT_keys_for_tokens(t0, t1):
            ks = set()
            for t in range(t0, t1, 128):
                ks.add(('hT', 0 if t < NCTX else 1 + (t - NCTX) // 512))
            return sorted(ks)

        with ExitStack() as es0:
            def sb0(name, shape, dt=F32):
                return es0.enter_context(nc.sbuf_tensor('sb_' + name, list(shape), dt))
            cc = sb0('cc', [128, 16])
            sc = sb0('sc', [128, 16])
            wm = [sb0('wm%d' % i, [128, 3 * D]) for i in range(2)]
            bm = sb0('bm', [2, 3 * D])
            modrow = sb0('modrow', [2, 3 * D])
            S.dma(cc[:], cc_d, [], ['cc'])
            S.dma(bm[:], bmod_d, [], ['bm'])
            ACT(sc[:], cc[:], AF.Silu, ['cc'], ['sc'])
            sc3 = sc[:].rearrange("p (k v) -> p k v", v=2)
            for k in range(8):
                S.dma(wm[k % 2][:], wmod_d[:, k, :], [], ['wm%d' % (k % 2)])
                for j in range(6):
                    MM(ps[j][0:2, :], sc3[:, k, :], wm[k % 2][:, j * 512:(j + 1) * 512],
                       ['sc', 'wm%d' % (k % 2)], pk(j), start=(k == 0), stop=(k == 7))
            for j in range(6):
                TT('dve', modrow[:, j * 512:(j + 1) * 512], ps[j][0:2, :], bm[:, j * 512:(j + 1) * 512], ALU.add,
                   pk(j) + ['bm'], ['modrow'])
            for j in range(24):
                MM(ps[6][:, 2 * j:2 * j + 2], modrow[0:2, j * 128:(j + 1) * 128], ident_f[0:2, 0:2],
                   ['modrow', 'ident_f'], pk(6))
            CP('dve', modT[:], ps[6][:, 0:48], pk(6), ['modT'])
            TS('dve', modT[:, 16:32], modT[:, 16:32], 1.0, None, ALU.add, None, ['modT'], ['modT'])
            for jj in range(2):
                MM(ps[7][:, :], selx[0:2, :], modrow[0:2, 2048 + jj * 512:2048 + (jj + 1) * 512],
                   ['modrow', 'selx'], pk(7))
                CP('act', gxb[:, jj * 512:(jj + 1) * 512], ps[7][:, :], pk(7), ['gxb'])

            NBF = 4
            xt = [sb0('xt%d' % i, [128, D]) for i in range(NBF)]
            pet = [sb0('pet%d' % i, [128, D]) for i in range(NBF)]
            sq = sb0('sq', [128, D], BF16)
            xn = [sb0('xn%d' % i, [128, 4, D], BF16) for i in range(2)]
            st = [sb0('st%d' % i, [128, 8]) for i in range(8)]
            groups = [(1, [(ctx_d, 0), (ctx_d, 1)], 0)]
            for g in range(8):
                groups.append((0, [(x_d, 4 * g + i) for i in range(4)], NCTX + g * 512))
            flat = []
            for gi, (v, tiles, tok0) in enumerate(groups):
                for i, (src, ti) in enumerate(tiles):
                    flat.append((gi, i, v, src, ti, len(tiles), tok0))

            def lnA(n):
                gi, i, v, src, ti, nt, tok0 = flat[n]
                j = n % NBF
                stt, sk, xk = st[n % 8], 'st%d' % (n % 8), 'xt%d' % j
                S.dma(xt[j][:], src[ti * 128:(ti + 1) * 128, :], [], [xk])
                if v == 0:
                    S.dma(pet[j][:], dr['pe'][ti * 128:(ti + 1) * 128, :], [], ['pet%d' % j])
                    TT('dve', xt[j][:], xt[j][:], pet[j][:], ALU.add, [xk, 'pet%d' % j], [xk])
                RED(stt[:, 0:1], xt[j][:], [xk], [sk])
                ACT(sq[:], xt[j][:], AF.Square, [xk], ['sq', sk], accum=stt[:, 1:2])

            def lnB(n):
                stt, sk = st[n % 8], 'st%d' % (n % 8)
                TS('dve', stt[:, 2:3], stt[:, 0:1], 1.0 / D, None, ALU.mult, None, [sk], [sk])
                TT('dve', stt[:, 3:4], stt[:, 2:3], stt[:, 2:3], ALU.mult, [sk], [sk])
                STT('dve', stt[:, 4:5], stt[:, 1:2], 1.0 / D, stt[:, 3:4], ALU.mult, ALU.subtract, [sk], [sk])
                TS('dve', stt[:, 7:8], stt[:, 4:5], EPS, None, ALU.add, None, [sk], [sk])
                TT('pool', stt[:, 5:6], stt[:, 7:8], mhalf[:, 0:1], ALU.pow, [sk, 'mhalf'], [sk])

            def lnC(n):
                gi, i, v, src, ti, nt, tok0 = flat[n]
                j = n % NBF
                gp = gi % 2
                stt, sk, xk = st[n % 8], 'st%d' % (n % 8), 'xt%d' % j
                STT('dve', stt[:, 6:7], stt[:, 2:3], -1.0, stt[:, 5:6], ALU.mult, ALU.mult, [sk], [sk])
                ACT(xn[gp][:, i, :], xt[j][:], AF.Identity, [xk, sk], [('xn', gp, i)], scale=stt[:, 5:6], bias=stt[:, 6:7])
                if i == nt - 1:
                    for kc in range(8):
                        pb = 6 + (kc % 2)
                        pst = ps[pb][:].bitcast(BF16)
                        for ii in range(nt):
                            TR(pst[:, ii * 128:(ii + 1) * 128], xn[gp][:, ii, kc * 128:(kc + 1) * 128], ident_bf[:],
                               [('xn', gp, ii), 'ident_bf'], pk(pb))
                        dst = hT[:, kc, tok0:tok0 + nt * 128]
                        sc_ap = modT[:, 2 * (8 + kc) + v:2 * (8 + kc) + v + 1]
                        sh_ap = modT[:, 2 * kc + v:2 * kc + v + 1]
                        if kc % 2 == 0:
                            ACT(dst, pst[:, 0:nt * 128], AF.Identity, pk(pb) + ['modT'], [('hT', gi)], scale=sc_ap, bias=sh_ap)
                        else:
                            TS('dve', dst, pst[:, 0:nt * 128], sc_ap, sh_ap, ALU.mult, ALU.add, pk(pb) + ['modT'], [('hT', gi)])

            NT0 = len(flat)
            for n in range(NT0 + 2):
                if n < NT0:
                    lnA(n)
                if 0 <= n - 1 < NT0:
                    lnB(n - 1)
                if 0 <= n - 2 < NT0:
                    lnC(n - 2)
        S.barrier()
        ALLHT = [('hT', g) for g in range(9)]
        if 'hT' in dbg_d:
            for kc in range(8):
                S.dma(dbg_d['hT'][kc], hT[:, kc, :], ALLHT, ['dbg_hT'])

        if 'dn' in phases:
            with ExitStack() as es2:
                def sb2(name, shape, dt=F32):
                    return es2.enter_context(nc.sbuf_tensor('sb_' + name, list(shape), dt))
                NB = NTOK // 128
                load_consts(('Uf', 'Ub', 'SCm', 'mnegF', 'mnegB', 'mdF', 'mdB', 'moF', 'moB'), sb2, 'consts_dn')
                wbd = sb2('wbd', [128, 8, 32], BF16)
                load_cast(wbd[:], win_d[:, :, OFF[4]:OFF[4] + 32], 8, 32, 'wbd')
                beta_tm = sb2('beta_tm', [128, NB, 16])
                gc_tm = sb2('gc_tm', [128, NB, 16])
                egc_tm = sb2('egc_tm', [128, NB, 16])
                egl_tm = sb2('egl_tm', [128, NB, 16])
                gch_tm = sb2('gch_tm', [128, NB, 16])
                gcl_tm = sb2('gcl_tm', [128, NB, 16])
                es2p = ExitStack()
                def sb2p(name, shape, dt=F32):
                    return es2p.enter_context(nc.sbuf_tensor('sb_' + name, list(shape), dt))
                raw_tm = sb2p('raw_tm', [128, NB, 32])
                gcb_tm = sb2p('gcb_tm', [128, NB, 16], BF16)
                g_tm = sb2p('g_tm', [128, NB, 16])
                tot_tm = sb2p('tot_tm', [128, NB, 16])
                tmp_tm = sb2p('tmp_tm', [128, NB, 16])
                alog = sb2p('alog', [128, 16])
                dtb = sb2p('dtb', [128, 16])
                nega = sb2p('nega', [128, 16])
                S.dma(alog[:], alog_d, [], ['alog'])
                S.dma(dtb[:], dtb_d, [], ['dtb'])
                ACT(nega[:], alog[:], AF.Exp, ['alog'], ['nega'])
                TS('dve', nega[:], nega[:], -1.0, None, ALU.mult, None, ['nega'], ['nega'])
                for b0 in range(0, NB, 16):
                    nb_ = min(16, NB - b0)
                    bank = 6 + (b0 // 16) % 2
                    for bi in range(nb_):
                        blk = b0 + bi
                        for kc in range(8):
                            MM(ps[bank][:, bi * 32:(bi + 1) * 32], hT[:, kc, blk * 128:(blk + 1) * 128], wbd[:, kc, :],
                               hT_keys_for_tokens(blk * 128, blk * 128 + 128) + ['wbd'], pk(bank),
                               start=(kc == 0), stop=(kc == 7))
                    CP('act', raw_tm[:, b0:b0 + nb_, :],
                       ps[bank][:, 0:nb_ * 32].rearrange("p (b c) -> p b c", c=32), pk(bank), ['raw_tm'])
                ACT(beta_tm[:], raw_tm[:, :, 0:16], AF.Sigmoid, ['raw_tm'], ['beta_tm'])
                TT('dve', tmp_tm[:], raw_tm[:, :, 16:32], dtb[:].unsqueeze(1).to_broadcast([128, NB, 16]), ALU.add,
                   ['raw_tm', 'dtb'], ['tmp_tm'])
                ACT(tmp_tm[:], tmp_tm[:], AF.Exp, ['tmp_tm'], ['tmp_tm'])
                ACT(tmp_tm[:], tmp_tm[:], AF.Ln, ['tmp_tm'], ['tmp_tm'], bias=1.0)
                TT('dve', g_tm[:], tmp_tm[:], nega[:].unsqueeze(1).to_broadcast([128, NB, 16]), ALU.mult,
                   ['tmp_tm', 'nega'], ['g_tm'])
                for half in range(2):
                    b0 = half * 17
                    for bi in range(17):
                        blk = b0 + bi
                        MM(ps[6][:, bi * 16:bi * 16 + 8], cst['Uf'][:], g_tm[:, blk, 0:8], ['g_tm', 'Uf'], pk(6))
                        MM(ps[6][:, bi * 16 + 8:bi * 16 + 16], cst['Ub'][:], g_tm[:, blk, 8:16], ['g_tm', 'Ub'], pk(6))
                        MM(ps[7][:, bi * 16:bi * 16 + 16], cst['SCm'][:], g_tm[:, blk, :], ['g_tm', 'SCm'], pk(7))
                    CP('dve', gc_tm[:, b0:b0 + 17, :], ps[6][:, 0:272].rearrange("p (b c) -> p b c", c=16), pk(6),
                       ['gc_tm'])
                    CP('dve', tot_tm[:, b0:b0 + 17, :], ps[7][:, 0:272].rearrange("p (b c) -> p b c", c=16), pk(7),
                       ['tot_tm'])
                ACT(egc_tm[:], gc_tm[:], AF.Exp, ['gc_tm'], ['egc_tm'])
                CP('dve', gcb_tm[:], gc_tm[:], ['gc_tm'], ['gcb_tm'])
                CP('dve', gch_tm[:], gcb_tm[:], ['gcb_tm'], ['gch_tm'])
                TT('dve', gcl_tm[:], gc_tm[:], gch_tm[:], ALU.subtract, ['gc_tm', 'gch_tm'], ['gcl_tm'])
                TT('dve', tmp_tm[:], tot_tm[:], gc_tm[:], ALU.subtract, ['tot_tm', 'gc_tm'], ['tmp_tm'])
                ACT(egl_tm[:], tmp_tm[:], AF.Exp, ['tmp_tm'], ['egl_tm'])
                if 'tm' in dbg_d:
                    S.dma(dbg_d['tm'][0], beta_tm[:], ['beta_tm'], ['dbg_tm0'])
                    S.dma(dbg_d['tm'][1], gc_tm[:], ['gc_tm'], ['dbg_tm1'])
                    S.dma(dbg_d['tm'][2], egl_tm[:], ['egl_tm'], ['dbg_tm2'])
                es2p.close()
                S.barrier()

                X0 = sb2('X0', [128, 4360], BF16)
                X1 = sb2('X1', [128, 4360], BF16)
                X2 = sb2('X2', [128, 4360], BF16)
                QT = sb2('QT', [128, NTOK], BF16)
                KT = sb2('KT', [128, NTOK], BF16)
                Ktok = sb2('Ktok', [128, NB, 128], BF16)
                Vtok = sb2('Vtok', [128, NB, 128], BF16)
                Obuf = sb2('Obuf', [128, 32, 128], BF16)
                wqkv = sb2('wqkv', [128, 8, 384], BF16)
                wdg = sb2('wdg', [128, 8, 128], BF16)
                convw = sb2('convw', [128, 24, 3])
                normw = sb2('normw', [128, 128])
                sqb = sb2('sqb', [128, 512], BF16)
                sqb2 = [sqb, sb2('sqb1', [128, 512], BF16)]
                Dg = [sb2('Dg%d' % j, [128, 128], BF16) for j in range(9)]
                lnt = [sb2('lnt0', [128, 512]), sb2('lnt1', [128, 512], BF16)]
                ssq = sb2('ssq', [128, 32])
                rstd_o = sb2('rstd_o', [128, 32])
                Onb = [sb2('Onb%d' % i, [128, 128], BF16) for i in range(4)]
                S.dma(convw[:], convw_d, [], ['convw'])
                S.dma(normw[:], normw_d, [], ['normw'])
                MEMSET('pool', X0[:], 0.0, ['X0'])
                Sf32 = [sb2('Sf32_%d' % d, [128, 128]) for d in range(2)]
                Sbf = [sb2('Sbf_%d' % d, [128, 128], BF16) for d in range(2)]
                vnew = [[sb2('vnew_%d%d' % (d, c_), [128, 128], BF16) for c_ in range(2)] for d in range(2)]
                LOOK = 3
                carve = [[X1, 0, 4360], [X2, 0, 4360], [X0, 260, 4352]]

                def newbuf(nm, w, dt):
                    ne = w * (2 if dt == F32 else 1)
                    for reg in carve:
                        a = (reg[1] + 1) // 2 * 2
                        if a + ne <= reg[2]:
                            reg[1] = a + ne
                            v = reg[0][:, a:a + ne]
                            return v.bitcast(F32) if dt == F32 else v
                    return sb2(nm, [128, w], dt)[:]

                INTERNAL = ('Gdh', 'Gdl', 'X', 'N', 'Bns', 'Bd', 'Bo', 'LLo', 'LB0', 'LB1', 'Q0', 'Q1', 'nMT',
                            'X0', 'Xit0', 'Xit1', 'Kg')
                ISET = {}
                PSET = {}
                for d in range(2):
                    for i in range(LOOK):
                        nb = lambda nm, w=128, dt=BF16: newbuf('%s_%d%d' % (nm, d, i), w, dt)
                        ISET[d, i] = dict(Gdh=nb('Gdh'), Gdl=nb('Gdl'), X=nb('X', 128, F32), N=nb('N'), Bns=nb('Bns'),
                                          Bd=nb('Bd'), Bo=nb('Bo'), LLo=nb('LLo', 256), LB=[nb('LB0', 256), nb('LB1', 256)],
                                          Q=[nb('Q0'), nb('Q1')], nMT=nb('nMT'), X0=nb('X0', 256),
                                          Xit=[nb('Xit0', 256), nb('Xit1', 256)], Kg=nb('Kg'))
                    for i in range(LOOK + 1):
                        nb = lambda nm, w=128, dt=BF16: newbuf('%s_%d%dp' % (nm, d, i), w, dt)
                        PSET[d, i] = dict(EG=nb('EG', 128, F32), qgT=nb('qgT'), QKT=nb('QKT'), Ub=nb('Ub'), nwT=nb('nwT'),
                                          kd=nb('kd'))

                def PQ(d, name, odd=0):
                    m = {'G': (0, 0), 'K': (0, 1), 'Qp': (0, 3), 'L': (1, 0), 'LB': (1, 1), 'X': (1, 1), 'M': (1, 3),
                         'W': (1, 3), 'v': (2, 0), 'o': (2, 1), 'dS': (2, 2)}[name]
                    if m[0] == 1 and odd:
                        return (6 + d, m[1])
                    return (3 * d + m[0], m[1])

                def chunk_of_block(blk):
                    return 0 if blk < 2 else 1 + (blk - 2) // 4

                hl = list(heads)

                def load_qkv_w(hh):
                    for wi in range(3):
                        c0 = OFF[2] + wi * 1024 + hh * 128
                        load_cast(wqkv[:, :, wi * 128:(wi + 1) * 128], win_d[:, :, c0:c0 + 128], 8, 128, ('wqkv', wi))

                for h in heads:
                    if h == hl[0]:
                        load_qkv_w(h)
                    c0 = OFF[3] + h * 128
                    load_cast(wdg[:], win_d[:, :, c0:c0 + 128], 8, 128, 'wdg')
                    def blocks_of_chunk(c):
                        return [0, 1] if c == 0 else list(range(2 + 4 * (c - 1), 6 + 4 * (c - 1)))

                    if True:
                        for wi in range(3):
                            for j in range(3):
                                TS('dve', Dg[wi * 3 + j][:], ident_bf[:], convw[:, wi * 8 + h, j:j + 1], None, ALU.mult, None,
                                   ['ident_bf', 'convw'], [('Dg', wi, j)])

                        def geom(c):
                            t0 = 0 if c == 0 else NCTX + (c - 1) * 512
                            n = 256 if c == 0 else 512
                            col0 = 1 if c == 0 else 259 + (c - 1) * 512
                            return t0, n, col0

                        def stA(wi, c, n_):
                            t0, n, col0 = geom(c)
                            bank = 6 + n_ % 2
                            for kc in range(8):
                                MM(ps[bank][:, 0:n], wqkv[:, kc, wi * 128:(wi + 1) * 128], hT[:, kc, t0:t0 + n],
                                   [('hT', c), ('wqkv', wi)], pk(bank), start=(kc == 0), stop=(kc == 7))
                            CP('dve', X0[:, col0:col0 + n], ps[bank][:, 0:n], pk(bank), [('X0', c)])

                        def stB(wi, c, n_):
                            t0, n, col0 = geom(c)
                            nb_keys = [('X0', cc) for cc in (c - 1, c, c + 1) if 0 <= cc <= 8] + ['X0']
                            bank = n_ % 2
                            for j in range(3):
                                MM(ps[bank][:, 0:n], Dg[wi * 3 + j][:], X0[:, col0 + j - 1:col0 + j - 1 + n], nb_keys + [('Dg', wi, j)],
                                   pk(bank), start=(j == 0), stop=(j == 2))
                            ACT(X2[:, t0:t0 + n], ps[bank][:, 0:n], AF.Silu, pk(bank), [('X2', c)])

                        def stC(wi, c, n_):
                            t0, n, col0 = geom(c)
                            if wi == 2:
                                blks = blocks_of_chunk(c)
                                bank = 4 + n_ % 2
                                pst = ps[bank][:].bitcast(BF16)
                                for bi_, blk in enumerate(blks):
                                    TR(pst[:, bi_ * 128:(bi_ + 1) * 128], X2[:, blk * 128:(blk + 1) * 128], ident_bf[:],
                                       [('X2', c), 'ident_bf'], pk(bank))
                                CP('dve' if n_ % 2 == 0 else 'act', Vtok[:, blks[0]:blks[0] + len(blks), :],
                                   pst[:, 0:len(blks) * 128].rearrange("p (b c) -> p b c", c=128), pk(bank),
                                   [('Vtok', blk) for blk in blks])
                            else:
                                bank = 2 + n_ % 2
                                sq_ = sqb2[n_ % 2]
                                TT('dve', sq_[:, 0:n], X2[:, t0:t0 + n], X2[:, t0:t0 + n], ALU.mult, [('X2', c)], [('sqb', n_ % 2)])
                                MM(ps[bank][:, 0:n], cst['ones_bf'][:], sq_[:, 0:n], [('sqb', n_ % 2), 'ones_bf'], pk(bank))

                        def stD(wi, c, n_):
                            t0, n, col0 = geom(c)
                            if wi == 2:
                                return
                            dstT = KT if wi == 1 else QT
                            dk = 'KT' if wi == 1 else 'QT'
                            bank = 2 + n_ % 2
                            ACT(lnt[0][:, 0:n], ps[bank][:, 0:n], AF.Ln, pk(bank), ['lnt0'], bias=EPS)
                            ACT(lnt[1][:, 0:n], lnt[0][:, 0:n], AF.Exp, ['lnt0'], ['lnt1'], scale=-0.5,
                                bias=(-0.5 * float(np.log(128.0)) if wi == 0 else 0.0))
                            TT('dve', dstT[:, t0:t0 + n], X2[:, t0:t0 + n], lnt[1][:, 0:n], ALU.mult,
                               [('X2', c), 'lnt1'], [(dk, c)])
                            if wi == 1:
                                blks = blocks_of_chunk(c)
                                bank = 4 + n_ % 2
                                pst = ps[bank][:].bitcast(BF16)
                                for bi_, blk in enumerate(blks):
                                    TR(pst[:, bi_ * 128:(bi_ + 1) * 128], KT[:, blk * 128:(blk + 1) * 128], ident_bf[:],
                                       [('KT', c), 'ident_bf'], pk(bank))
                                CP('dve' if n_ % 2 == 0 else 'act', Ktok[:, blks[0]:blks[0] + len(blks), :],
                                   pst[:, 0:len(blks) * 128].rearrange("p (b c) -> p b c", c=128), pk(bank),
                                   [('Ktok', blk) for blk in blks])

                        items = [(wi_, c_) for wi_ in (1, 2, 0) for c_ in range(9)]
                        for i in range(len(items) + 4):
                            if i < len(items):
                                stA(*items[i], i)
                            if 0 <= i - 2 < len(items):
                                stB(*items[i - 2], i - 2)
                            if 0 <= i - 3 < len(items):
                                stC(*items[i - 3], i - 3)
                            if 0 <= i - 4 < len(items):
                                stD(*items[i - 4], i - 4)
                    if 'qkv' in dbg_d and h == list(heads)[0]:
                        S.dma(dbg_d['qkv'][0], QT[:], [('QT', c) for c in range(9)], ['dbg_q'])
                        S.dma(dbg_d['qkv'][1], KT[:], [('KT', c) for c in range(9)], ['dbg_k'])
                        S.dma(dbg_d['vtok'], Vtok[:], [('Vtok', b) for b in range(NB)], ['dbg_v'])

                    if hl.index(h) + 1 < len(hl):
                        load_qkv_w(hl[hl.index(h) + 1])
                    S.barrier()
                    for d in range(2):
                        MEMSET('pool', Sf32[d][:], 0.0, [('S', d)])
                        MEMSET('pool', Sbf[d][:], 0.0, [('Sb', d)])
                        for c_ in range(2):
                            MEMSET('pool', vnew[d][c_][:], 0.0, [('vn', d, c_)])
                    orderC = [list(range(68)), [3, 2, 1, 0] + list(range(67, 3, -1))]
                    orderB = []
                    for d in range(2):
                        ob = []
                        for c in orderC[d]:
                            if c // 2 not in ob:
                                ob.append(c // 2)
                        orderB.append(ob)
                    owritten = set()

                    def setup_micro(d, bi):
                        blk = orderB[d][bi]
                        I = ISET[d, bi % LOOK]
                        P = PSET[d, bi % (LOOK + 1)]
                        odd = bi % 2
                        col = d * 8 + h
                        kq = [('QT', chunk_of_block(blk)), ('KT', chunk_of_block(blk))]
                        tks = slice(blk * 128, (blk + 1) * 128)
                        bG, qG = PQ(d, 'G')
                        bK, qK = PQ(d, 'K')
                        bQ, qQ = PQ(d, 'Qp')
                        bL, qL = PQ(d, 'L', odd)
                        bLB, qLB = PQ(d, 'LB', odd)
                        bM, qM = PQ(d, 'M', odd)
                        ik = lambda nm: ('i', nm, d, bi % LOOK)
                        pkk = lambda nm: ('p', nm, d, bi % (LOOK + 1))
                        gcc = gc_tm[:, blk, col:col + 1]
                        bcol = beta_tm[:, blk, col:col + 1]
                        mneg = cst['mnegF'] if d == 0 else cst['mnegB']
                        md = cst['mdF'] if d == 0 else cst['mdB']
                        mo = cst['moF'] if d == 0 else cst['moB']
                        pstL = ps[bL][:].bitcast(BF16)[:, qL * 256:qL * 256 + 256]
                        pstW = ps[bM][:].bitcast(BF16)[:, qM * 256:qM * 256 + 128]
                        L0, Lo = I['LLo'][:, 0:128], I['LLo'][:, 128:256]

                        def m0():
                            ACT(I['Gdh'], ident_bf[:], AF.Copy, ['ident_bf', 'gch_tm'], [ik('Gdh')],
                                scale=gch_tm[:, blk, col:col + 1])
                            ACT(I['Gdl'], ident_bf[:], AF.Copy, ['ident_bf', 'gcl_tm'], [ik('Gdl')],
                                scale=gcl_tm[:, blk, col:col + 1])
                            ACT(P['kd'], Ktok[:, blk, :], AF.Copy, [('Ktok', blk), 'egl_tm'], [pkk('kd')],
                                scale=egl_tm[:, blk, col:col + 1])
                            TS('pool', I['Kg'], Ktok[:, blk, :], egc_tm[:, blk, col:col + 1], None, ALU.mult, None,
                               [('Ktok', blk), 'egc_tm'], [ik('Kg')])

                        def m1():
                            MM(psq(bG, qG), cst['ones_bf'][:], I['Gdh'], ['ones_bf', ik('Gdh')], pk(bG, qG, 1),
                               start=True, stop=False)
                            MM(psq(bG, qG), cst['ones_bf'][:], I['Gdl'], ['ones_bf', ik('Gdl')], pk(bG, qG, 1),
                               start=False, stop=True)
                            MM(psq(bK, qK), KT[:, tks], QT[:, tks], kq, pk(bK, qK, 1))
                            MM(psq(bK, qK + 1), KT[:, tks], KT[:, tks], kq, pk(bK, qK + 1, 1))

                        def m2():
                            STT('dve', I['X'], psq(bG, qG), gcc, mneg[:], ALU.subtract, ALU.add,
                                pk(bG, qG, 1) + ['gc_tm', 'mnegF', 'mnegB'], [ik('X')])
                            ACT(P['EG'], psq(bG, qG), AF.Exp, pk(bG, qG, 1), [pkk('EG')])

                        def m3():
                            ACT(I['N'], I['X'], AF.Exp, [ik('X')], [ik('N')])
                            TT('dve', P['qgT'], QT[:, tks], P['EG'], ALU.mult, kq + [pkk('EG')], [pkk('qgT')])

                        def m4():
                            STT('dve', I['Bns'], psq(bK, qK + 1), bcol, I['N'], ALU.mult, ALU.mult,
                                pk(bK, qK + 1, 1) + [ik('N'), 'beta_tm'], [ik('Bns')])
                            TT('dve', P['QKT'], psq(bK, qK), I['N'], ALU.mult, pk(bK, qK, 1) + [ik('N')], [pkk('QKT')])

                        def m5():
                            TT('dve', I['Bd'], I['Bns'], md[:], ALU.mult, [ik('Bns'), 'mdF', 'mdB'], [ik('Bd')])
                            TT('pool', I['Bo'], I['Bns'], mo[:], ALU.mult, [ik('Bns'), 'moF', 'moB'], [ik('Bo')])

                        def m6():
                            TT('dve', I['Q'][0], ident_bf[:], I['Bd'], ALU.subtract, [ik('Bd'), 'ident_bf'], [ik('Q0')])
                            TR(pstL[:, 0:128], I['Bd'], ident_bf[:], [ik('Bd'), 'ident_bf'], pk(bL, qL, 1))
                            TR(pstL[:, 128:256], I['Bo'], ident_bf[:], [ik('Bo'), 'ident_bf'], pk(bL, qL, 1))

                        def m7():
                            CP('act', I['LLo'], pstL, pk(bL, qL, 1), [ik('LLo')])

                        def levA(k):
                            def f():
                                if k == 1:
                                    Lp, Bp, kLp, kBp = L0, I['Bd'], ik('LLo'), ik('Bd')
                                else:
                                    prev = I['LB'][(k - 1) % 2]
                                    Lp, Bp = prev[:, 0:128], prev[:, 128:256]
                                    kLp = kBp = ik('LB%d' % ((k - 1) % 2))
                                MM(psq(bLB, qLB), Bp, Lp, [kLp, kBp], pk(bLB, qLB, 1))
                                if k < 3:
                                    MM(psq(bLB, qLB + 1), Lp, Bp, [kLp, kBp], pk(bLB, qLB + 1, 1))
                            return f

                        def levB(k):
                            def f():
                                cur = I['LB'][k % 2]
                                if k < 3:
                                    CP('act', cur[:, 0:256], psq(bLB, qLB, 2), pk(bLB, qLB, 2), [ik('LB%d' % (k % 2))])
                                else:
                                    CP('act', cur[:, 0:128], psq(bLB, qLB), pk(bLB, qLB, 1), [ik('LB%d' % (k % 2))])
                            return f

                        def levC(k):
                            def f():
                                cur = I['LB'][k % 2]
                                MM(psq(bQ, qQ), cur[:, 0:128], I['Q'][(k - 1) % 2],
                                   [ik('LB%d' % (k % 2)), ik('Q%d' % ((k - 1) % 2))], pk(bQ, qQ, 1))
                            return f

                        def levD(k):
                            def f():
                                TT('dve', I['Q'][k % 2], psq(bQ, qQ), I['Q'][(k - 1) % 2], ALU.add,
                                   pk(bQ, qQ, 1) + [ik('Q%d' % ((k - 1) % 2))], [ik('Q%d' % (k % 2))])
                            return f

                        def m20():
                            TdT = I['Q'][1]
                            MM(psq(bM, qM), Lo, TdT, [ik('LLo'), ik('Q1')], pk(bM, qM, 1))
                            MM(psq(bLB, qLB), TdT, Vtok[:, blk, :], [ik('Q1'), ('Vtok', blk)], pk(bLB, qLB, 1))
                            MM(psq(bLB, qLB + 1), TdT, I['Kg'], [ik('Q1'), ik('Kg')], pk(bLB, qLB + 1, 1))

                        def m21():
                            ACT(I['nMT'], psq(bM, qM), AF.Copy, pk(bM, qM, 1), [ik('nMT')], scale=-1.0)
                            CP('dve', I['X0'], psq(bLB, qLB, 2), pk(bLB, qLB, 2), [ik('X0')])

                        def itA(n):
                            def f():
                                prev = I['X0'] if n == 1 else I['Xit'][(n - 1) % 2]
                                kprev = ik('X0') if n == 1 else ik('Xit%d' % ((n - 1) % 2))
                                MM(psq(bLB, qLB, 2), I['nMT'], prev[:, 0:256], [ik('nMT'), kprev], pk(bLB, qLB, 2))
                            return f

                        def itB(n):
                            def f():
                                TT('dve', I['Xit'][n % 2], psq(bLB, qLB, 2), I['X0'], ALU.add,
                                   pk(bLB, qLB, 2) + [ik('X0')], [ik('Xit%d' % (n % 2))])
                            return f

                        def m28():
                            X3 = I['Xit'][1]
                            ACT(P['Ub'], X3[:, 0:128], AF.Copy, [ik('Xit1'), 'beta_tm'], [pkk('Ub')], scale=bcol)
                            TR(pstW, X3[:, 128:256], ident_bf[:], [ik('Xit1'), 'ident_bf'], pk(bM, qM, 1))

                        def m29():
                            ACT(P['nwT'], pstW, AF.Copy, pk(bM, qM, 1), [pkk('nwT')], scale=-1.0)
                        return [m0, m1, m2, m3, m4, m5, m6, m7,
                                levA(1), levB(1), levC(1), levD(1), levA(2), levB(2), levC(2), levD(2),
                                levA(3), levB(3), levC(3), levD(3), m20, m21,
                                itA(1), itB(1), itA(2), itB(2), itA(3), itB(3), m28, m29]

                    def step_micro(d, s_):
                        c = orderC[d][s_]
                        blk, ci = c // 2, c % 2
                        bi = orderB[d].index(blk)
                        P = PSET[d, bi % (LOOK + 1)]
                        col = d * 8 + h
                        cs = slice(ci * 64, ci * 64 + 64)
                        bv, qv = PQ(d, 'v')
                        bo, qo = PQ(d, 'o')
                        bs, qs = PQ(d, 'dS')
                        pkk = lambda nm: ('p', nm, d, bi % (LOOK + 1))

                        def u0():
                            MM(psq(bv, qv), P['nwT'], Sbf[d][:], [pkk('nwT'), ('Sb', d)], pk(bv, qv, 1))

                        def u1():
                            STT('dve', vnew[d][ci][cs, :], ps[bv][cs, qv * 128:(qv + 1) * 128],
                                beta_tm[cs, blk, col:col + 1], P['Ub'][cs, :], ALU.mult, ALU.add,
                                pk(bv, qv, 1) + ['beta_tm', pkk('Ub')], [('vn', d, ci)])

                        def u2():
                            MM(psq(bs, qs), P['kd'], vnew[d][ci][:], [pkk('kd'), ('vn', d, ci)], pk(bs, qs, 1))
                            if blk >= 2:
                                MM(psq(bo, qo), P['qgT'], Sbf[d][:], [pkk('qgT'), ('Sb', d)], pk(bo, qo, 1),
                                   start=True, stop=False)
                                MM(psq(bo, qo), P['QKT'], vnew[d][ci][:], [pkk('QKT'), ('vn', d, ci)], pk(bo, qo, 1),
                                   start=False, stop=True)

                        def u3():
                            lastcol = ci * 64 + (63 if d == 0 else 0)
                            STT('dve', Sf32[d][:], Sf32[d][:], P['EG'][:, lastcol:lastcol + 1], psq(bs, qs), ALU.mult,
                                ALU.add, [('S', d), pkk('EG')] + pk(bs, qs, 1), [('S', d)])
                            if blk >= 2:
                                okey = ('O', blk, ci)
                                src = ps[bo][cs, qo * 128:(qo + 1) * 128]
                                if okey not in owritten:
                                    owritten.add(okey)
                                    CP('act', Obuf[cs, blk - 2, :], src, pk(bo, qo, 1), [okey])
                                else:
                                    TT('dve', Obuf[cs, blk - 2, :], src, Obuf[cs, blk - 2, :], ALU.add,
                                       pk(bo, qo, 1) + [okey], [okey])

                        def u4():
                            CP('act', Sbf[d][:], Sf32[d][:], [('S', d)], [('Sb', d)])
                        return [u0, u1, u2, u3, u4]

                    NBK = len(orderB[0])
                    smic = {}
                    for g in range(-10 * LOOK, 5 * 68):
                        if g >= 0:
                            s_, u = g // 5, g % 5
                            for d in range(2):
                                if u == 0:
                                    smic[d] = step_micro(d, s_)
                                smic[d][u]()
                        for bi in range(NBK):
                            g0 = 10 * (bi - LOOK)
                            if g0 <= g < g0 + 30:
                                m = g - g0
                                for d in range(2):
                                    key = ('setup', d, bi)
                                    if key not in smic:
                                        smic[key] = setup_micro(d, bi)
                                    smic[key][m]()
                    S.barrier()
                    if 'O' in dbg_d and h == list(heads)[0]:
                        S.dma(dbg_d['O'], Obuf[:], [('O', b, ci) for b in range(2, 34) for ci in range(2)], ['dbg_O'])
                    if 'Sfin' in dbg_d and h == list(heads)[0]:
                        S.dma(dbg_d['Sfin'][0], Sf32[0][:], [('S', 0)], ['dbg_S0'])
                        S.dma(dbg_d['Sfin'][1], Sf32[1][:], [('S', 1)], ['dbg_S1'])

                    OK_ALL = lambda b: [('O', b + 2, 0), ('O', b + 2, 1)]
                    for b in range(32):
                        ACT(sqb[:, 0:128], Obuf[:, b, :], AF.Square, OK_ALL(b), ['sqb', 'ssq'], accum=ssq[:, b:b + 1])
                    TS('dve', ssq[:], ssq[:], 1.0 / 128, EPS, ALU.mult, ALU.add, ['ssq'], ['ssq'])
                    TT('pool', rstd_o[:], ssq[:], mhalf[:, 0:32], ALU.pow, ['ssq', 'mhalf'], ['rstd_o'])
                    for tc in range(8):
                        bank = 6 + tc % 2
                        t0 = NCTX + tc * 512
                        for kc in range(8):
                            MM(ps[bank][:, :], wdg[:, kc, :], hT[:, kc, t0:t0 + 512], [('hT', 1 + tc), 'wdg'], pk(bank),
                               start=(kc == 0), stop=(kc == 7))
                        ACT(X2[:, tc * 512:(tc + 1) * 512], ps[bank][:, :], AF.Silu, pk(bank), ['X2'])
                    for tc in range(8):
                        bank = 6 + tc % 2
                        pst = ps[bank][:].bitcast(BF16)
                        for bi in range(4):
                            b = tc * 4 + bi
                            STT('dve', Onb[bi][:], Obuf[:, b, :], rstd_o[:, b:b + 1], normw[:], ALU.mult, ALU.mult,
                                OK_ALL(b) + ['rstd_o', 'normw'], [('Onb', bi)])
                            TR(pst[:, bi * 128:(bi + 1) * 128], Onb[bi][:], ident_bf[:], [('Onb', bi), 'ident_bf'], pk(bank))
                        TT('dve', X1[:, tc * 512:(tc + 1) * 512], pst[:, 0:512], X2[:, tc * 512:(tc + 1) * 512], ALU.mult,
                           pk(bank) + ['X2'], ['X1'])
                    S.dma(og_s[h], X1[:, 0:NLAT], ['X1'], [('og_s', h)], semkey='og_s', waitall=True, eng='pool')
                    S.barrier()
            S.barrier()
        if 'fo' in phases:
            with ExitStack() as es3:
                def sb3(name, shape, dt=F32):
                    return es3.enter_context(nc.sbuf_tensor('sb_' + name, list(shape), dt))
                load_consts(('ccsc',), sb3, 'consts_fo')
                c64 = sb3('c64', [64, 128], BF16)
                rr = sb3('rr', [64, 64, 2, 128], BF16)
                S.dma(c64[:], dr['c64'], [], ['c64'])
                S.dma(rr[:], dr['rr'], [], ['rr'])
                wfv = sb3('wfv', [128, 8, 512], BF16)
                wfg = sb3('wfg', [128, 8, 512], BF16)
                wfm = sb3('wfm', [128, 4, 128], BF16)
                load_cast(wfv[:], win_d[:, :, OFF[0]:OFF[0] + 512], 8, 512, 'wfv')
                load_cast(wfg[:], win_d[:, :, OFF[1]:OFF[1] + 512], 8, 512, 'wfg')
                load_cast(wfm[:], wfm_d, 4, 128, 'wfm')
                uA = sb3('uA', [64, 64, 256], BF16)
                Y = sb3('Y', [64, 128, 2, 64], BF16)
                PT = sb3('PT', [128, 2, NLAT], BF16)
                G = sb3('G', [128, 256], BF16)
                SG = [sb3('SG0', [128, 512], BF16)] * 2
                FMc = [sb3('FMc0', [128, 512], BF16)] * 2
                LAT = [('hT', 1 + tc) for tc in range(8)]
                for g in range(4):
                    if g % 2 == 0:
                        for n0 in range(0, 64, 2):
                            bank = (n0 // 2) % 2
                            for a in range(2):
                                n2 = n0 + a
                                for kc in range(8):
                                    MM(ps[bank][0:64, a * 256:(a + 1) * 256], hT[:, kc, NCTX + n2:NCTX + NLAT:64],
                                       wfv[:, kc, g * 128:g * 128 + 256], LAT + ['wfv'], pk(bank),
                                       start=(kc == 0), stop=(kc == 7))
                            CP('act' if bank == 0 else 'dve', uA[:, n0:n0 + 2, :],
                               ps[bank][0:64, :].rearrange("p (a c) -> p a c", c=256), pk(bank), ['uA'])
                    go = (g % 2) * 128
                    for c0 in range(0, 128, 4):
                        bank = 2 + (c0 // 4) % 2
                        for a in range(4):
                            MM(ps[bank][0:64, a * 128:(a + 1) * 128], uA[:, :, go + c0 + a], c64[:, :], ['uA', 'c64'], pk(bank))
                        CP('act' if bank == 2 else 'dve', Y[:, c0:c0 + 4, :, :],
                           ps[bank][0:64, :].rearrange("p (c r k) -> p c r k", c=4, r=2), pk(bank), ['Y'])
                    if g == 0:
                        if 'uA' in dbg_d:
                            S.dma(dbg_d['uA'], uA[:, :, 0:128], ['uA'], ['dbg_uA'])
                    PTv = PT[:].rearrange("p r (k2 k1) -> p r k2 k1", k1=64)
                    for k0 in range(0, 64, 4):
                        bank = 4 + (k0 // 4) % 2
                        for a in range(4):
                            k1 = k0 + a
                            MM(ps[bank][:, a * 128:(a + 1) * 128], Y[:, :, 0, k1], rr[:, k1, 0, :], ['Y', 'rr'], pk(bank),
                               start=True, stop=False)
                            MM(ps[bank][:, a * 128:(a + 1) * 128], Y[:, :, 1, k1], rr[:, k1, 1, :], ['Y', 'rr'], pk(bank),
                               start=False, stop=True)
                        CP('act' if bank == 4 else 'dve',
                           PTv[:, :, :, k0:k0 + 4].rearrange("p r k a -> p a r k"),
                           ps[bank][:, :].rearrange("p (a r k) -> p a r k", a=4, r=2), pk(bank), ['PT'])
                    if g == 0 and 'PT' in dbg_d:
                        S.dma(dbg_d['PT'], PT[:], ['PT'], ['dbg_PT'])
                    MM(ps[6][:, 0:128], cst['ccsc'][:, 0:128], wfm[:, g, :], ['ccsc', 'wfm'], pk(6))
                    MM(ps[6][:, 128:256], cst['ccsc'][:, 128:256], wfm[:, g, :], ['ccsc', 'wfm'], pk(6))
                    CP('dve', G[:], ps[6][:, 0:256], pk(6), ['G'])
                    for tc in range(8):
                        i = tc % 2
                        t0 = NCTX + tc * 512
                        for kc in range(8):
                            MM(ps[7][:, :], wfg[:, kc, g * 128:(g + 1) * 128], hT[:, kc, t0:t0 + 512],
                               [('hT', 1 + tc), 'wfg'], pk(7), start=(kc == 0), stop=(kc == 7))
                        ACT(SG[i][:], ps[7][:, :], AF.Silu, pk(7), [('SG', 0)])
                        MM(ps[6][:, :], G[:, 0:128], PT[:, 0, tc * 512:(tc + 1) * 512], ['G', 'PT'], pk(6),
                           start=True, stop=False)
                        MM(ps[6][:, :], G[:, 128:256], PT[:, 1, tc * 512:(tc + 1) * 512], ['G', 'PT'], pk(6),
                           start=False, stop=True)
                        TT('dve', FMc[i][:], ps[6][:, :], SG[i][:], ALU.mult, pk(6) + [('SG', 0)], [('FMc', 0)])
                        if 'fm' in dbg_d:
                            S.dma(dbg_d['fm'][g, :, tc * 512:(tc + 1) * 512], FMc[i][:], [('FMc', 0)], ['dbg_fm'])
                        S.dma(fm_s[g, :, tc * 512:(tc + 1) * 512], FMc[i][:], [('FMc', 0)], [('fm_s', g, tc)], semkey='fm_s', waitall=True, eng='pool')
            S.barrier()

        if 'fin' in phases:
            wout = sb('wout', [128, 8, D], BF16)
            with ExitStack() as es4:
                def sb4(name, shape, dt=F32):
                    return es4.enter_context(nc.sbuf_tensor('sb_' + name, list(shape), dt))
                wrf = sb4('wrf', [128, 8, D], BF16)
                wrd = sb4('wrd', [128, 8, D], BF16)
                wfo = sb4('wfo', [128, 4, D], BF16)
                wdn = sb4('wdn', [128, 8, D], BF16)
                for cb in range(4):
                    cs_ = slice(cb * 256, (cb + 1) * 256)
                    S.dma(wrf[:, :, cs_], win_d[:, :, OFF[6] + cb * 256:OFF[6] + (cb + 1) * 256], [], [('wrf', cb)], eng='pool')
                    S.dma(wrd[:, :, cs_], win_d[:, :, OFF[7] + cb * 256:OFF[7] + (cb + 1) * 256], [], [('wrd', cb)], eng='pool')
                    S.dma(wfo[:, :, cs_], wfo_d[:, :, cs_], [], [('wfo', cb)], eng='pool')
                    S.dma(wdn[:, :, cs_], wdn_d[:, :, cs_], [], [('wdn', cb)], eng='pool')
                load_cast(wout[:], wout_d, 8, D, 'wout')
                if 'wrf' in dbg_d:
                    S.dma(dbg_d['wrf'], wrf[:], [('wrf', cb) for cb in range(4)], ['dbg_wrf'])
                    S.dma(dbg_d['hT2'], hT[:, :, NCTX:NCTX + 512], ALLHT, ['dbg_hT2'])
                ogc = [sb4('ogc%d' % i, [128, 8, 512], BF16) for i in range(2)]
                fmc = [sb4('fmc%d' % i, [128, 4, 512], BF16) for i in range(2)]
                sgf = [sb4('sgf%d' % i, [128, 512], BF16) for i in range(2)]
                sgd = [sb4('sgd%d' % i, [128, 512], BF16) for i in range(2)]
                m1 = [sb4('m1_%d' % i, [128, 512], BF16) for i in range(2)]
                m2 = [sb4('m2_%d' % i, [128, 512], BF16) for i in range(2)]
                mgo = [sb4('mgo%d' % i, [128, 512], BF16) for i in range(2)]
                for tc in range(8):
                    ci = tc % 2
                    t0 = NCTX + tc * 512
                    if 'dn' in phases:
                        S.dma(ogc[ci][:], og_s[:, :, tc * 512:(tc + 1) * 512].rearrange("h p t -> p h t"),
                              [('og_s', h) for h in range(H)], [('ogc', ci)])
                    else:
                        MEMSET('pool', ogc[ci][:], 0.0, [('ogc', ci)])
                    if 'fo' in phases:
                        S.dma(fmc[ci][:], fm_s[:, :, tc * 512:(tc + 1) * 512].rearrange("g p t -> p g t"),
                              [('fm_s', g, tc) for g in range(4)], [('fmc', ci)])
                    else:
                        MEMSET('pool', fmc[ci][:], 0.0, [('fmc', ci)])
                    for fc in range(8):
                        i = fc % 2
                        bb = 4 * i
                        fs = slice(fc * 128, (fc + 1) * 128)
                        for kc in range(8):
                            MM(ps[bb][:, :], wrf[:, kc, fs], hT[:, kc, t0:t0 + 512], [('hT', 1 + tc), ('wrf', fc // 2)], pk(bb),
                               start=(kc == 0), stop=(kc == 7))
                        for kc in range(8):
                            MM(ps[bb + 1][:, :], wrd[:, kc, fs], hT[:, kc, t0:t0 + 512], [('hT', 1 + tc), ('wrd', fc // 2)], pk(bb + 1),
                               start=(kc == 0), stop=(kc == 7))
                        for g in range(4):
                            MM(ps[bb + 2][:, :], wfo[:, g, fs], fmc[ci][:, g, :], [('fmc', ci), ('wfo', fc // 2)], pk(bb + 2),
                               start=(g == 0), stop=(g == 3))
                        for h in range(H):
                            MM(ps[bb + 3][:, :], wdn[:, h, fs], ogc[ci][:, h, :], [('ogc', ci), ('wdn', fc // 2)], pk(bb + 3),
                               start=(h == 0), stop=(h == 7))
                        ACT(sgf[i][:], ps[bb][:, :], AF.Sigmoid, pk(bb), [('sgf', i)])
                        ACT(sgd[i][:], ps[bb + 1][:, :], AF.Sigmoid, pk(bb + 1), [('sgd', i)])
                        TT('dve', m1[i][:], ps[bb + 2][:, :], sgf[i][:], ALU.mult, pk(bb + 2) + [('sgf', i)], [('m1', i)])
                        TT('dve', m2[i][:], ps[bb + 3][:, :], sgd[i][:], ALU.mult, pk(bb + 3) + [('sgd', i)], [('m2', i)])
                        TT('dve', mgo[i][:], m1[i][:], m2[i][:], ALU.add, [('m1', i), ('m2', i)], [('mgo', i)])
                        S.dma(mg_s[fc, :, tc * 512:(tc + 1) * 512], mgo[i][:], [('mgo', i)], [('mg_s', tc)], semkey='mg_s', waitall=True, eng='pool')
                        if 'taps' in dbg_d and tc == 0:
                            S.dma(dbg_d['taps'][0, fc], sgf[i][:], [('sgf', i)], ['dbg_t0'])
                            S.dma(dbg_d['taps'][1, fc], sgd[i][:], [('sgd', i)], ['dbg_t1'])
                            S.dma(dbg_d['taps'][2, fc], m1[i][:], [('m1', i)], ['dbg_t2'])
                            S.dma(dbg_d['taps'][3, fc], m2[i][:], [('m2', i)], ['dbg_t3'])
                        if 'merged' in dbg_d and tc == 0:
                            S.dma(dbg_d['merged'][fc], mgo[i][:], [('mgo', i)], ['dbg_mg'])
            S.barrier()
            with ExitStack() as es5:
                def sb5(name, shape, dt=F32):
                    return es5.enter_context(nc.sbuf_tensor('sb_' + name, list(shape), dt))
                lng = sb5('lng', [128, D])
                lnb = sb5('lnb', [128, D])
                S.dma(lng[:], lng_d, [], ['lng'])
                S.dma(lnb[:], lnb_d, [], ['lnb'])
                mg = [sb5('mg%d' % i, [128, 8, 512], BF16) for i in range(2)]
                NBF = 4
                xt = [sb5('fxt%d' % i, [128, D]) for i in range(NBF)]
                pet = [sb5('fpet%d' % i, [128, D]) for i in range(NBF)]
                pre = [sb5('pre%d' % i, [128, D]) for i in range(NBF)]
                sq = sb5('fsq', [128, D], BF16)
                st = [sb5('fst%d' % i, [128, 8]) for i in range(8)]

                def fA(ti):
                    tc, tt = ti // 4, ti % 4
                    ci = tc % 2
                    j = ti % NBF
                    xk, pk_, prk, sk = ('fxt', j), ('fpet', j), ('pre', j), ('fst', ti % 8)
                    stt = st[ti % 8]
                    if tt == 0:
                        S.dma(mg[ci][:], mg_s[:, :, tc * 512:(tc + 1) * 512].rearrange("f p t -> p f t"),
                              [('mg_s', tc)], [('mg', ci)])
                    S.dma(xt[j][:], x_d[ti * 128:(ti + 1) * 128, :], [], [xk])
                    S.dma(pet[j][:], dr['pe'][ti * 128:(ti + 1) * 128, :], [], [pk_])
                    TT('dve', xt[j][:], xt[j][:], pet[j][:], ALU.add, [xk, pk_], [xk])
                    for half in range(2):
                        bank = 2 * (ti % 4) + half
                        for kc in range(8):
                            MM(ps[bank][:, :], mg[ci][:, kc, tt * 128:(tt + 1) * 128],
                               wout[:, kc, half * 512:(half + 1) * 512], [('mg', ci), 'wout'], pk(bank),
                               start=(kc == 0), stop=(kc == 7))
                        TT('dve', pre[j][:, half * 512:(half + 1) * 512], ps[bank][:, :],
                           gxb[:, half * 512:(half + 1) * 512], ALU.mult, pk(bank) + ['gxb'], [prk])
                    STT('dve', pre[j][:], xt[j][:], ALPHA, pre[j][:], ALU.mult, ALU.add, [xk, prk], [prk])
                    RED(stt[:, 0:1], pre[j][:], [prk], [sk])
                    ACT(sq[:], pre[j][:], AF.Square, [prk], ['fsq', sk], accum=stt[:, 1:2])

                def fB(ti):
                    sk = ('fst', ti % 8)
                    stt = st[ti % 8]
                    TS('dve', stt[:, 2:3], stt[:, 0:1], 1.0 / D, None, ALU.mult, None, [sk], [sk])
                    TT('dve', stt[:, 3:4], stt[:, 2:3], stt[:, 2:3], ALU.mult, [sk], [sk])
                    STT('dve', stt[:, 4:5], stt[:, 1:2], 1.0 / D, stt[:, 3:4], ALU.mult, ALU.subtract, [sk], [sk])
                    TS('dve', stt[:, 7:8], stt[:, 4:5], EPS, None, ALU.add, None, [sk], [sk])
                    TT('pool', stt[:, 5:6], stt[:, 7:8], mhalf[:, 0:1], ALU.pow, [sk, 'mhalf'], [sk])

                def fC(ti):
                    j = ti % NBF
                    prk, sk = ('pre', j), ('fst', ti % 8)
                    stt = st[ti % 8]
                    STT('dve', stt[:, 6:7], stt[:, 2:3], -1.0, stt[:, 5:6], ALU.mult, ALU.mult, [sk], [sk])
                    ACT(pre[j][:], pre[j][:], AF.Identity, [prk, sk], [prk], scale=stt[:, 5:6], bias=stt[:, 6:7])
                    TT('dve', pre[j][:], pre[j][:], lng[:], ALU.mult, [prk, 'lng'], [prk])
                    TT('dve', pre[j][:], pre[j][:], lnb[:], ALU.add, [prk, 'lnb'], [prk])
                    S.dma(out_d[ti * 128:(ti + 1) * 128, :], pre[j][:], [prk], [('out', ti)], semkey='out', eng='pool')

                for n in range(32 + 2):
                    if n < 32:
                        fA(n)
                    if 0 <= n - 1 < 32:
                        fB(n - 1)
                    if 0 <= n - 2 < 32:
                        fC(n - 2)
        S.emit()
    return nc, consts


_CACHE = {}


def _prep_shared(inputs):
    f = np.float32
    w_in = np.asarray(inputs['w_in'], f)[0]
    sh = {}
    sh['w_mod'] = np.ascontiguousarray(np.asarray(inputs['w_mod'], f)[0].reshape(128, 8, 3 * D))
    sh['b_mod2'] = np.ascontiguousarray(np.tile(np.asarray(inputs['b_mod'], f)[0][None], (2, 1)))
    sh['w_in_r'] = np.ascontiguousarray(w_in.reshape(8, 128, INCOLS).transpose(1, 0, 2))
    sh['w_dn_r'] = np.ascontiguousarray(np.asarray(inputs['w_dn_out'], f)[0].reshape(8, 128, D).transpose(1, 0, 2))
    sh['w_out_r'] = np.ascontiguousarray(np.asarray(inputs['w_out'], f)[0].reshape(8, 128, D).transpose(1, 0, 2))
    sh['w_fo_r'] = np.ascontiguousarray(np.asarray(inputs['w_f_out'], f)[0].reshape(4, 128, D).transpose(1, 0, 2))
    sh['w_fmix_r'] = np.ascontiguousarray(np.asarray(inputs['w_fmix'], f)[0].transpose(1, 0, 2))
    sh['convw_r'] = np.ascontiguousarray(np.asarray(inputs['conv_w'], f)[0].T.reshape(24, 128, 3).transpose(1, 0, 2))
    sh['alog_t'] = np.ascontiguousarray(np.tile(np.asarray(inputs['a_log'], f)[0].reshape(1, 16), (128, 1)))
    sh['dtb_t'] = np.ascontiguousarray(np.tile(np.asarray(inputs['dt_bias'], f)[0].reshape(1, 16), (128, 1)))
    sh['normw_t'] = np.ascontiguousarray(np.tile(np.asarray(inputs['dn_norm_w'], f)[0].reshape(1, 128), (128, 1)))
    sh['lng_t'] = np.ascontiguousarray(np.tile(np.asarray(inputs['ln_g'], f)[0].reshape(1, D), (128, 1)))
    sh['lnb_t'] = np.ascontiguousarray(np.tile(np.asarray(inputs['ln_b'], f)[0].reshape(1, D), (128, 1)))
    return sh


def make_in_maps(inputs, consts, cores):
    f = np.float32
    sh = _prep_shared(inputs)
    x = np.asarray(inputs['x'], f)
    ctx = np.asarray(inputs['ctx'], f)
    c = np.asarray(inputs['c'], f)
    c_ctx = np.asarray(inputs['c_ctx'], f)
    maps = []
    for b in cores:
        m = dict(consts)
        m.update(sh)
        m['x'] = np.ascontiguousarray(x[b])
        m['ctx'] = np.ascontiguousarray(ctx[b])
        m['cc'] = np.ascontiguousarray(np.stack([c[b].reshape(128, 8), c_ctx.reshape(128, 8)], -1).reshape(128, 16))
        maps.append(m)
    return maps


def kernel(**inputs):
    if 'nc' not in _CACHE:
        _CACHE['nc'] = build()
    nc, consts = _CACHE['nc']
    maps = make_in_maps(inputs, consts, range(8))
    res = run_bass_kernel_spmd(nc, maps, core_ids=list(range(8)))
    out = np.stack([np.asarray(r['out'], np.float32) for r in res.results], 0)
    return out
```

```python
from contextlib import ExitStack
import numpy as np
import ml_dtypes
import concourse.bass as bass
import concourse.mybir as mybir
from concourse.bass_utils import run_bass_kernel_spmd

F32 = mybir.dt.float32
BF16 = mybir.dt.bfloat16
AF = mybir.ActivationFunctionType
ALU = mybir.AluOpType
AX = mybir.AxisListType

D = 1024
NLAT = 4096
NCTX = 256
NTOK = NCTX + NLAT
H = 8
IN_SPLITS = (512, 512, 3072, 1024, 16, 16, 1024, 1024)
OFF = [0]
for _s in IN_SPLITS:
    OFF.append(OFF[-1] + _s)
INCOLS = OFF[-1]
EPS = 1e-6
ALPHA = 2.0 ** 0.25
NEG = -30000.0
import os
NOEXCL = bool(os.environ.get('NOEXCL'))


class Sched:
    def __init__(self, nc):
        self.nc = nc
        self.ops = []
        self.barriers = []

    def barrier(self):
        self.barriers.append(len(self.ops))

    def add(self, eng, fn, reads=(), writes=(), dma=False, semkey=None, waitall=False):
        self.ops.append(dict(eng=eng, fn=fn, reads=tuple(reads), writes=tuple(writes), dma=dma,
                             semkey=semkey, waitall=waitall))

    def pe(self, fn, r, w):
        self.add('pe', fn, r, w)

    def act(self, fn, r, w):
        self.add('act', fn, r, w)

    def dve(self, fn, r, w):
        self.add('dve', fn, r, w)

    def pool(self, fn, r, w):
        self.add('pool', fn, r, w)

    def dma(self, out, in_, r, w, eng='sp', semkey=None, waitall=False):
        self.add(eng, lambda e: e.dma_start(out=out, in_=in_), r, w, dma=True, semkey=semkey, waitall=waitall)

    def emit(self):
        nc = self.nc
        ops = self.ops
        last_writer = {}
        readers = {}
        last_on_eng = {}
        last_dma = {}
        pending = {}
        bank_last = {}
        bars = set(self.barriers)
        for i, op in enumerate(ops):
            if i in bars:
                bd = set(last_on_eng.values()) | set(last_dma.values())
                for en in ('pe', 'act', 'dve', 'pool', 'sp'):
                    pending[en] = set(bd) | pending.get(en, set())
            deps = set()
            if pending.get(op['eng']):
                deps |= pending.pop(op['eng'])
            for k in op['reads']:
                if k in last_writer:
                    deps.add(last_writer[k])
            for k in op['writes']:
                if k in last_writer:
                    deps.add(last_writer[k])
                deps.update(readers.get(k, ()))
            deps.discard(i)
            for k in op['reads'] + op['writes']:
                if isinstance(k, tuple) and k and k[0] == 'ps':
                    b = k[1]
                    la = bank_last.get(b)
                    if la is not None and ops[la]['eng'] != op['eng'] and not NOEXCL:
                        deps.add(la)
                    bank_last[b] = i
            if op['eng'] == 'pe':
                deps = {d for d in deps if not (ops[d]['eng'] == 'pe' and not ops[d]['dma'])}
            op['deps'] = deps
            if op['dma']:
                last_dma[op['writes'][0]] = i
            else:
                last_on_eng[op['eng']] = i
            for k in op['reads']:
                readers.setdefault(k, []).append(i)
            for k in op['writes']:
                last_writer[k] = i
                readers[k] = []
        needed = set()
        for op in ops:
            needed.update(op['deps'])
        with ExitStack() as es:
            engsem = {}
            for en in ('pe', 'act', 'dve', 'pool'):
                engsem[en] = es.enter_context(nc.semaphore('s_' + en))
            dmasem = {}
            dmacnt = {}
            cnt = {en: 0 for en in engsem}
            for i, op in enumerate(ops):
                if op['dma']:
                    key = op['semkey'] if op['semkey'] is not None else op['writes'][0]
                    if key not in dmasem:
                        dmasem[key] = es.enter_context(nc.semaphore('d%d' % len(dmasem)))
                        dmacnt[key] = 0
                    dmacnt[key] += 16
                    op['sig'] = (dmasem[key], dmacnt[key], 16)
                    op['sigkey'] = key
                elif i in needed:
                    cnt[op['eng']] += 1
                    op['sig'] = (engsem[op['eng']], cnt[op['eng']], 1)
                else:
                    op['sig'] = None
            grp_last = {}
            for i, op in enumerate(ops):
                if op['dma']:
                    grp_last[op['sigkey']] = i
            if os.environ.get('KDEBUG'):
                print('nsem_dma', len(dmasem), 'nops', len(ops), 'cnt', cnt, 'sems', [str(v) for v in list(dmasem.values())[:3]], [str(v) for v in engsem.values()])
            block = es.enter_context(nc.Block())
            by_eng = {}
            for i, op in enumerate(ops):
                by_eng.setdefault(op['eng'], []).append(i)

            def run(e, idxs, final=False):
                waited = {}
                for i in idxs:
                    op = ops[i]
                    need = {}
                    for d in op['deps']:
                        s = ops[d]['sig']
                        if ops[d]['dma'] and ops[d]['waitall'] and i > grp_last[ops[d]['sigkey']]:
                            s = (s[0], dmacnt[ops[d]['sigkey']], 16)
                        sid = id(s[0])
                        if s[1] > need.get(sid, (None, 0))[1]:
                            need[sid] = (s[0], s[1])
                    for sid, (sem, val) in need.items():
                        if waited.get(sid, 0) < val:
                            e.wait_ge(sem, val)
                            waited[sid] = val
                    ins = op['fn'](e)
                    if op['sig'] is not None:
                        ins.then_inc(op['sig'][0], op['sig'][2])
                if final:
                    for key, sem in dmasem.items():
                        if waited.get(id(sem), 0) < dmacnt[key]:
                            e.wait_ge(sem, dmacnt[key])

            @block.sync
            def _(e):
                run(e, by_eng.get('sp', []), final=True)

            @block.tensor
            def _(e):
                run(e, by_eng.get('pe', []))

            @block.scalar
            def _(e):
                run(e, by_eng.get('act', []))

            @block.vector
            def _(e):
                run(e, by_eng.get('dve', []))

            @block.gpsimd
            def _(e):
                run(e, by_eng.get('pool', []))


def _bf(a):
    return np.ascontiguousarray(a).astype(ml_dtypes.bfloat16)


def host_consts():
    c = {}
    c['ident_bf'] = _bf(np.eye(128))
    c['ident_f'] = np.eye(128, dtype=np.float32)
    c['ones_bf'] = _bf(np.ones((128, 128)))
    selx = np.zeros((2, 128), np.float32)
    selx[0] = 1.0
    c['selx'] = selx
    q = D // 4
    om = 1.0 / (10000.0 ** (np.arange(q, dtype=np.float32) / q))
    pr = np.arange(64, dtype=np.float32)[:, None] * om
    er = np.concatenate([np.sin(pr), np.cos(pr)], -1)
    pe = np.concatenate([np.broadcast_to(er[:, None, :], (64, 64, 512)),
                         np.broadcast_to(er[None, :, :], (64, 64, 512))], -1)
    c['pe'] = np.ascontiguousarray(pe.reshape(4096, D)).astype(np.float32)
    idx = np.arange(128)
    same = (idx[:, None] // 64) == (idx[None, :] // 64)
    c['Uf'] = (same & (idx[:, None] <= idx[None, :])).astype(np.float32)
    c['Ub'] = (same & (idx[:, None] >= idx[None, :])).astype(np.float32)
    c['SCm'] = same.astype(np.float32)
    vf = same & (idx[None, :] >= idx[:, None])
    vb = same & (idx[None, :] <= idx[:, None])
    c['mnegF'] = np.where(vf, 0.0, NEG).astype(np.float32)
    c['mnegB'] = np.where(vb, 0.0, NEG).astype(np.float32)
    sub = (idx[:, None] // 16) == (idx[None, :] // 16)
    c['mdF'] = _bf(sub & (idx[None, :] > idx[:, None]))
    c['mdB'] = _bf(sub & (idx[None, :] < idx[:, None]))
    c['moF'] = _bf(same & ~sub & (idx[None, :] > idx[:, None]))
    c['moB'] = _bf(same & ~sub & (idx[None, :] < idx[:, None]))
    sel = np.zeros((128, 16, 128), np.float32)
    for c_ in range(16):
        sel[c_, c_, :] = 1.0
        sel[16 + c_, c_, :] = 1.0
    c['sel32'] = _bf(sel)
    a64 = 2 * np.pi * np.outer(np.arange(64), np.arange(64)) / 64
    c['c64'] = _bf(np.concatenate([np.cos(a64), -np.sin(a64)], 1))
    kk = np.arange(4096)
    th = 2 * np.pi * np.outer(np.arange(64), kk) / 4096
    Mc = (np.cos(th) / 64.0).reshape(64, 64, 64)
    Ms = (np.sin(th) / 64.0).reshape(64, 64, 64)
    Mc = Mc.transpose(0, 2, 1)
    Ms = Ms.transpose(0, 2, 1)
    rr = np.zeros((64, 64, 2, 2, 64))
    rr[:, :, 0, 0] = Mc
    rr[:, :, 0, 1] = -Ms
    rr[:, :, 1, 0] = Ms
    rr[:, :, 1, 1] = Mc
    c['rr'] = _bf(rr.reshape(64, 64, 2, 128))
    ac = 2 * np.pi * np.outer(np.arange(128), np.arange(128)) / 128
    c['ccsc'] = _bf(np.concatenate([np.cos(ac), np.sin(ac)], 1) / np.sqrt(128.0))
    return c


def build(dbg=(), phases=('dn', 'fo', 'fin'), heads=range(H)):
    nc = bass.Bass("TRN2", target_bir_lowering=False)
    consts = host_consts()
    dr = {}

    def din(name, shape, dt=F32):
        dr[name] = nc.dram_tensor(name, list(shape), dt, kind="ExternalInput").ap()
        return dr[name]

    x_d = din('x', [NLAT, D])
    ctx_d = din('ctx', [NCTX, D])
    cc_d = din('cc', [128, 16])
    wmod_d = din('w_mod', [128, 8, 3 * D])
    bmod_d = din('b_mod2', [2, 3 * D])
    win_d = din('w_in_r', [128, 8, INCOLS])
    wdn_d = din('w_dn_r', [128, 8, D])
    wout_d = din('w_out_r', [128, 8, D])
    wfo_d = din('w_fo_r', [128, 4, D])
    wfm_d = din('w_fmix_r', [128, 4, 128])
    convw_d = din('convw_r', [128, 24, 3])
    alog_d = din('alog_t', [128, 16])
    dtb_d = din('dtb_t', [128, 16])
    normw_d = din('normw_t', [128, 128])
    lng_d = din('lng_t', [128, D])
    lnb_d = din('lnb_t', [128, D])
    for k, v in consts.items():
        din(k, v.shape, BF16 if v.dtype == ml_dtypes.bfloat16 else F32)
    out_d = nc.dram_tensor('out', [NLAT, D], F32, kind="ExternalOutput").ap()
    og_s = nc.dram_tensor('og_s', [H, 128, NLAT], BF16).ap()
    fm_s = nc.dram_tensor('fm_s', [4, 128, NLAT], BF16).ap()
    mg_s = nc.dram_tensor('mg_s', [8, 128, NLAT], BF16).ap()
    dbg_d = {}
    for name, shape, dt in dbg:
        dbg_d[name] = nc.dram_tensor('dbg_' + name, list(shape), dt, kind="ExternalOutput").ap()

    S = Sched(nc)

    def MM(out, lhsT, rhs, r, w, start=True, stop=True):
        S.pe(lambda e: e.matmul(out, lhsT=lhsT, rhs=rhs, start=start, stop=stop), r, w)

    def TR(out, in_, ident, r, w):
        S.pe(lambda e: e.transpose(out=out, in_=in_, identity=ident), r, w)

    def TT(eng, out, in0, in1, op, r, w):
        S.add(eng, lambda e: e.tensor_tensor(out=out, in0=in0, in1=in1, op=op), r, w)

    def TS(eng, out, in0, s1, s2, op0, op1, r, w):
        if s2 is None:
            S.add(eng, lambda e: e.tensor_scalar(out=out, in0=in0, scalar1=s1, scalar2=None, op0=op0), r, w)
        else:
            S.add(eng, lambda e: e.tensor_scalar(out=out, in0=in0, scalar1=s1, scalar2=s2, op0=op0, op1=op1), r, w)

    def STT(eng, out, in0, scalar, in1, op0, op1, r, w):
        S.add(eng, lambda e: e.scalar_tensor_tensor(out=out, in0=in0, scalar=scalar, in1=in1, op0=op0, op1=op1),
              r, w)

    def ACT(out, in_, func, r, w, bias=None, scale=None, accum=None):
        kw = {}
        if bias is not None:
            kw['bias'] = bias
        if scale is not None:
            kw['scale'] = scale
        if accum is not None:
            kw['accum_out'] = accum
        S.act(lambda e: e.activation(out=out, in_=in_, func=func, **kw), r, w)

    def CP(eng, out, in_, r, w):
        if eng == 'act':
            S.act(lambda e: e.activation(out=out, in_=in_, func=AF.Copy), r, w)
        else:
            S.add(eng, lambda e: e.tensor_copy(out=out, in_=in_), r, w)

    def RED(out, in_, r, w):
        S.dve(lambda e: e.tensor_reduce(out=out, in_=in_, axis=AX.X, op=ALU.add), r, w)

    def MEMSET(eng, ap, val, w):
        S.add(eng, lambda e: e.memset(ap, val), [], w)

    with ExitStack() as es:
        def sb(name, shape, dt=F32):
            return es.enter_context(nc.sbuf_tensor('sb_' + name, list(shape), dt))

        ps = [es.enter_context(nc.psum_tensor('ps%d' % i, [128, 512], F32)) for i in range(8)]

        def pk(b, q0=0, nq=4):
            return [('ps', b, q) for q in range(q0, q0 + nq)]

        def psq(b, q, nq=1):
            return ps[b][:, q * 128:(q + nq) * 128]

        cst = {}

        def load_consts(names, alloc, grp):
            for name in names:
                v = consts[name]
                cst[name] = alloc(name, v.shape, BF16 if v.dtype == ml_dtypes.bfloat16 else F32)
                S.dma(cst[name][:], dr[name], [], [name], semkey=grp, waitall=True)

        load_consts(('ident_bf', 'ident_f', 'ones_bf', 'selx'), sb, 'consts')
        ident_bf, ident_f, selx = cst['ident_bf'], cst['ident_f'], cst['selx']
        hT = sb('hT', [128, 8, NTOK], BF16)
        mhalf = sb('mhalf', [128, 32])
        MEMSET('pool', mhalf[:], -0.5, ['mhalf'])
        modT = sb('modT', [128, 48])
        gxb = sb('gxb', [128, D])

        def load_cast(dst, src, a, b, wkey):
            bc = max(1, 2048 // a)
            for b0 in range(0, b, bc):
                bn = min(bc, b - b0)
                S.dma(dst[:, :, b0:b0 + bn], src[:, :, b0:b0 + bn], [], [wkey], eng='pool')

        def hT_keys_for_tokens(t0, t1):
            ks = set()
            for t in range(t0, t1, 128):
                ks.add(('hT', 0 if t < NCTX else 1 + (t - NCTX) // 512))
            return sorted(ks)

        with ExitStack() as es0:
            def sb0(name, shape, dt=F32):
                return es0.enter_context(nc.sbuf_tensor('sb_' + name, list(shape), dt))
            cc = sb0('cc', [128, 16])
            sc = sb0('sc', [128, 16])
            wm = [sb0('wm%d' % i, [128, 3 * D]) for i in range(2)]
            bm = sb0('bm', [2, 3 * D])
            modrow = sb0('modrow', [2, 3 * D])
            S.dma(cc[:], cc_d, [], ['cc'])
            S.dma(bm[:], bmod_d, [], ['bm'])
            ACT(sc[:], cc[:], AF.Silu, ['cc'], ['sc'])
            sc3 = sc[:].rearrange("p (k v) -> p k v", v=2)
            for k in range(8):
                S.dma(wm[k % 2][:], wmod_d[:, k, :], [], ['wm%d' % (k % 2)])
                for j in range(6):
                    MM(ps[j][0:2, :], sc3[:, k, :], wm[k % 2][:, j * 512:(j + 1) * 512],
                       ['sc', 'wm%d' % (k % 2)], pk(j), start=(k == 0), stop=(k == 7))
            for j in range(6):
                TT('dve', modrow[:, j * 512:(j + 1) * 512], ps[j][0:2, :], bm[:, j * 512:(j + 1) * 512], ALU.add,
                   pk(j) + ['bm'], ['modrow'])
            for j in range(24):
                MM(ps[6][:, 2 * j:2 * j + 2], modrow[0:2, j * 128:(j + 1) * 128], ident_f[0:2, 0:2],
                   ['modrow', 'ident_f'], pk(6))
            CP('dve', modT[:], ps[6][:, 0:48], pk(6), ['modT'])
            TS('dve', modT[:, 16:32], modT[:, 16:32], 1.0, None, ALU.add, None, ['modT'], ['modT'])
            for jj in range(2):
                MM(ps[7][:, :], selx[0:2, :], modrow[0:2, 2048 + jj * 512:2048 + (jj + 1) * 512],
                   ['modrow', 'selx'], pk(7))
                CP('act', gxb[:, jj * 512:(jj + 1) * 512], ps[7][:, :], pk(7), ['gxb'])

            NBF = 4
            xt = [sb0('xt%d' % i, [128, D]) for i in range(NBF)]
            pet = [sb0('pet%d' % i, [128, D]) for i in range(NBF)]
            sq = sb0('sq', [128, D], BF16)
            xn = [sb0('xn%d' % i, [128, 4, D], BF16) for i in range(2)]
            st = [sb0('st%d' % i, [128, 8]) for i in range(8)]
            groups = [(1, [(ctx_d, 0), (ctx_d, 1)], 0)]
            for g in range(8):
                groups.append((0, [(x_d, 4 * g + i) for i in range(4)], NCTX + g * 512))
            flat = []
            for gi, (v, tiles, tok0) in enumerate(groups):
                for i, (src, ti) in enumerate(tiles):
                    flat.append((gi, i, v, src, ti, len(tiles), tok0))

            def lnA(n):
                gi, i, v, src, ti, nt, tok0 = flat[n]
                j = n % NBF
                stt, sk, xk = st[n % 8], 'st%d' % (n % 8), 'xt%d' % j
                S.dma(xt[j][:], src[ti * 128:(ti + 1) * 128, :], [], [xk])
                if v == 0:
                    S.dma(pet[j][:], dr['pe'][ti * 128:(ti + 1) * 128, :], [], ['pet%d' % j])
                    TT('dve', xt[j][:], xt[j][:], pet[j][:], ALU.add, [xk, 'pet%d' % j], [xk])
                RED(stt[:, 0:1], xt[j][:], [xk], [sk])
                ACT(sq[:], xt[j][:], AF.Square, [xk], ['sq', sk], accum=stt[:, 1:2])

            def lnB(n):
                stt, sk = st[n % 8], 'st%d' % (n % 8)
                TS('dve', stt[:, 2:3], stt[:, 0:1], 1.0 / D, None, ALU.mult, None, [sk], [sk])
                TT('dve', stt[:, 3:4], stt[:, 2:3], stt[:, 2:3], ALU.mult, [sk], [sk])
                STT('dve', stt[:, 4:5], stt[:, 1:2], 1.0 / D, stt[:, 3:4], ALU.mult, ALU.subtract, [sk], [sk])
                TS('dve', stt[:, 7:8], stt[:, 4:5], EPS, None, ALU.add, None, [sk], [sk])
                TT('pool', stt[:, 5:6], stt[:, 7:8], mhalf[:, 0:1], ALU.pow, [sk, 'mhalf'], [sk])

            def lnC(n):
                gi, i, v, src, ti, nt, tok0 = flat[n]
                j = n % NBF
                gp = gi % 2
                stt, sk, xk = st[n % 8], 'st%d' % (n % 8), 'xt%d' % j
                STT('dve', stt[:, 6:7], stt[:, 2:3], -1.0, stt[:, 5:6], ALU.mult, ALU.mult, [sk], [sk])
                ACT(xn[gp][:, i, :], xt[j][:], AF.Identity, [xk, sk], [('xn', gp, i)], scale=stt[:, 5:6], bias=stt[:, 6:7])
                if i == nt - 1:
                    for kc in range(8):
                        pb = 6 + (kc % 2)
                        pst = ps[pb][:].bitcast(BF16)
                        for ii in range(nt):
                            TR(pst[:, ii * 128:(ii + 1) * 128], xn[gp][:, ii, kc * 128:(kc + 1) * 128], ident_bf[:],
                               [('xn', gp, ii), 'ident_bf'], pk(pb))
                        dst = hT[:, kc, tok0:tok0 + nt * 128]
                        sc_ap = modT[:, 2 * (8 + kc) + v:2 * (8 + kc) + v + 1]
                        sh_ap = modT[:, 2 * kc + v:2 * kc + v + 1]
                        if kc % 2 == 0:
                            ACT(dst, pst[:, 0:nt * 128], AF.Identity, pk(pb) + ['modT'], [('hT', gi)], scale=sc_ap, bias=sh_ap)
                        else:
                            TS('dve', dst, pst[:, 0:nt * 128], sc_ap, sh_ap, ALU.mult, ALU.add, pk(pb) + ['modT'], [('hT', gi)])

            NT0 = len(flat)
            for n in range(NT0 + 2):
                if n < NT0:
                    lnA(n)
                if 0 <= n - 1 < NT0:
                    lnB(n - 1)
                if 0 <= n - 2 < NT0:
                    lnC(n - 2)
        S.barrier()
        ALLHT = [('hT', g) for g in range(9)]
        if 'hT' in dbg_d:
            for kc in range(8):
                S.dma(dbg_d['hT'][kc], hT[:, kc, :], ALLHT, ['dbg_hT'])

        if 'dn' in phases:
            with ExitStack() as es2:
                def sb2(name, shape, dt=F32):
                    return es2.enter_context(nc.sbuf_tensor('sb_' + name, list(shape), dt))
                NB = NTOK // 128
                load_consts(('Uf', 'Ub', 'SCm', 'mnegF', 'mnegB', 'mdF', 'mdB', 'moF', 'moB'), sb2, 'consts_dn')
                wbd = sb2('wbd', [128, 8, 32], BF16)
                load_cast(wbd[:], win_d[:, :, OFF[4]:OFF[4] + 32], 8, 32, 'wbd')
                beta_tm = sb2('beta_tm', [128, NB, 16])
                gc_tm = sb2('gc_tm', [128, NB, 16])
                egc_tm = sb2('egc_tm', [128, NB, 16])
                egl_tm = sb2('egl_tm', [128, NB, 16])
                gcT = sb2('gcT', [128, NB, 128], BF16)
                selh = sb2('selh', [128, 2, 128], BF16)
                MEMSET('pool', gcT[:], 0.0, ['gcT'])
                es2p = ExitStack()
                def sb2p(name, shape, dt=F32):
                    return es2p.enter_context(nc.sbuf_tensor('sb_' + name, list(shape), dt))
                raw_tm = sb2p('raw_tm', [128, NB, 32])
                gcb_tm = sb2p('gcb_tm', [128, NB, 16], BF16)
                gch_tm = sb2p('gch_tm', [128, NB, 16])
                gcl_tm = sb2p('gcl_tm', [128, NB, 16])
                gcHL_tm = sb2p('gcHL_tm', [128, NB, 32], BF16)
                g_tm = sb2p('g_tm', [128, NB, 16])
                tot_tm = sb2p('tot_tm', [128, NB, 16])
                tmp_tm = sb2p('tmp_tm', [128, NB, 16])
                alog = sb2p('alog', [128, 16])
                dtb = sb2p('dtb', [128, 16])
                nega = sb2p('nega', [128, 16])
                S.dma(alog[:], alog_d, [], ['alog'])
                S.dma(dtb[:], dtb_d, [], ['dtb'])
                ACT(nega[:], alog[:], AF.Exp, ['alog'], ['nega'])
                TS('dve', nega[:], nega[:], -1.0, None, ALU.mult, None, ['nega'], ['nega'])
                for b0 in range(0, NB, 16):
                    nb_ = min(16, NB - b0)
                    bank = 6 + (b0 // 16) % 2
                    for bi in range(nb_):
                        blk = b0 + bi
                        for kc in range(8):
                            MM(ps[bank][:, bi * 32:(bi + 1) * 32], hT[:, kc, blk * 128:(blk + 1) * 128], wbd[:, kc, :],
                               hT_keys_for_tokens(blk * 128, blk * 128 + 128) + ['wbd'], pk(bank),
                               start=(kc == 0), stop=(kc == 7))
                    CP('act', raw_tm[:, b0:b0 + nb_, :],
                       ps[bank][:, 0:nb_ * 32].rearrange("p (b c) -> p b c", c=32), pk(bank), ['raw_tm'])
                ACT(beta_tm[:], raw_tm[:, :, 0:16], AF.Sigmoid, ['raw_tm'], ['beta_tm'])
                TT('dve', tmp_tm[:], raw_tm[:, :, 16:32], dtb[:].unsqueeze(1).to_broadcast([128, NB, 16]), ALU.add,
                   ['raw_tm', 'dtb'], ['tmp_tm'])
                ACT(tmp_tm[:], tmp_tm[:], AF.Exp, ['tmp_tm'], ['tmp_tm'])
                ACT(tmp_tm[:], tmp_tm[:], AF.Ln, ['tmp_tm'], ['tmp_tm'], bias=1.0)
                TT('dve', g_tm[:], tmp_tm[:], nega[:].unsqueeze(1).to_broadcast([128, NB, 16]), ALU.mult,
                   ['tmp_tm', 'nega'], ['g_tm'])
                for half in range(2):
                    b0 = half * 17
                    for bi in range(17):
                        blk = b0 + bi
                        MM(ps[6][:, bi * 16:bi * 16 + 8], cst['Uf'][:], g_tm[:, blk, 0:8], ['g_tm', 'Uf'], pk(6))
                        MM(ps[6][:, bi * 16 + 8:bi * 16 + 16], cst['Ub'][:], g_tm[:, blk, 8:16], ['g_tm', 'Ub'], pk(6))
                        MM(ps[7][:, bi * 16:bi * 16 + 16], cst['SCm'][:], g_tm[:, blk, :], ['g_tm', 'SCm'], pk(7))
                    CP('dve', gc_tm[:, b0:b0 + 17, :], ps[6][:, 0:272].rearrange("p (b c) -> p b c", c=16), pk(6),
                       ['gc_tm'])
                    CP('dve', tot_tm[:, b0:b0 + 17, :], ps[7][:, 0:272].rearrange("p (b c) -> p b c", c=16), pk(7),
                       ['tot_tm'])
                ACT(egc_tm[:], gc_tm[:], AF.Exp, ['gc_tm'], ['egc_tm'])
                CP('dve', gcb_tm[:], gc_tm[:], ['gc_tm'], ['gcb_tm'])
                CP('dve', gch_tm[:], gcb_tm[:], ['gcb_tm'], ['gch_tm'])
                TT('dve', gcl_tm[:], gc_tm[:], gch_tm[:], ALU.subtract, ['gc_tm', 'gch_tm'], ['gcl_tm'])
                CP('dve', gcHL_tm[:, :, 0:16], gcb_tm[:], ['gcb_tm'], ['gcHL_tm'])
                CP('dve', gcHL_tm[:, :, 16:32], gcl_tm[:], ['gcl_tm'], ['gcHL_tm'])
                for b0 in range(0, NB, 8):
                    nb_ = min(8, NB - b0)
                    pst = ps[6][:].bitcast(BF16)
                    for bi in range(nb_):
                        TR(pst[0:32, bi * 128:(bi + 1) * 128], gcHL_tm[:, b0 + bi, :], ident_bf[:], ['gcHL_tm', 'ident_bf'], pk(6))
                    CP('dve', gcT[0:32, b0:b0 + nb_, :], pst[0:32, 0:nb_ * 128].rearrange("p (b t) -> p b t", t=128),
                       pk(6), ['gcT'])
                TT('dve', tmp_tm[:], tot_tm[:], gc_tm[:], ALU.subtract, ['tot_tm', 'gc_tm'], ['tmp_tm'])
                ACT(egl_tm[:], tmp_tm[:], AF.Exp, ['tmp_tm'], ['egl_tm'])
                if 'gcT' in dbg_d:
                    S.dma(dbg_d['gcT'], gcT[:], ['gcT'], ['dbg_gcT'])
                if 'tm' in dbg_d:
                    S.dma(dbg_d['tm'][0], beta_tm[:], ['beta_tm'], ['dbg_tm0'])
                    S.dma(dbg_d['tm'][1], gc_tm[:], ['gc_tm'], ['dbg_tm1'])
                    S.dma(dbg_d['tm'][2], egl_tm[:], ['egl_tm'], ['dbg_tm2'])
                es2p.close()
                S.barrier()

                X0 = sb2('X0', [128, 4360], BF16)
                X1 = sb2('X1', [128, 4360], BF16)
                X2 = sb2('X2', [128, 4360], BF16)
                QT = sb2('QT', [128, NTOK], BF16)
                KT = sb2('KT', [128, NTOK], BF16)
                Ktok = sb2('Ktok', [128, NB, 128], BF16)
                Vtok = sb2('Vtok', [128, NB, 128], BF16)
                Obuf = sb2('Obuf', [128, 32, 128], BF16)
                wqkv = sb2('wqkv', [128, 8, 384], BF16)
                wdg = sb2('wdg', [128, 8, 128], BF16)
                convw = sb2('convw', [128, 24, 3])
                normw = sb2('normw', [128, 128])
                sqb = sb2('sqb', [128, 512], BF16)
                sqb2 = [sqb, sb2('sqb1', [128, 512], BF16)]
                Dg = [sb2('Dg%d' % j, [128, 128], BF16) for j in range(3)]
                lnt = [sb2('lnt0', [128, 512]), sb2('lnt1', [128, 512], BF16)]
                ssq = sb2('ssq', [128, 32])
                rstd_o = sb2('rstd_o', [128, 32])
                Onb = [sb2('Onb%d' % i, [128, 128], BF16) for i in range(4)]
                S.dma(convw[:], convw_d, [], ['convw'])
                S.dma(normw[:], normw_d, [], ['normw'])
                MEMSET('pool', X0[:], 0.0, ['X0'])
                Sf32 = [sb2('Sf32_%d' % d, [128, 128]) for d in range(2)]
                Sbf = [sb2('Sbf_%d' % d, [128, 128], BF16) for d in range(2)]
                vnew = [[sb2('vnew_%d%d' % (d, c_), [128, 128], BF16) for c_ in range(2)] for d in range(2)]
                LOOK = 3
                carve = [[X1, 0, 4360], [X2, 0, 4360], [X0, 260, 4352]]

                def newbuf(nm, w, dt):
                    ne = w * (2 if dt == F32 else 1)
                    for reg in carve:
                        a = (reg[1] + 1) // 2 * 2
                        if a + ne <= reg[2]:
                            reg[1] = a + ne
                            v = reg[0][:, a:a + ne]
                            return v.bitcast(F32) if dt == F32 else v
                    return sb2(nm, [128, w], dt)[:]

                INTERNAL = ('Gdh', 'Gdl', 'X', 'N', 'Bns', 'Bd', 'Bo', 'LLo', 'LB0', 'LB1', 'Q0', 'Q1', 'nMT',
                            'X0', 'Xit0', 'Xit1', 'Kg')
                ISET = {}
                PSET = {}
                for d in range(2):
                    for i in range(LOOK):
                        nb = lambda nm, w=128, dt=BF16: newbuf('%s_%d%d' % (nm, d, i), w, dt)
                        ISET[d, i] = dict(X=nb('X', 128, F32), N=nb('N'), Bns=nb('Bns'),
                                          Bd=nb('Bd'), Bo=nb('Bo'), LLo=nb('LLo', 256), LB=[nb('LB0', 256), nb('LB1', 256)],
                                          Q=[nb('Q0'), nb('Q1')], nMT=nb('nMT'), X0=nb('X0', 256),
                                          Xit=[nb('Xit0', 256), nb('Xit1', 256)], Kg=nb('Kg'))
                    for i in range(LOOK + 1):
                        nb = lambda nm, w=128, dt=BF16: newbuf('%s_%d%dp' % (nm, d, i), w, dt)
                        PSET[d, i] = dict(EG=nb('EG', 128, F32), qgT=nb('qgT'), QKT=nb('QKT'), Ub=nb('Ub'), nwT=nb('nwT'),
                                          kd=nb('kd'))

                def PQ(d, name, odd=0):
                    m = {'G': (0, 0), 'K': (0, 1), 'Qp': (0, 3), 'L': (1, 0), 'LB': (1, 1), 'X': (1, 1), 'M': (1, 3),
                         'W': (1, 3), 'v': (2, 0), 'o': (2, 1), 'dS': (2, 2)}[name]
                    if m[0] == 1 and odd:
                        return (6 + d, m[1])
                    return (3 * d + m[0], m[1])

                def chunk_of_block(blk):
                    return 0 if blk < 2 else 1 + (blk - 2) // 4

                hl = list(heads)

                def load_qkv_w(hh):
                    for wi in range(3):
                        c0 = OFF[2] + wi * 1024 + hh * 128
                        load_cast(wqkv[:, :, wi * 128:(wi + 1) * 128], win_d[:, :, c0:c0 + 128], 8, 128, ('wqkv', wi))

                for h in heads:
                    if h == hl[0]:
                        load_qkv_w(h)
                    for d_ in range(2):
                        S.dma(selh[:, d_, :], dr['sel32'][:, d_ * 8 + h, :], [], ['selh'])
                    c0 = OFF[3] + h * 128
                    load_cast(wdg[:], win_d[:, :, c0:c0 + 128], 8, 128, 'wdg')
                    def blocks_of_chunk(c):
                        return [0, 1] if c == 0 else list(range(2 + 4 * (c - 1), 6 + 4 * (c - 1)))

                    for wi in (1, 2, 0):
                        cw = convw[:, wi * 8 + h, :]
                        dstT = KT if wi == 1 else QT
                        dk = 'KT' if wi == 1 else 'QT'
                        for j in range(3):
                            TS('dve', Dg[j][:], ident_bf[:], cw[:, j:j + 1], None, ALU.mult, None, ['ident_bf', 'convw'],
                               [('Dg', j)])

                        def geom(c):
                            t0 = 0 if c == 0 else NCTX + (c - 1) * 512
                            n = 256 if c == 0 else 512
                            col0 = 1 if c == 0 else 259 + (c - 1) * 512
                            return t0, n, col0

                        def stA(c):
                            t0, n, col0 = geom(c)
                            bank = 6 + c % 2
                            for kc in range(8):
                                MM(ps[bank][:, 0:n], wqkv[:, kc, wi * 128:(wi + 1) * 128], hT[:, kc, t0:t0 + n],
                                   [('hT', c), ('wqkv', wi)], pk(bank), start=(kc == 0), stop=(kc == 7))
                            CP('dve', X0[:, col0:col0 + n], ps[bank][:, 0:n], pk(bank), [('X0', c)])

                        def stB(c):
                            t0, n, col0 = geom(c)
                            nb_keys = [('X0', cc) for cc in (c - 1, c, c + 1) if 0 <= cc <= 8] + ['X0']
                            bank = c % 2
                            for j in range(3):
                                MM(ps[bank][:, 0:n], Dg[j][:], X0[:, col0 + j - 1:col0 + j - 1 + n], nb_keys + [('Dg', j)],
                                   pk(bank), start=(j == 0), stop=(j == 2))
                            ACT(X2[:, t0:t0 + n], ps[bank][:, 0:n], AF.Silu, pk(bank), [('X2', c)])

                        def stC(c):
                            t0, n, col0 = geom(c)
                            if wi == 2:
                                blks = blocks_of_chunk(c)
                                bank = 4 + c % 2
                                pst = ps[bank][:].bitcast(BF16)
                                for bi_, blk in enumerate(blks):
                                    TR(pst[:, bi_ * 128:(bi_ + 1) * 128], X2[:, blk * 128:(blk + 1) * 128], ident_bf[:],
                                       [('X2', c), 'ident_bf'], pk(bank))
                                CP('dve' if c % 2 == 0 else 'act', Vtok[:, blks[0]:blks[0] + len(blks), :],
                                   pst[:, 0:len(blks) * 128].rearrange("p (b c) -> p b c", c=128), pk(bank),
                                   [('Vtok', blk) for blk in blks])
                            else:
                                bank = 2 + c % 2
                                sq_ = sqb2[c % 2]
                                TT('dve', sq_[:, 0:n], X2[:, t0:t0 + n], X2[:, t0:t0 + n], ALU.mult, [('X2', c)], [('sqb', c % 2)])
                                MM(ps[bank][:, 0:n], cst['ones_bf'][:], sq_[:, 0:n], [('sqb', c % 2), 'ones_bf'], pk(bank))

                        def stD(c):
                            t0, n, col0 = geom(c)
                            if wi == 2:
                                return
                            bank = 2 + c % 2
                            ACT(lnt[0][:, 0:n], ps[bank][:, 0:n], AF.Ln, pk(bank), ['lnt0'], bias=EPS)
                            ACT(lnt[1][:, 0:n], lnt[0][:, 0:n], AF.Exp, ['lnt0'], ['lnt1'], scale=-0.5,
                                bias=(-0.5 * float(np.log(128.0)) if wi == 0 else 0.0))
                            TT('dve', dstT[:, t0:t0 + n], X2[:, t0:t0 + n], lnt[1][:, 0:n], ALU.mult,
                               [('X2', c), 'lnt1'], [(dk, c)])
                            if wi == 1:
                                blks = blocks_of_chunk(c)
                                bank = 4 + c % 2
                                pst = ps[bank][:].bitcast(BF16)
                                for bi_, blk in enumerate(blks):
                                    TR(pst[:, bi_ * 128:(bi_ + 1) * 128], KT[:, blk * 128:(blk + 1) * 128], ident_bf[:],
                                       [('KT', c), 'ident_bf'], pk(bank))
                                CP('dve' if c % 2 == 0 else 'act', Ktok[:, blks[0]:blks[0] + len(blks), :],
                                   pst[:, 0:len(blks) * 128].rearrange("p (b c) -> p b c", c=128), pk(bank),
                                   [('Ktok', blk) for blk in blks])

                        for i in range(9 + 4):
                            if i < 9:
                                stA(i)
                            if 0 <= i - 2 < 9:
                                stB(i - 2)
                            if 0 <= i - 3 < 9:
                                stC(i - 3)
                            if 0 <= i - 4 < 9:
                                stD(i - 4)
                    if 'qkv' in dbg_d and h == list(heads)[0]:
                        S.dma(dbg_d['qkv'][0], QT[:], [('QT', c) for c in range(9)], ['dbg_q'])
                        S.dma(dbg_d['qkv'][1], KT[:], [('KT', c) for c in range(9)], ['dbg_k'])
                        S.dma(dbg_d['vtok'], Vtok[:], [('Vtok', b) for b in range(NB)], ['dbg_v'])

                    if 'selh' in dbg_d:
                        S.dma(dbg_d['selh'], selh[:], ['selh'], ['dbg_selh'])
                    if hl.index(h) + 1 < len(hl):
                        load_qkv_w(hl[hl.index(h) + 1])
                    S.barrier()
                    for d in range(2):
                        MEMSET('pool', Sf32[d][:], 0.0, [('S', d)])
                        MEMSET('pool', Sbf[d][:], 0.0, [('Sb', d)])
                        for c_ in range(2):
                            MEMSET('pool', vnew[d][c_][:], 0.0, [('vn', d, c_)])
                    orderC = [list(range(68)), [3, 2, 1, 0] + list(range(67, 3, -1))]
                    orderB = []
                    for d in range(2):
                        ob = []
                        for c in orderC[d]:
                            if c // 2 not in ob:
                                ob.append(c // 2)
                        orderB.append(ob)
                    owritten = set()

                    def setup_micro(d, bi):
                        blk = orderB[d][bi]
                        I = ISET[d, bi % LOOK]
                        P = PSET[d, bi % (LOOK + 1)]
                        odd = bi % 2
                        col = d * 8 + h
                        kq = [('QT', chunk_of_block(blk)), ('KT', chunk_of_block(blk))]
                        tks = slice(blk * 128, (blk + 1) * 128)
                        bG, qG = PQ(d, 'G')
                        bK, qK = PQ(d, 'K')
                        bQ, qQ = PQ(d, 'Qp')
                        bL, qL = PQ(d, 'L', odd)
                        bLB, qLB = PQ(d, 'LB', odd)
                        bM, qM = PQ(d, 'M', odd)
                        ik = lambda nm: ('i', nm, d, bi % LOOK)
                        pkk = lambda nm: ('p', nm, d, bi % (LOOK + 1))
                        gcc = gc_tm[:, blk, col:col + 1]
                        bcol = beta_tm[:, blk, col:col + 1]
                        mneg = cst['mnegF'] if d == 0 else cst['mnegB']
                        md = cst['mdF'] if d == 0 else cst['mdB']
                        mo = cst['moF'] if d == 0 else cst['moB']
                        pstL = ps[bL][:].bitcast(BF16)[:, qL * 256:qL * 256 + 256]
                        pstW = ps[bM][:].bitcast(BF16)[:, qM * 256:qM * 256 + 128]
                        L0, Lo = I['LLo'][:, 0:128], I['LLo'][:, 128:256]

                        def m0():
                            ACT(P['kd'], Ktok[:, blk, :], AF.Copy, [('Ktok', blk), 'egl_tm'], [pkk('kd')],
                                scale=egl_tm[:, blk, col:col + 1])
                            TS('pool', I['Kg'], Ktok[:, blk, :], egc_tm[:, blk, col:col + 1], None, ALU.mult, None,
                               [('Ktok', blk), 'egc_tm'], [ik('Kg')])

                        def m1():
                            MM(psq(bG, qG), selh[:, d, :], gcT[:, blk, :], ['selh', 'gcT'], pk(bG, qG, 1))
                            MM(psq(bK, qK), KT[:, tks], QT[:, tks], kq, pk(bK, qK, 1))
                            MM(psq(bK, qK + 1), KT[:, tks], KT[:, tks], kq, pk(bK, qK + 1, 1))

                        def m2():
                            STT('dve', I['X'], psq(bG, qG), gcc, mneg[:], ALU.subtract, ALU.add,
                                pk(bG, qG, 1) + ['gc_tm', 'mnegF', 'mnegB'], [ik('X')])
                            ACT(P['EG'], psq(bG, qG), AF.Exp, pk(bG, qG, 1), [pkk('EG')])

                        def m3():
                            if 'EG' in dbg_d and d == 0 and bi == 0 and h == hl[0]:
                                S.dma(dbg_d['EG'], P['EG'], [pkk('EG')], ['dbg_EG'])
                                S.dma(dbg_d['Xd'], I['X'], [ik('X')], ['dbg_Xd'])
                            ACT(I['N'], I['X'], AF.Exp, [ik('X')], [ik('N')])
                            TT('dve', P['qgT'], QT[:, tks], P['EG'], ALU.mult, kq + [pkk('EG')], [pkk('qgT')])

                        def m4():
                            STT('dve', I['Bns'], psq(bK, qK + 1), bcol, I['N'], ALU.mult, ALU.mult,
                                pk(bK, qK + 1, 1) + [ik('N'), 'beta_tm'], [ik('Bns')])
                            TT('dve', P['QKT'], psq(bK, qK), I['N'], ALU.mult, pk(bK, qK, 1) + [ik('N')], [pkk('QKT')])

                        def m5():
                            TT('dve', I['Bd'], I['Bns'], md[:], ALU.mult, [ik('Bns'), 'mdF', 'mdB'], [ik('Bd')])
                            TT('pool', I['Bo'], I['Bns'], mo[:], ALU.mult, [ik('Bns'), 'moF', 'moB'], [ik('Bo')])

                        def m6():
                            TT('dve', I['Q'][0], ident_bf[:], I['Bd'], ALU.subtract, [ik('Bd'), 'ident_bf'], [ik('Q0')])
                            TR(pstL[:, 0:128], I['Bd'], ident_bf[:], [ik('Bd'), 'ident_bf'], pk(bL, qL, 1))
                            TR(pstL[:, 128:256], I['Bo'], ident_bf[:], [ik('Bo'), 'ident_bf'], pk(bL, qL, 1))

                        def m7():
                            CP('act', I['LLo'], pstL, pk(bL, qL, 1), [ik('LLo')])

                        def levA(k):
                            def f():
                                if k == 1:
                                    Lp, Bp, kLp, kBp = L0, I['Bd'], ik('LLo'), ik('Bd')
                                else:
                                    prev = I['LB'][(k - 1) % 2]
                                    Lp, Bp = prev[:, 0:128], prev[:, 128:256]
                                    kLp = kBp = ik('LB%d' % ((k - 1) % 2))
                                MM(psq(bLB, qLB), Bp, Lp, [kLp, kBp], pk(bLB, qLB, 1))
                                if k < 3:
                                    MM(psq(bLB, qLB + 1), Lp, Bp, [kLp, kBp], pk(bLB, qLB + 1, 1))
                            return f

                        def levB(k):
                            def f():
                                cur = I['LB'][k % 2]
                                if k < 3:
                                    CP('act' if k == 1 else 'dve', cur[:, 0:256], psq(bLB, qLB, 2), pk(bLB, qLB, 2),
                                       [ik('LB%d' % (k % 2))])
                                else:
                                    CP('act', cur[:, 0:128], psq(bLB, qLB), pk(bLB, qLB, 1), [ik('LB%d' % (k % 2))])
                            return f

                        def levC(k):
                            def f():
                                cur = I['LB'][k % 2]
                                MM(psq(bQ, qQ), cur[:, 0:128], I['Q'][(k - 1) % 2],
                                   [ik('LB%d' % (k % 2)), ik('Q%d' % ((k - 1) % 2))], pk(bQ, qQ, 1))
                            return f

                        def levD(k):
                            def f():
                                TT('dve', I['Q'][k % 2], psq(bQ, qQ), I['Q'][(k - 1) % 2], ALU.add,
                                   pk(bQ, qQ, 1) + [ik('Q%d' % ((k - 1) % 2))], [ik('Q%d' % (k % 2))])
                            return f

                        def m20():
                            TdT = I['Q'][1]
                            MM(psq(bM, qM), Lo, TdT, [ik('LLo'), ik('Q1')], pk(bM, qM, 1))
                            MM(psq(bLB, qLB), TdT, Vtok[:, blk, :], [ik('Q1'), ('Vtok', blk)], pk(bLB, qLB, 1))
                            MM(psq(bLB, qLB + 1), TdT, I['Kg'], [ik('Q1'), ik('Kg')], pk(bLB, qLB + 1, 1))

                        def m21():
                            ACT(I['nMT'], psq(bM, qM), AF.Copy, pk(bM, qM, 1), [ik('nMT')], scale=-1.0)
                            CP('dve', I['X0'], psq(bLB, qLB, 2), pk(bLB, qLB, 2), [ik('X0')])

                        def itA(n):
                            def f():
                                prev = I['X0'] if n == 1 else I['Xit'][(n - 1) % 2]
                                kprev = ik('X0') if n == 1 else ik('Xit%d' % ((n - 1) % 2))
                                MM(psq(bLB, qLB, 2), I['nMT'], prev[:, 0:256], [ik('nMT'), kprev], pk(bLB, qLB, 2))
                            return f

                        def itB(n):
                            def f():
                                TT('dve', I['Xit'][n % 2], psq(bLB, qLB, 2), I['X0'], ALU.add,
                                   pk(bLB, qLB, 2) + [ik('X0')], [ik('Xit%d' % (n % 2))])
                            return f

                        def m28():
                            X3 = I['Xit'][1]
                            ACT(P['Ub'], X3[:, 0:128], AF.Copy, [ik('Xit1'), 'beta_tm'], [pkk('Ub')], scale=bcol)
                            TR(pstW, X3[:, 128:256], ident_bf[:], [ik('Xit1'), 'ident_bf'], pk(bM, qM, 1))

                        def m29():
                            ACT(P['nwT'], pstW, AF.Copy, pk(bM, qM, 1), [pkk('nwT')], scale=-1.0)
                        return [m0, m1, m2, m3, m4, m5, m6, m7,
                                levA(1), levB(1), levC(1), levD(1), levA(2), levB(2), levC(2), levD(2),
                                levA(3), levB(3), levC(3), levD(3), m20, m21,
                                itA(1), itB(1), itA(2), itB(2), itA(3), itB(3), m28, m29]

                    def step_micro(d, s_):
                        c = orderC[d][s_]
                        blk, ci = c // 2, c % 2
                        bi = orderB[d].index(blk)
                        P = PSET[d, bi % (LOOK + 1)]
                        col = d * 8 + h
                        cs = slice(ci * 64, ci * 64 + 64)
                        bv, qv = PQ(d, 'v')
                        bo, qo = PQ(d, 'o')
                        bs, qs = PQ(d, 'dS')
                        pkk = lambda nm: ('p', nm, d, bi % (LOOK + 1))

                        def u0():
                            MM(psq(bv, qv), P['nwT'], Sbf[d][:], [pkk('nwT'), ('Sb', d)], pk(bv, qv, 1))

                        def u1():
                            STT('dve', vnew[d][ci][cs, :], ps[bv][cs, qv * 128:(qv + 1) * 128],
                                beta_tm[cs, blk, col:col + 1], P['Ub'][cs, :], ALU.mult, ALU.add,
                                pk(bv, qv, 1) + ['beta_tm', pkk('Ub')], [('vn', d, ci)])

                        def u2():
                            MM(psq(bs, qs), P['kd'], vnew[d][ci][:], [pkk('kd'), ('vn', d, ci)], pk(bs, qs, 1))
                            if blk >= 2:
                                MM(psq(bo, qo), P['qgT'], Sbf[d][:], [pkk('qgT'), ('Sb', d)], pk(bo, qo, 1),
                                   start=True, stop=False)
                                MM(psq(bo, qo), P['QKT'], vnew[d][ci][:], [pkk('QKT'), ('vn', d, ci)], pk(bo, qo, 1),
                                   start=False, stop=True)

                        def u3():
                            lastcol = ci * 64 + (63 if d == 0 else 0)
                            STT('dve', Sf32[d][:], Sf32[d][:], P['EG'][:, lastcol:lastcol + 1], psq(bs, qs), ALU.mult,
                                ALU.add, [('S', d), pkk('EG')] + pk(bs, qs, 1), [('S', d)])
                            if blk >= 2:
                                okey = ('O', blk, ci)
                                src = ps[bo][cs, qo * 128:(qo + 1) * 128]
                                if okey not in owritten:
                                    owritten.add(okey)
                                    CP('act', Obuf[cs, blk - 2, :], src, pk(bo, qo, 1), [okey])
                                else:
                                    TT('dve', Obuf[cs, blk - 2, :], src, Obuf[cs, blk - 2, :], ALU.add,
                                       pk(bo, qo, 1) + [okey], [okey])

                        def u4():
                            CP('act', Sbf[d][:], Sf32[d][:], [('S', d)], [('Sb', d)])
                        return [u0, u1, u2, u3, u4]

                    NBK = len(orderB[0])
                    smic = {}
                    for g in range(-10 * LOOK, 5 * 68):
                        if g >= 0:
                            s_, u = g // 5, g % 5
                            for d in range(2):
                                if u == 0:
                                    smic[d] = step_micro(d, s_)
                                smic[d][u]()
                        for bi in range(NBK):
                            g0 = 10 * (bi - LOOK)
                            if g0 <= g < g0 + 30:
                                m = g - g0
                                for d in range(2):
                                    key = ('setup', d, bi)
                                    if key not in smic:
                                        smic[key] = setup_micro(d, bi)
                                    smic[key][m]()
                    S.barrier()
                    if 'O' in dbg_d and h == list(heads)[0]:
                        S.dma(dbg_d['O'], Obuf[:], [('O', b, ci) for b in range(2, 34) for ci in range(2)], ['dbg_O'])
                    if 'Sfin' in dbg_d and h == list(heads)[0]:
                        S.dma(dbg_d['Sfin'][0], Sf32[0][:], [('S', 0)], ['dbg_S0'])
                        S.dma(dbg_d['Sfin'][1], Sf32[1][:], [('S', 1)], ['dbg_S1'])

                    OK_ALL = lambda b: [('O', b + 2, 0), ('O', b + 2, 1)]
                    for b in range(32):
                        ACT(sqb[:, 0:128], Obuf[:, b, :], AF.Square, OK_ALL(b), ['sqb', 'ssq'], accum=ssq[:, b:b + 1])
                    TS('dve', ssq[:], ssq[:], 1.0 / 128, EPS, ALU.mult, ALU.add, ['ssq'], ['ssq'])
                    TT('pool', rstd_o[:], ssq[:], mhalf[:, 0:32], ALU.pow, ['ssq', 'mhalf'], ['rstd_o'])
                    for tc in range(8):
                        bank = 6 + tc % 2
                        t0 = NCTX + tc * 512
                        for kc in range(8):
                            MM(ps[bank][:, :], wdg[:, kc, :], hT[:, kc, t0:t0 + 512], [('hT', 1 + tc), 'wdg'], pk(bank),
                               start=(kc == 0), stop=(kc == 7))
                        ACT(X2[:, tc * 512:(tc + 1) * 512], ps[bank][:, :], AF.Silu, pk(bank), ['X2'])
                    for tc in range(8):
                        bank = 6 + tc % 2
                        pst = ps[bank][:].bitcast(BF16)
                        for bi in range(4):
                            b = tc * 4 + bi
                            STT('dve', Onb[bi][:], Obuf[:, b, :], rstd_o[:, b:b + 1], normw[:], ALU.mult, ALU.mult,
                                OK_ALL(b) + ['rstd_o', 'normw'], [('Onb', bi)])
                            TR(pst[:, bi * 128:(bi + 1) * 128], Onb[bi][:], ident_bf[:], [('Onb', bi), 'ident_bf'], pk(bank))
                        TT('dve', X1[:, tc * 512:(tc + 1) * 512], pst[:, 0:512], X2[:, tc * 512:(tc + 1) * 512], ALU.mult,
                           pk(bank) + ['X2'], ['X1'])
                    S.dma(og_s[h], X1[:, 0:NLAT], ['X1'], [('og_s', h)], semkey='og_s', waitall=True, eng='pool')
                    S.barrier()
            S.barrier()
        if 'fo' in phases:
            with ExitStack() as es3:
                def sb3(name, shape, dt=F32):
                    return es3.enter_context(nc.sbuf_tensor('sb_' + name, list(shape), dt))
                load_consts(('ccsc',), sb3, 'consts_fo')
                c64 = sb3('c64', [64, 128], BF16)
                rr = sb3('rr', [64, 64, 2, 128], BF16)
                S.dma(c64[:], dr['c64'], [], ['c64'])
                S.dma(rr[:], dr['rr'], [], ['rr'])
                wfv = sb3('wfv', [128, 8, 512], BF16)
                wfg = sb3('wfg', [128, 8, 512], BF16)
                wfm = sb3('wfm', [128, 4, 128], BF16)
                load_cast(wfv[:], win_d[:, :, OFF[0]:OFF[0] + 512], 8, 512, 'wfv')
                load_cast(wfg[:], win_d[:, :, OFF[1]:OFF[1] + 512], 8, 512, 'wfg')
                load_cast(wfm[:], wfm_d, 4, 128, 'wfm')
                uA = sb3('uA', [64, 64, 256], BF16)
                Y = sb3('Y', [64, 128, 2, 64], BF16)
                PT = sb3('PT', [128, 2, NLAT], BF16)
                G = sb3('G', [128, 256], BF16)
                SG = [sb3('SG0', [128, 512], BF16)] * 2
                FMc = [sb3('FMc0', [128, 512], BF16)] * 2
                LAT = [('hT', 1 + tc) for tc in range(8)]
                for g in range(4):
                    if g % 2 == 0:
                        for n0 in range(0, 64, 2):
                            bank = (n0 // 2) % 2
                            for a in range(2):
                                n2 = n0 + a
                                for kc in range(8):
                                    MM(ps[bank][0:64, a * 256:(a + 1) * 256], hT[:, kc, NCTX + n2:NCTX + NLAT:64],
                                       wfv[:, kc, g * 128:g * 128 + 256], LAT + ['wfv'], pk(bank),
                                       start=(kc == 0), stop=(kc == 7))
                            CP('act' if bank == 0 else 'dve', uA[:, n0:n0 + 2, :],
                               ps[bank][0:64, :].rearrange("p (a c) -> p a c", c=256), pk(bank), ['uA'])
                    go = (g % 2) * 128
                    for c0 in range(0, 128, 4):
                        bank = 2 + (c0 // 4) % 2
                        for a in range(4):
                            MM(ps[bank][0:64, a * 128:(a + 1) * 128], uA[:, :, go + c0 + a], c64[:, :], ['uA', 'c64'], pk(bank))
                        CP('act' if bank == 2 else 'dve', Y[:, c0:c0 + 4, :, :],
                           ps[bank][0:64, :].rearrange("p (c r k) -> p c r k", c=4, r=2), pk(bank), ['Y'])
                    if g == 0:
                        if 'uA' in dbg_d:
                            S.dma(dbg_d['uA'], uA[:, :, 0:128], ['uA'], ['dbg_uA'])
                    PTv = PT[:].rearrange("p r (k2 k1) -> p r k2 k1", k1=64)
                    for k0 in range(0, 64, 4):
                        bank = 4 + (k0 // 4) % 2
                        for a in range(4):
                            k1 = k0 + a
                            MM(ps[bank][:, a * 128:(a + 1) * 128], Y[:, :, 0, k1], rr[:, k1, 0, :], ['Y', 'rr'], pk(bank),
                               start=True, stop=False)
                            MM(ps[bank][:, a * 128:(a + 1) * 128], Y[:, :, 1, k1], rr[:, k1, 1, :], ['Y', 'rr'], pk(bank),
                               start=False, stop=True)
                        CP('act' if bank == 4 else 'dve',
                           PTv[:, :, :, k0:k0 + 4].rearrange("p r k a -> p a r k"),
                           ps[bank][:, :].rearrange("p (a r k) -> p a r k", a=4, r=2), pk(bank), ['PT'])
                    if g == 0 and 'PT' in dbg_d:
                        S.dma(dbg_d['PT'], PT[:], ['PT'], ['dbg_PT'])
                    MM(ps[6][:, 0:128], cst['ccsc'][:, 0:128], wfm[:, g, :], ['ccsc', 'wfm'], pk(6))
                    MM(ps[6][:, 128:256], cst['ccsc'][:, 128:256], wfm[:, g, :], ['ccsc', 'wfm'], pk(6))
                    CP('dve', G[:], ps[6][:, 0:256], pk(6), ['G'])
                    for tc in range(8):
                        i = tc % 2
                        t0 = NCTX + tc * 512
                        for kc in range(8):
                            MM(ps[7][:, :], wfg[:, kc, g * 128:(g + 1) * 128], hT[:, kc, t0:t0 + 512],
                               [('hT', 1 + tc), 'wfg'], pk(7), start=(kc == 0), stop=(kc == 7))
                        ACT(SG[i][:], ps[7][:, :], AF.Silu, pk(7), [('SG', 0)])
                        MM(ps[6][:, :], G[:, 0:128], PT[:, 0, tc * 512:(tc + 1) * 512], ['G', 'PT'], pk(6),
                           start=True, stop=False)
                        MM(ps[6][:, :], G[:, 128:256], PT[:, 1, tc * 512:(tc + 1) * 512], ['G', 'PT'], pk(6),
                           start=False, stop=True)
                        TT('dve', FMc[i][:], ps[6][:, :], SG[i][:], ALU.mult, pk(6) + [('SG', 0)], [('FMc', 0)])
                        if 'fm' in dbg_d:
                            S.dma(dbg_d['fm'][g, :, tc * 512:(tc + 1) * 512], FMc[i][:], [('FMc', 0)], ['dbg_fm'])
                        S.dma(fm_s[g, :, tc * 512:(tc + 1) * 512], FMc[i][:], [('FMc', 0)], [('fm_s', g, tc)], semkey='fm_s', waitall=True, eng='pool')
            S.barrier()

        if 'fin' in phases:
            wout = sb('wout', [128, 8, D], BF16)
            with ExitStack() as es4:
                def sb4(name, shape, dt=F32):
                    return es4.enter_context(nc.sbuf_tensor('sb_' + name, list(shape), dt))
                wrf = sb4('wrf', [128, 8, D], BF16)
                wrd = sb4('wrd', [128, 8, D], BF16)
                wfo = sb4('wfo', [128, 4, D], BF16)
                wdn = sb4('wdn', [128, 8, D], BF16)
                for cb in range(4):
                    cs_ = slice(cb * 256, (cb + 1) * 256)
                    S.dma(wrf[:, :, cs_], win_d[:, :, OFF[6] + cb * 256:OFF[6] + (cb + 1) * 256], [], [('wrf', cb)], eng='pool')
                    S.dma(wrd[:, :, cs_], win_d[:, :, OFF[7] + cb * 256:OFF[7] + (cb + 1) * 256], [], [('wrd', cb)], eng='pool')
                    S.dma(wfo[:, :, cs_], wfo_d[:, :, cs_], [], [('wfo', cb)], eng='pool')
                    S.dma(wdn[:, :, cs_], wdn_d[:, :, cs_], [], [('wdn', cb)], eng='pool')
                load_cast(wout[:], wout_d, 8, D, 'wout')
                if 'wrf' in dbg_d:
                    S.dma(dbg_d['wrf'], wrf[:], [('wrf', cb) for cb in range(4)], ['dbg_wrf'])
                    S.dma(dbg_d['hT2'], hT[:, :, NCTX:NCTX + 512], ALLHT, ['dbg_hT2'])
                ogc = [sb4('ogc%d' % i, [128, 8, 512], BF16) for i in range(2)]
                fmc = [sb4('fmc%d' % i, [128, 4, 512], BF16) for i in range(2)]
                sgf = [sb4('sgf%d' % i, [128, 512], BF16) for i in range(2)]
                sgd = [sb4('sgd%d' % i, [128, 512], BF16) for i in range(2)]
                m1 = [sb4('m1_%d' % i, [128, 512], BF16) for i in range(2)]
                m2 = [sb4('m2_%d' % i, [128, 512], BF16) for i in range(2)]
                mgo = [sb4('mgo%d' % i, [128, 512], BF16) for i in range(2)]
                for tc in range(8):
                    ci = tc % 2
                    t0 = NCTX + tc * 512
                    if 'dn' in phases:
                        S.dma(ogc[ci][:], og_s[:, :, tc * 512:(tc + 1) * 512].rearrange("h p t -> p h t"),
                              [('og_s', h) for h in range(H)], [('ogc', ci)])
                    else:
                        MEMSET('pool', ogc[ci][:], 0.0, [('ogc', ci)])
                    if 'fo' in phases:
                        S.dma(fmc[ci][:], fm_s[:, :, tc * 512:(tc + 1) * 512].rearrange("g p t -> p g t"),
                              [('fm_s', g, tc) for g in range(4)], [('fmc', ci)])
                    else:
                        MEMSET('pool', fmc[ci][:], 0.0, [('fmc', ci)])
                    for fc in range(8):
                        i = fc % 2
                        bb = 4 * i
                        fs = slice(fc * 128, (fc + 1) * 128)
                        for kc in range(8):
                            MM(ps[bb][:, :], wrf[:, kc, fs], hT[:, kc, t0:t0 + 512], [('hT', 1 + tc), ('wrf', fc // 2)], pk(bb),
                               start=(kc == 0), stop=(kc == 7))
                        for kc in range(8):
                            MM(ps[bb + 1][:, :], wrd[:, kc, fs], hT[:, kc, t0:t0 + 512], [('hT', 1 + tc), ('wrd', fc // 2)], pk(bb + 1),
                               start=(kc == 0), stop=(kc == 7))
                        for g in range(4):
                            MM(ps[bb + 2][:, :], wfo[:, g, fs], fmc[ci][:, g, :], [('fmc', ci), ('wfo', fc // 2)], pk(bb + 2),
                               start=(g == 0), stop=(g == 3))
                        for h in range(H):
                            MM(ps[bb + 3][:, :], wdn[:, h, fs], ogc[ci][:, h, :], [('ogc', ci), ('wdn', fc // 2)], pk(bb + 3),
                               start=(h == 0), stop=(h == 7))
                        ACT(sgf[i][:], ps[bb][:, :], AF.Sigmoid, pk(bb), [('sgf', i)])
                        ACT(sgd[i][:], ps[bb + 1][:, :], AF.Sigmoid, pk(bb + 1), [('sgd', i)])
                        TT('dve', m1[i][:], ps[bb + 2][:, :], sgf[i][:], ALU.mult, pk(bb + 2) + [('sgf', i)], [('m1', i)])
                        TT('dve', m2[i][:], ps[bb + 3][:, :], sgd[i][:], ALU.mult, pk(bb + 3) + [('sgd', i)], [('m2', i)])
                        TT('dve', mgo[i][:], m1[i][:], m2[i][:], ALU.add, [('m1', i), ('m2', i)], [('mgo', i)])
                        S.dma(mg_s[fc, :, tc * 512:(tc + 1) * 512], mgo[i][:], [('mgo', i)], [('mg_s', tc)], semkey='mg_s', waitall=True, eng='pool')
                        if 'taps' in dbg_d and tc == 0:
                            S.dma(dbg_d['taps'][0, fc], sgf[i][:], [('sgf', i)], ['dbg_t0'])
                            S.dma(dbg_d['taps'][1, fc], sgd[i][:], [('sgd', i)], ['dbg_t1'])
                            S.dma(dbg_d['taps'][2, fc], m1[i][:], [('m1', i)], ['dbg_t2'])
                            S.dma(dbg_d['taps'][3, fc], m2[i][:], [('m2', i)], ['dbg_t3'])
                        if 'merged' in dbg_d and tc == 0:
                            S.dma(dbg_d['merged'][fc], mgo[i][:], [('mgo', i)], ['dbg_mg'])
            S.barrier()
            with ExitStack() as es5:
                def sb5(name, shape, dt=F32):
                    return es5.enter_context(nc.sbuf_tensor('sb_' + name, list(shape), dt))
                lng = sb5('lng', [128, D])
                lnb = sb5('lnb', [128, D])
                S.dma(lng[:], lng_d, [], ['lng'])
                S.dma(lnb[:], lnb_d, [], ['lnb'])
                mg = [sb5('mg%d' % i, [128, 8, 512], BF16) for i in range(2)]
                NBF = 4
                xt = [sb5('fxt%d' % i, [128, D]) for i in range(NBF)]
                pet = [sb5('fpet%d' % i, [128, D]) for i in range(NBF)]
                pre = [sb5('pre%d' % i, [128, D]) for i in range(NBF)]
                sq = sb5('fsq', [128, D], BF16)
                st = [sb5('fst%d' % i, [128, 8]) for i in range(8)]

                def fA(ti):
                    tc, tt = ti // 4, ti % 4
                    ci = tc % 2
                    j = ti % NBF
                    xk, pk_, prk, sk = ('fxt', j), ('fpet', j), ('pre', j), ('fst', ti % 8)
                    stt = st[ti % 8]
                    if tt == 0:
                        S.dma(mg[ci][:], mg_s[:, :, tc * 512:(tc + 1) * 512].rearrange("f p t -> p f t"),
                              [('mg_s', tc)], [('mg', ci)])
                    S.dma(xt[j][:], x_d[ti * 128:(ti + 1) * 128, :], [], [xk])
                    S.dma(pet[j][:], dr['pe'][ti * 128:(ti + 1) * 128, :], [], [pk_])
                    TT('dve', xt[j][:], xt[j][:], pet[j][:], ALU.add, [xk, pk_], [xk])
                    for half in range(2):
                        bank = 2 * (ti % 4) + half
                        for kc in range(8):
                            MM(ps[bank][:, :], mg[ci][:, kc, tt * 128:(tt + 1) * 128],
                               wout[:, kc, half * 512:(half + 1) * 512], [('mg', ci), 'wout'], pk(bank),
                               start=(kc == 0), stop=(kc == 7))
                        TT('dve', pre[j][:, half * 512:(half + 1) * 512], ps[bank][:, :],
                           gxb[:, half * 512:(half + 1) * 512], ALU.mult, pk(bank) + ['gxb'], [prk])
                    STT('dve', pre[j][:], xt[j][:], ALPHA, pre[j][:], ALU.mult, ALU.add, [xk, prk], [prk])
                    RED(stt[:, 0:1], pre[j][:], [prk], [sk])
                    ACT(sq[:], pre[j][:], AF.Square, [prk], ['fsq', sk], accum=stt[:, 1:2])

                def fB(ti):
                    sk = ('fst', ti % 8)
                    stt = st[ti % 8]
                    TS('dve', stt[:, 2:3], stt[:, 0:1], 1.0 / D, None, ALU.mult, None, [sk], [sk])
                    TT('dve', stt[:, 3:4], stt[:, 2:3], stt[:, 2:3], ALU.mult, [sk], [sk])
                    STT('dve', stt[:, 4:5], stt[:, 1:2], 1.0 / D, stt[:, 3:4], ALU.mult, ALU.subtract, [sk], [sk])
                    TS('dve', stt[:, 7:8], stt[:, 4:5], EPS, None, ALU.add, None, [sk], [sk])
                    TT('pool', stt[:, 5:6], stt[:, 7:8], mhalf[:, 0:1], ALU.pow, [sk, 'mhalf'], [sk])

                def fC(ti):
                    j = ti % NBF
                    prk, sk = ('pre', j), ('fst', ti % 8)
                    stt = st[ti % 8]
                    STT('dve', stt[:, 6:7], stt[:, 2:3], -1.0, stt[:, 5:6], ALU.mult, ALU.mult, [sk], [sk])
                    ACT(pre[j][:], pre[j][:], AF.Identity, [prk, sk], [prk], scale=stt[:, 5:6], bias=stt[:, 6:7])
                    TT('dve', pre[j][:], pre[j][:], lng[:], ALU.mult, [prk, 'lng'], [prk])
                    TT('dve', pre[j][:], pre[j][:], lnb[:], ALU.add, [prk, 'lnb'], [prk])
                    S.dma(out_d[ti * 128:(ti + 1) * 128, :], pre[j][:], [prk], [('out', ti)], semkey='out', eng='pool')

                for n in range(32 + 2):
                    if n < 32:
                        fA(n)
                    if 0 <= n - 1 < 32:
                        fB(n - 1)
                    if 0 <= n - 2 < 32:
                        fC(n - 2)
        S.emit()
    return nc, consts


_CACHE = {}


def _prep_shared(inputs):
    f = np.float32
    w_in = np.asarray(inputs['w_in'], f)[0]
    sh = {}
    sh['w_mod'] = np.ascontiguousarray(np.asarray(inputs['w_mod'], f)[0].reshape(128, 8, 3 * D))
    sh['b_mod2'] = np.ascontiguousarray(np.tile(np.asarray(inputs['b_mod'], f)[0][None], (2, 1)))
    sh['w_in_r'] = np.ascontiguousarray(w_in.reshape(8, 128, INCOLS).transpose(1, 0, 2))
    sh['w_dn_r'] = np.ascontiguousarray(np.asarray(inputs['w_dn_out'], f)[0].reshape(8, 128, D).transpose(1, 0, 2))
    sh['w_out_r'] = np.ascontiguousarray(np.asarray(inputs['w_out'], f)[0].reshape(8, 128, D).transpose(1, 0, 2))
    sh['w_fo_r'] = np.ascontiguousarray(np.asarray(inputs['w_f_out'], f)[0].reshape(4, 128, D).transpose(1, 0, 2))
    sh['w_fmix_r'] = np.ascontiguousarray(np.asarray(inputs['w_fmix'], f)[0].transpose(1, 0, 2))
    sh['convw_r'] = np.ascontiguousarray(np.asarray(inputs['conv_w'], f)[0].T.reshape(24, 128, 3).transpose(1, 0, 2))
    sh['alog_t'] = np.ascontiguousarray(np.tile(np.asarray(inputs['a_log'], f)[0].reshape(1, 16), (128, 1)))
    sh['dtb_t'] = np.ascontiguousarray(np.tile(np.asarray(inputs['dt_bias'], f)[0].reshape(1, 16), (128, 1)))
    sh['normw_t'] = np.ascontiguousarray(np.tile(np.asarray(inputs['dn_norm_w'], f)[0].reshape(1, 128), (128, 1)))
    sh['lng_t'] = np.ascontiguousarray(np.tile(np.asarray(inputs['ln_g'], f)[0].reshape(1, D), (128, 1)))
    sh['lnb_t'] = np.ascontiguousarray(np.tile(np.asarray(inputs['ln_b'], f)[0].reshape(1, D), (128, 1)))
    return sh


def make_in_maps(inputs, consts, cores):
    f = np.float32
    sh = _prep_shared(inputs)
    x = np.asarray(inputs['x'], f)
    ctx = np.asarray(inputs['ctx'], f)
    c = np.asarray(inputs['c'], f)
    c_ctx = np.asarray(inputs['c_ctx'], f)
    maps = []
    for b in cores:
        m = dict(consts)
        m.update(sh)
        m['x'] = np.ascontiguousarray(x[b])
        m['ctx'] = np.ascontiguousarray(ctx[b])
        m['cc'] = np.ascontiguousarray(np.stack([c[b].reshape(128, 8), c_ctx.reshape(128, 8)], -1).reshape(128, 16))
        maps.append(m)
    return maps


def kernel(**inputs):
    if 'nc' not in _CACHE:
        _CACHE['nc'] = build()
    nc, consts = _CACHE['nc']
    maps = make_in_maps(inputs, consts, range(8))
    res = run_bass_kernel_spmd(nc, maps, core_ids=list(range(8)))
    out = np.stack([np.asarray(r['out'], np.float32) for r in res.results], 0)
    return out
```

```python
from contextlib import ExitStack
import numpy as np
import ml_dtypes
import concourse.bass as bass
import concourse.mybir as mybir
from concourse.bass_utils import run_bass_kernel_spmd

F32 = mybir.dt.float32
BF16 = mybir.dt.bfloat16
AF = mybir.ActivationFunctionType
ALU = mybir.AluOpType
AX = mybir.AxisListType

D = 1024
NLAT = 4096
NCTX = 256
NTOK = NCTX + NLAT
H = 8
IN_SPLITS = (512, 512, 3072, 1024, 16, 16, 1024, 1024)
OFF = [0]
for _s in IN_SPLITS:
    OFF.append(OFF[-1] + _s)
INCOLS = OFF[-1]
EPS = 1e-6
ALPHA = 2.0 ** 0.25
NEG = -30000.0
import os
NOEXCL = bool(os.environ.get('NOEXCL'))


class Sched:
    def __init__(self, nc):
        self.nc = nc
        self.ops = []
        self.barriers = []

    def barrier(self):
        self.barriers.append(len(self.ops))

    def add(self, eng, fn, reads=(), writes=(), dma=False, semkey=None, waitall=False):
        self.ops.append(dict(eng=eng, fn=fn, reads=tuple(reads), writes=tuple(writes), dma=dma,
                             semkey=semkey, waitall=waitall))

    def pe(self, fn, r, w):
        self.add('pe', fn, r, w)

    def act(self, fn, r, w):
        self.add('act', fn, r, w)

    def dve(self, fn, r, w):
        self.add('dve', fn, r, w)

    def pool(self, fn, r, w):
        self.add('pool', fn, r, w)

    def dma(self, out, in_, r, w, eng='sp', semkey=None, waitall=False):
        self.add(eng, lambda e: e.dma_start(out=out, in_=in_), r, w, dma=True, semkey=semkey, waitall=waitall)

    def emit(self):
        nc = self.nc
        ops = self.ops
        last_writer = {}
        readers = {}
        last_on_eng = {}
        last_dma = {}
        pending = {}
        bank_last = {}
        bars = set(self.barriers)
        for i, op in enumerate(ops):
            if i in bars:
                bd = set(last_on_eng.values()) | set(last_dma.values())
                for en in ('pe', 'act', 'dve', 'pool', 'sp'):
                    pending[en] = set(bd) | pending.get(en, set())
            deps = set()
            if pending.get(op['eng']):
                deps |= pending.pop(op['eng'])
            for k in op['reads']:
                if k in last_writer:
                    deps.add(last_writer[k])
            for k in op['writes']:
                if k in last_writer:
                    deps.add(last_writer[k])
                deps.update(readers.get(k, ()))
            deps.discard(i)
            for k in op['reads'] + op['writes']:
                if isinstance(k, tuple) and k and k[0] == 'ps':
                    b = k[1]
                    la = bank_last.get(b)
                    if la is not None and ops[la]['eng'] != op['eng'] and not NOEXCL:
                        deps.add(la)
                    bank_last[b] = i
            if op['eng'] == 'pe':
                deps = {d for d in deps if not (ops[d]['eng'] == 'pe' and not ops[d]['dma'])}
            op['deps'] = deps
            if op['dma']:
                last_dma[op['writes'][0]] = i
            else:
                last_on_eng[op['eng']] = i
            for k in op['reads']:
                readers.setdefault(k, []).append(i)
            for k in op['writes']:
                last_writer[k] = i
                readers[k] = []
        needed = set()
        for op in ops:
            needed.update(op['deps'])
        with ExitStack() as es:
            engsem = {}
            for en in ('pe', 'act', 'dve', 'pool'):
                engsem[en] = es.enter_context(nc.semaphore('s_' + en))
            dmasem = {}
            dmacnt = {}
            cnt = {en: 0 for en in engsem}
            for i, op in enumerate(ops):
                if op['dma']:
                    key = op['semkey'] if op['semkey'] is not None else op['writes'][0]
                    if key not in dmasem:
                        dmasem[key] = es.enter_context(nc.semaphore('d%d' % len(dmasem)))
                        dmacnt[key] = 0
                    dmacnt[key] += 16
                    op['sig'] = (dmasem[key], dmacnt[key], 16)
                    op['sigkey'] = key
                elif i in needed:
                    cnt[op['eng']] += 1
                    op['sig'] = (engsem[op['eng']], cnt[op['eng']], 1)
                else:
                    op['sig'] = None
            grp_last = {}
            for i, op in enumerate(ops):
                if op['dma']:
                    grp_last[op['sigkey']] = i
            if os.environ.get('KDEBUG'):
                print('nsem_dma', len(dmasem), 'nops', len(ops), 'cnt', cnt, 'sems', [str(v) for v in list(dmasem.values())[:3]], [str(v) for v in engsem.values()])
            block = es.enter_context(nc.Block())
            by_eng = {}
            for i, op in enumerate(ops):
                by_eng.setdefault(op['eng'], []).append(i)

            def run(e, idxs, final=False):
                waited = {}
                for i in idxs:
                    op = ops[i]
                    need = {}
                    for d in op['deps']:
                        s = ops[d]['sig']
                        if ops[d]['dma'] and ops[d]['waitall'] and i > grp_last[ops[d]['sigkey']]:
                            s = (s[0], dmacnt[ops[d]['sigkey']], 16)
                        sid = id(s[0])
                        if s[1] > need.get(sid, (None, 0))[1]:
                            need[sid] = (s[0], s[1])
                    for sid, (sem, val) in need.items():
                        if waited.get(sid, 0) < val:
                            e.wait_ge(sem, val)
                            waited[sid] = val
                    ins = op['fn'](e)
                    if op['sig'] is not None:
                        ins.then_inc(op['sig'][0], op['sig'][2])
                if final:
                    for key, sem in dmasem.items():
                        if waited.get(id(sem), 0) < dmacnt[key]:
                            e.wait_ge(sem, dmacnt[key])

            @block.sync
            def _(e):
                run(e, by_eng.get('sp', []), final=True)

            @block.tensor
            def _(e):
                run(e, by_eng.get('pe', []))

            @block.scalar
            def _(e):
                run(e, by_eng.get('act', []))

            @block.vector
            def _(e):
                run(e, by_eng.get('dve', []))

            @block.gpsimd
            def _(e):
                run(e, by_eng.get('pool', []))


def _bf(a):
    return np.ascontiguousarray(a).astype(ml_dtypes.bfloat16)


def host_consts():
    c = {}
    c['ident_bf'] = _bf(np.eye(128))
    c['ident_f'] = np.eye(128, dtype=np.float32)
    c['ones_bf'] = _bf(np.ones((128, 128)))
    selx = np.zeros((2, 128), np.float32)
    selx[0] = 1.0
    c['selx'] = selx
    q = D // 4
    om = 1.0 / (10000.0 ** (np.arange(q, dtype=np.float32) / q))
    pr = np.arange(64, dtype=np.float32)[:, None] * om
    er = np.concatenate([np.sin(pr), np.cos(pr)], -1)
    pe = np.concatenate([np.broadcast_to(er[:, None, :], (64, 64, 512)),
                         np.broadcast_to(er[None, :, :], (64, 64, 512))], -1)
    c['pe'] = np.ascontiguousarray(pe.reshape(4096, D)).astype(np.float32)
    idx = np.arange(128)
    same = (idx[:, None] // 64) == (idx[None, :] // 64)
    c['Uf'] = (same & (idx[:, None] <= idx[None, :])).astype(np.float32)
    c['Ub'] = (same & (idx[:, None] >= idx[None, :])).astype(np.float32)
    c['SCm'] = same.astype(np.float32)
    vf = same & (idx[None, :] >= idx[:, None])
    vb = same & (idx[None, :] <= idx[:, None])
    c['mnegF'] = np.where(vf, 0.0, NEG).astype(np.float32)
    c['mnegB'] = np.where(vb, 0.0, NEG).astype(np.float32)
    sub = (idx[:, None] // 16) == (idx[None, :] // 16)
    c['mdF'] = _bf(sub & (idx[None, :] > idx[:, None]))
    c['mdB'] = _bf(sub & (idx[None, :] < idx[:, None]))
    c['moF'] = _bf(same & ~sub & (idx[None, :] > idx[:, None]))
    c['moB'] = _bf(same & ~sub & (idx[None, :] < idx[:, None]))
    sel = np.zeros((128, 16, 128), np.float32)
    for c_ in range(16):
        sel[c_, c_, :] = 1.0
        sel[16 + c_, c_, :] = 1.0
    c['sel32'] = _bf(sel)
    a64 = 2 * np.pi * np.outer(np.arange(64), np.arange(64)) / 64
    c['c64'] = _bf(np.concatenate([np.cos(a64), -np.sin(a64)], 1))
    kk = np.arange(4096)
    th = 2 * np.pi * np.outer(np.arange(64), kk) / 4096
    Mc = (np.cos(th) / 64.0).reshape(64, 64, 64)
    Ms = (np.sin(th) / 64.0).reshape(64, 64, 64)
    Mc = Mc.transpose(0, 2, 1)
    Ms = Ms.transpose(0, 2, 1)
    rr = np.zeros((64, 64, 2, 2, 64))
    rr[:, :, 0, 0] = Mc
    rr[:, :, 0, 1] = -Ms
    rr[:, :, 1, 0] = Ms
    rr[:, :, 1, 1] = Mc
    c['rr'] = _bf(rr.reshape(64, 64, 2, 128))
    ac = 2 * np.pi * np.outer(np.arange(128), np.arange(128)) / 128
    c['ccsc'] = _bf(np.concatenate([np.cos(ac), np.sin(ac)], 1) / np.sqrt(128.0))
    return c


def build(dbg=(), phases=('dn', 'fo', 'fin'), heads=range(H)):
    nc = bass.Bass("TRN2", target_bir_lowering=False)
    consts = host_consts()
    dr = {}

    def din(name, shape, dt=F32):
        dr[name] = nc.dram_tensor(name, list(shape), dt, kind="ExternalInput").ap()
        return dr[name]

    x_d = din('x', [NLAT, D])
    ctx_d = din('ctx', [NCTX, D])
    cc_d = din('cc', [128, 16])
    wmod_d = din('w_mod', [128, 8, 3 * D])
    bmod_d = din('b_mod2', [2, 3 * D])
    win_d = din('w_in_r', [128, 8, INCOLS])
    wdn_d = din('w_dn_r', [128, 8, D])
    wout_d = din('w_out_r', [128, 8, D])
    wfo_d = din('w_fo_r', [128, 4, D])
    wfm_d = din('w_fmix_r', [128, 4, 128])
    convw_d = din('convw_r', [128, 24, 3])
    alog_d = din('alog_t', [128, 16])
    dtb_d = din('dtb_t', [128, 16])
    normw_d = din('normw_t', [128, 128])
    lng_d = din('lng_t', [128, D])
    lnb_d = din('lnb_t', [128, D])
    for k, v in consts.items():
        din(k, v.shape, BF16 if v.dtype == ml_dtypes.bfloat16 else F32)
    out_d = nc.dram_tensor('out', [NLAT, D], F32, kind="ExternalOutput").ap()
    og_s = nc.dram_tensor('og_s', [H, 128, NLAT], BF16).ap()
    fm_s = nc.dram_tensor('fm_s', [4, 128, NLAT], BF16).ap()
    mg_s = nc.dram_tensor('mg_s', [8, 128, NLAT], BF16).ap()
    dbg_d = {}
    for name, shape, dt in dbg:
        dbg_d[name] = nc.dram_tensor('dbg_' + name, list(shape), dt, kind="ExternalOutput").ap()

    S = Sched(nc)

    def MM(out, lhsT, rhs, r, w, start=True, stop=True):
        S.pe(lambda e: e.matmul(out, lhsT=lhsT, rhs=rhs, start=start, stop=stop), r, w)

    def TR(out, in_, ident, r, w):
        S.pe(lambda e: e.transpose(out=out, in_=in_, identity=ident), r, w)

    def TT(eng, out, in0, in1, op, r, w):
        S.add(eng, lambda e: e.tensor_tensor(out=out, in0=in0, in1=in1, op=op), r, w)

    def TS(eng, out, in0, s1, s2, op0, op1, r, w):
        if s2 is None:
            S.add(eng, lambda e: e.tensor_scalar(out=out, in0=in0, scalar1=s1, scalar2=None, op0=op0), r, w)
        else:
            S.add(eng, lambda e: e.tensor_scalar(out=out, in0=in0, scalar1=s1, scalar2=s2, op0=op0, op1=op1), r, w)

    def STT(eng, out, in0, scalar, in1, op0, op1, r, w):
        S.add(eng, lambda e: e.scalar_tensor_tensor(out=out, in0=in0, scalar=scalar, in1=in1, op0=op0, op1=op1),
              r, w)

    def ACT(out, in_, func, r, w, bias=None, scale=None, accum=None):
        kw = {}
        if bias is not None:
            kw['bias'] = bias
        if scale is not None:
            kw['scale'] = scale
        if accum is not None:
            kw['accum_out'] = accum
        S.act(lambda e: e.activation(out=out, in_=in_, func=func, **kw), r, w)

    def CP(eng, out, in_, r, w):
        if eng == 'act':
            S.act(lambda e: e.activation(out=out, in_=in_, func=AF.Copy), r, w)
        else:
            S.add(eng, lambda e: e.tensor_copy(out=out, in_=in_), r, w)

    def RED(out, in_, r, w):
        S.dve(lambda e: e.tensor_reduce(out=out, in_=in_, axis=AX.X, op=ALU.add), r, w)

    def MEMSET(eng, ap, val, w):
        S.add(eng, lambda e: e.memset(ap, val), [], w)

    with ExitStack() as es:
        def sb(name, shape, dt=F32):
            return es.enter_context(nc.sbuf_tensor('sb_' + name, list(shape), dt))

        ps = [es.enter_context(nc.psum_tensor('ps%d' % i, [128, 512], F32)) for i in range(8)]

        def pk(b, q0=0, nq=4):
            return [('ps', b, q) for q in range(q0, q0 + nq)]

        def psq(b, q, nq=1):
            return ps[b][:, q * 128:(q + nq) * 128]

        cst = {}

        def load_consts(names, alloc, grp):
            for name in names:
                v = consts[name]
                cst[name] = alloc(name, v.shape, BF16 if v.dtype == ml_dtypes.bfloat16 else F32)
                S.dma(cst[name][:], dr[name], [], [name], semkey=grp, waitall=True)

        load_consts(('ident_bf', 'ident_f', 'ones_bf', 'selx'), sb, 'consts')
        ident_bf, ident_f, selx = cst['ident_bf'], cst['ident_f'], cst['selx']
        hT = sb('hT', [128, 8, NTOK], BF16)
        mhalf = sb('mhalf', [128, 32])
        MEMSET('pool', mhalf[:], -0.5, ['mhalf'])
        modT = sb('modT', [128, 48])
        gxb = sb('gxb', [128, D])

        def load_cast(dst, src, a, b, wkey):
            bc = max(1, 2048 // a)
            for b0 in range(0, b, bc):
                bn = min(bc, b - b0)
                S.dma(dst[:, :, b0:b0 + bn], src[:, :, b0:b0 + bn], [], [wkey], eng='pool')

        def hT_keys_for_tokens(t0, t1):
            ks = set()
            for t in range(t0, t1, 128):
                ks.add(('hT', 0 if t < NCTX else 1 + (t - NCTX) // 512))
            return sorted(ks)

        with ExitStack() as es0:
            def sb0(name, shape, dt=F32):
                return es0.enter_context(nc.sbuf_tensor('sb_' + name, list(shape), dt))
            cc = sb0('cc', [128, 16])
            sc = sb0('sc', [128, 16])
            wm = [sb0('wm%d' % i, [128, 3 * D]) for i in range(2)]
            bm = sb0('bm', [2, 3 * D])
            modrow = sb0('modrow', [2, 3 * D])
            S.dma(cc[:], cc_d, [], ['cc'])
            S.dma(bm[:], bmod_d, [], ['bm'])
            ACT(sc[:], cc[:], AF.Silu, ['cc'], ['sc'])
            sc3 = sc[:].rearrange("p (k v) -> p k v", v=2)
            for k in range(8):
                S.dma(wm[k % 2][:], wmod_d[:, k, :], [], ['wm%d' % (k % 2)])
                for j in range(6):
                    MM(ps[j][0:2, :], sc3[:, k, :], wm[k % 2][:, j * 512:(j + 1) * 512],
                       ['sc', 'wm%d' % (k % 2)], pk(j), start=(k == 0), stop=(k == 7))
            for j in range(6):
                TT('dve', modrow[:, j * 512:(j + 1) * 512], ps[j][0:2, :], bm[:, j * 512:(j + 1) * 512], ALU.add,
                   pk(j) + ['bm'], ['modrow'])
            for j in range(24):
                MM(ps[6][:, 2 * j:2 * j + 2], modrow[0:2, j * 128:(j + 1) * 128], ident_f[0:2, 0:2],
                   ['modrow', 'ident_f'], pk(6))
            CP('dve', modT[:], ps[6][:, 0:48], pk(6), ['modT'])
            TS('dve', modT[:, 16:32], modT[:, 16:32], 1.0, None, ALU.add, None, ['modT'], ['modT'])
            for jj in range(2):
                MM(ps[7][:, :], selx[0:2, :], modrow[0:2, 2048 + jj * 512:2048 + (jj + 1) * 512],
                   ['modrow', 'selx'], pk(7))
                CP('act', gxb[:, jj * 512:(jj + 1) * 512], ps[7][:, :], pk(7), ['gxb'])

            NBF = 4
            xt = [sb0('xt%d' % i, [128, D]) for i in range(NBF)]
            pet = [sb0('pet%d' % i, [128, D]) for i in range(NBF)]
            sq = sb0('sq', [128, D], BF16)
            xn = [sb0('xn%d' % i, [128, 4, D], BF16) for i in range(2)]
            st = [sb0('st%d' % i, [128, 8]) for i in range(8)]
            groups = [(1, [(ctx_d, 0), (ctx_d, 1)], 0)]
            for g in range(8):
                groups.append((0, [(x_d, 4 * g + i) for i in range(4)], NCTX + g * 512))
            flat = []
            for gi, (v, tiles, tok0) in enumerate(groups):
                for i, (src, ti) in enumerate(tiles):
                    flat.append((gi, i, v, src, ti, len(tiles), tok0))

            def lnA(n):
                gi, i, v, src, ti, nt, tok0 = flat[n]
                j = n % NBF
                stt, sk, xk = st[n % 8], 'st%d' % (n % 8), 'xt%d' % j
                S.dma(xt[j][:], src[ti * 128:(ti + 1) * 128, :], [], [xk])
                if v == 0:
                    S.dma(pet[j][:], dr['pe'][ti * 128:(ti + 1) * 128, :], [], ['pet%d' % j])
                    TT('dve', xt[j][:], xt[j][:], pet[j][:], ALU.add, [xk, 'pet%d' % j], [xk])
                RED(stt[:, 0:1], xt[j][:], [xk], [sk])
                ACT(sq[:], xt[j][:], AF.Square, [xk], ['sq', sk], accum=stt[:, 1:2])

            def lnB(n):
                stt, sk = st[n % 8], 'st%d' % (n % 8)
                TS('dve', stt[:, 2:3], stt[:, 0:1], 1.0 / D, None, ALU.mult, None, [sk], [sk])
                TT('dve', stt[:, 3:4], stt[:, 2:3], stt[:, 2:3], ALU.mult, [sk], [sk])
                STT('dve', stt[:, 4:5], stt[:, 1:2], 1.0 / D, stt[:, 3:4], ALU.mult, ALU.subtract, [sk], [sk])
                TS('dve', stt[:, 7:8], stt[:, 4:5], EPS, None, ALU.add, None, [sk], [sk])
                TT('pool', stt[:, 5:6], stt[:, 7:8], mhalf[:, 0:1], ALU.pow, [sk, 'mhalf'], [sk])

            def lnC(n):
                gi, i, v, src, ti, nt, tok0 = flat[n]
                j = n % NBF
                gp = gi % 2
                stt, sk, xk = st[n % 8], 'st%d' % (n % 8), 'xt%d' % j
                STT('dve', stt[:, 6:7], stt[:, 2:3], -1.0, stt[:, 5:6], ALU.mult, ALU.mult, [sk], [sk])
                ACT(xn[gp][:, i, :], xt[j][:], AF.Identity, [xk, sk], [('xn', gp, i)], scale=stt[:, 5:6], bias=stt[:, 6:7])
                if i == nt - 1:
                    for kc in range(8):
                        pb = 6 + (kc % 2)
                        pst = ps[pb][:].bitcast(BF16)
                        for ii in range(nt):
                            TR(pst[:, ii * 128:(ii + 1) * 128], xn[gp][:, ii, kc * 128:(kc + 1) * 128], ident_bf[:],
                               [('xn', gp, ii), 'ident_bf'], pk(pb))
                        dst = hT[:, kc, tok0:tok0 + nt * 128]
                        sc_ap = modT[:, 2 * (8 + kc) + v:2 * (8 + kc) + v + 1]
                        sh_ap = modT[:, 2 * kc + v:2 * kc + v + 1]
                        if kc % 2 == 0:
                            ACT(dst, pst[:, 0:nt * 128], AF.Identity, pk(pb) + ['modT'], [('hT', gi)], scale=sc_ap, bias=sh_ap)
                        else:
                            TS('dve', dst, pst[:, 0:nt * 128], sc_ap, sh_ap, ALU.mult, ALU.add, pk(pb) + ['modT'], [('hT', gi)])

            NT0 = len(flat)
            for n in range(NT0 + 2):
                if n < NT0:
                    lnA(n)
                if 0 <= n - 1 < NT0:
                    lnB(n - 1)
                if 0 <= n - 2 < NT0:
                    lnC(n - 2)
        S.barrier()
        ALLHT = [('hT', g) for g in range(9)]
        if 'hT' in dbg_d:
            for kc in range(8):
                S.dma(dbg_d['hT'][kc], hT[:, kc, :], ALLHT, ['dbg_hT'])

        if 'dn' in phases:
            with ExitStack() as es2:
                def sb2(name, shape, dt=F32):
                    return es2.enter_context(nc.sbuf_tensor('sb_' + name, list(shape), dt))
                NB = NTOK // 128
                load_consts(('Uf', 'Ub', 'SCm', 'mnegF', 'mnegB', 'mdF', 'mdB', 'moF', 'moB'), sb2, 'consts_dn')
                wbd = sb2('wbd', [128, 8, 32], BF16)
                load_cast(wbd[:], win_d[:, :, OFF[4]:OFF[4] + 32], 8, 32, 'wbd')
                beta_tm = sb2('beta_tm', [128, NB, 16])
                gc_tm = sb2('gc_tm', [128, NB, 16])
                egc_tm = sb2('egc_tm', [128, NB, 16])
                egl_tm = sb2('egl_tm', [128, NB, 16])
                gcT = sb2('gcT', [128, NB, 128], BF16)
                selh = sb2('selh', [128, 2, 128], BF16)
                MEMSET('pool', gcT[:], 0.0, ['gcT'])
                es2p = ExitStack()
                def sb2p(name, shape, dt=F32):
                    return es2p.enter_context(nc.sbuf_tensor('sb_' + name, list(shape), dt))
                raw_tm = sb2p('raw_tm', [128, NB, 32])
                gcb_tm = sb2p('gcb_tm', [128, NB, 16], BF16)
                gch_tm = sb2p('gch_tm', [128, NB, 16])
                gcl_tm = sb2p('gcl_tm', [128, NB, 16])
                gcHL_tm = sb2p('gcHL_tm', [128, NB, 32], BF16)
                g_tm = sb2p('g_tm', [128, NB, 16])
                tot_tm = sb2p('tot_tm', [128, NB, 16])
                tmp_tm = sb2p('tmp_tm', [128, NB, 16])
                alog = sb2p('alog', [128, 16])
                dtb = sb2p('dtb', [128, 16])
                nega = sb2p('nega', [128, 16])
                S.dma(alog[:], alog_d, [], ['alog'])
                S.dma(dtb[:], dtb_d, [], ['dtb'])
                ACT(nega[:], alog[:], AF.Exp, ['alog'], ['nega'])
                TS('dve', nega[:], nega[:], -1.0, None, ALU.mult, None, ['nega'], ['nega'])
                for b0 in range(0, NB, 16):
                    nb_ = min(16, NB - b0)
                    bank = 6 + (b0 // 16) % 2
                    for bi in range(nb_):
                        blk = b0 + bi
                        for kc in range(8):
                            MM(ps[bank][:, bi * 32:(bi + 1) * 32], hT[:, kc, blk * 128:(blk + 1) * 128], wbd[:, kc, :],
                               hT_keys_for_tokens(blk * 128, blk * 128 + 128) + ['wbd'], pk(bank),
                               start=(kc == 0), stop=(kc == 7))
                    CP('act', raw_tm[:, b0:b0 + nb_, :],
                       ps[bank][:, 0:nb_ * 32].rearrange("p (b c) -> p b c", c=32), pk(bank), ['raw_tm'])
                ACT(beta_tm[:], raw_tm[:, :, 0:16], AF.Sigmoid, ['raw_tm'], ['beta_tm'])
                TT('dve', tmp_tm[:], raw_tm[:, :, 16:32], dtb[:].unsqueeze(1).to_broadcast([128, NB, 16]), ALU.add,
                   ['raw_tm', 'dtb'], ['tmp_tm'])
                ACT(tmp_tm[:], tmp_tm[:], AF.Exp, ['tmp_tm'], ['tmp_tm'])
                ACT(tmp_tm[:], tmp_tm[:], AF.Ln, ['tmp_tm'], ['tmp_tm'], bias=1.0)
                TT('dve', g_tm[:], tmp_tm[:], nega[:].unsqueeze(1).to_broadcast([128, NB, 16]), ALU.mult,
                   ['tmp_tm', 'nega'], ['g_tm'])
                for half in range(2):
                    b0 = half * 17
                    for bi in range(17):
                        blk = b0 + bi
                        MM(ps[6][:, bi * 16:bi * 16 + 8], cst['Uf'][:], g_tm[:, blk, 0:8], ['g_tm', 'Uf'], pk(6))
                        MM(ps[6][:, bi * 16 + 8:bi * 16 + 16], cst['Ub'][:], g_tm[:, blk, 8:16], ['g_tm', 'Ub'], pk(6))
                        MM(ps[7][:, bi * 16:bi * 16 + 16], cst['SCm'][:], g_tm[:, blk, :], ['g_tm', 'SCm'], pk(7))
                    CP('dve', gc_tm[:, b0:b0 + 17, :], ps[6][:, 0:272].rearrange("p (b c) -> p b c", c=16), pk(6),
                       ['gc_tm'])
                    CP('dve', tot_tm[:, b0:b0 + 17, :], ps[7][:, 0:272].rearrange("p (b c) -> p b c", c=16), pk(7),
                       ['tot_tm'])
                ACT(egc_tm[:], gc_tm[:], AF.Exp, ['gc_tm'], ['egc_tm'])
                CP('dve', gcb_tm[:], gc_tm[:], ['gc_tm'], ['gcb_tm'])
                CP('dve', gch_tm[:], gcb_tm[:], ['gcb_tm'], ['gch_tm'])
                TT('dve', gcl_tm[:], gc_tm[:], gch_tm[:], ALU.subtract, ['gc_tm', 'gch_tm'], ['gcl_tm'])
                CP('dve', gcHL_tm[:, :, 0:16], gcb_tm[:], ['gcb_tm'], ['gcHL_tm'])
                CP('dve', gcHL_tm[:, :, 16:32], gcl_tm[:], ['gcl_tm'], ['gcHL_tm'])
                for b0 in range(0, NB, 8):
                    nb_ = min(8, NB - b0)
                    pst = ps[6][:].bitcast(BF16)
                    for bi in range(nb_):
                        TR(pst[0:32, bi * 128:(bi + 1) * 128], gcHL_tm[:, b0 + bi, :], ident_bf[:], ['gcHL_tm', 'ident_bf'], pk(6))
                    CP('dve', gcT[0:32, b0:b0 + nb_, :], pst[0:32, 0:nb_ * 128].rearrange("p (b t) -> p b t", t=128),
                       pk(6), ['gcT'])
                TT('dve', tmp_tm[:], tot_tm[:], gc_tm[:], ALU.subtract, ['tot_tm', 'gc_tm'], ['tmp_tm'])
                ACT(egl_tm[:], tmp_tm[:], AF.Exp, ['tmp_tm'], ['egl_tm'])
                if 'gcT' in dbg_d:
                    S.dma(dbg_d['gcT'], gcT[:], ['gcT'], ['dbg_gcT'])
                if 'tm' in dbg_d:
                    S.dma(dbg_d['tm'][0], beta_tm[:], ['beta_tm'], ['dbg_tm0'])
                    S.dma(dbg_d['tm'][1], gc_tm[:], ['gc_tm'], ['dbg_tm1'])
                    S.dma(dbg_d['tm'][2], egl_tm[:], ['egl_tm'], ['dbg_tm2'])
                es2p.close()
                S.barrier()

                X0 = sb2('X0', [128, 4360], BF16)
                X1 = sb2('X1', [128, 4360], BF16)
                X2 = sb2('X2', [128, 4360], BF16)
                QT = sb2('QT', [128, NTOK], BF16)
                KT = sb2('KT', [128, NTOK], BF16)
                Ktok = sb2('Ktok', [128, NB, 128], BF16)
                Vtok = sb2('Vtok', [128, NB, 128], BF16)
                Obuf = sb2('Obuf', [128, 32, 128], BF16)
                wqkv = sb2('wqkv', [128, 8, 384], BF16)
                wdg = sb2('wdg', [128, 8, 128], BF16)
                convw = sb2('convw', [128, 24, 3])
                normw = sb2('normw', [128, 128])
                sqb = sb2('sqb', [128, 512], BF16)
                sqb2 = [sqb, sb2('sqb1', [128, 512], BF16)]
                Dg = [sb2('Dg%d' % j, [128, 128], BF16) for j in range(3)]
                lnt = [sb2('lnt0', [128, 512]), sb2('lnt1', [128, 512], BF16)]
                ssq = sb2('ssq', [128, 32])
                rstd_o = sb2('rstd_o', [128, 32])
                Onb = [sb2('Onb%d' % i, [128, 128], BF16) for i in range(4)]
                S.dma(convw[:], convw_d, [], ['convw'])
                S.dma(normw[:], normw_d, [], ['normw'])
                MEMSET('pool', X0[:], 0.0, ['X0'])
                Sf32 = [sb2('Sf32_%d' % d, [128, 128]) for d in range(2)]
                Sbf = [sb2('Sbf_%d' % d, [128, 128], BF16) for d in range(2)]
                vnew = [[sb2('vnew_%d%d' % (d, c_), [128, 128], BF16) for c_ in range(2)] for d in range(2)]
                LOOK = 3
                carve = [[X1, 0, 4360], [X2, 0, 4360], [X0, 260, 4352]]

                def newbuf(nm, w, dt):
                    ne = w * (2 if dt == F32 else 1)
                    for reg in carve:
                        a = (reg[1] + 1) // 2 * 2
                        if a + ne <= reg[2]:
                            reg[1] = a + ne
                            v = reg[0][:, a:a + ne]
                            return v.bitcast(F32) if dt == F32 else v
                    return sb2(nm, [128, w], dt)[:]

                INTERNAL = ('Gdh', 'Gdl', 'X', 'N', 'Bns', 'Bd', 'Bo', 'LLo', 'LB0', 'LB1', 'Q0', 'Q1', 'nMT',
                            'X0', 'Xit0', 'Xit1', 'Kg')
                ISET = {}
                PSET = {}
                for d in range(2):
                    for i in range(LOOK):
                        nb = lambda nm, w=128, dt=BF16: newbuf('%s_%d%d' % (nm, d, i), w, dt)
                        ISET[d, i] = dict(X=nb('X', 128, F32), N=nb('N'), Bns=nb('Bns'),
                                          Bd=nb('Bd'), Bo=nb('Bo'), LLo=nb('LLo', 256), LB=[nb('LB0', 256), nb('LB1', 256)],
                                          Q=[nb('Q0'), nb('Q1')], nMT=nb('nMT'), X0=nb('X0', 256),
                                          Xit=[nb('Xit0', 256), nb('Xit1', 256)], Kg=nb('Kg'))
                    for i in range(LOOK + 1):
                        nb = lambda nm, w=128, dt=BF16: newbuf('%s_%d%dp' % (nm, d, i), w, dt)
                        PSET[d, i] = dict(EG=nb('EG', 128, F32), qgT=nb('qgT'), QKT=nb('QKT'), Ub=nb('Ub'), nwT=nb('nwT'),
                                          kd=nb('kd'))

                def PQ(d, name, odd=0):
                    m = {'G': (0, 0), 'K': (0, 1), 'Qp': (0, 3), 'L': (1, 0), 'LB': (1, 1), 'X': (1, 1), 'M': (1, 3),
                         'W': (1, 3), 'v': (2, 0), 'o': (2, 1), 'dS': (2, 2)}[name]
                    if m[0] == 1 and odd:
                        return (6 + d, m[1])
                    return (3 * d + m[0], m[1])

                def chunk_of_block(blk):
                    return 0 if blk < 2 else 1 + (blk - 2) // 4

                hl = list(heads)

                def load_qkv_w(hh):
                    for wi in range(3):
                        c0 = OFF[2] + wi * 1024 + hh * 128
                        load_cast(wqkv[:, :, wi * 128:(wi + 1) * 128], win_d[:, :, c0:c0 + 128], 8, 128, ('wqkv', wi))

                for h in heads:
                    if h == hl[0]:
                        load_qkv_w(h)
                    for d_ in range(2):
                        S.dma(selh[:, d_, :], dr['sel32'][:, d_ * 8 + h, :], [], ['selh'])
                    c0 = OFF[3] + h * 128
                    load_cast(wdg[:], win_d[:, :, c0:c0 + 128], 8, 128, 'wdg')
                    def blocks_of_chunk(c):
                        return [0, 1] if c == 0 else list(range(2 + 4 * (c - 1), 6 + 4 * (c - 1)))

                    for wi in (1, 2, 0):
                        cw = convw[:, wi * 8 + h, :]
                        dstT = KT if wi == 1 else QT
                        dk = 'KT' if wi == 1 else 'QT'
                        for j in range(3):
                            TS('dve', Dg[j][:], ident_bf[:], cw[:, j:j + 1], None, ALU.mult, None, ['ident_bf', 'convw'],
                               [('Dg', j)])

                        def geom(c):
                            t0 = 0 if c == 0 else NCTX + (c - 1) * 512
                            n = 256 if c == 0 else 512
                            col0 = 1 if c == 0 else 259 + (c - 1) * 512
                            return t0, n, col0

                        def stA(c):
                            t0, n, col0 = geom(c)
                            bank = 6 + c % 2
                            for kc in range(8):
                                MM(ps[bank][:, 0:n], wqkv[:, kc, wi * 128:(wi + 1) * 128], hT[:, kc, t0:t0 + n],
                                   [('hT', c), ('wqkv', wi)], pk(bank), start=(kc == 0), stop=(kc == 7))
                            CP('dve', X0[:, col0:col0 + n], ps[bank][:, 0:n], pk(bank), [('X0', c)])

                        def stB(c):
                            t0, n, col0 = geom(c)
                            nb_keys = [('X0', cc) for cc in (c - 1, c, c + 1) if 0 <= cc <= 8] + ['X0']
                            bank = c % 2
                            for j in range(3):
                                MM(ps[bank][:, 0:n], Dg[j][:], X0[:, col0 + j - 1:col0 + j - 1 + n], nb_keys + [('Dg', j)],
                                   pk(bank), start=(j == 0), stop=(j == 2))
                            ACT(X2[:, t0:t0 + n], ps[bank][:, 0:n], AF.Silu, pk(bank), [('X2', c)])

                        def stC(c):
                            t0, n, col0 = geom(c)
                            if wi == 2:
                                blks = blocks_of_chunk(c)
                                bank = 4 + c % 2
                                pst = ps[bank][:].bitcast(BF16)
                                for bi_, blk in enumerate(blks):
                                    TR(pst[:, bi_ * 128:(bi_ + 1) * 128], X2[:, blk * 128:(blk + 1) * 128], ident_bf[:],
                                       [('X2', c), 'ident_bf'], pk(bank))
                                CP('dve' if c % 2 == 0 else 'act', Vtok[:, blks[0]:blks[0] + len(blks), :],
                                   pst[:, 0:len(blks) * 128].rearrange("p (b c) -> p b c", c=128), pk(bank),
                                   [('Vtok', blk) for blk in blks])
                            else:
                                bank = 2 + c % 2
                                sq_ = sqb2[c % 2]
                                TT('dve', sq_[:, 0:n], X2[:, t0:t0 + n], X2[:, t0:t0 + n], ALU.mult, [('X2', c)], [('sqb', c % 2)])
                                MM(ps[bank][:, 0:n], cst['ones_bf'][:], sq_[:, 0:n], [('sqb', c % 2), 'ones_bf'], pk(bank))

                        def stD(c):
                            t0, n, col0 = geom(c)
                            if wi == 2:
                                return
                            bank = 2 + c % 2
                            ACT(lnt[0][:, 0:n], ps[bank][:, 0:n], AF.Ln, pk(bank), ['lnt0'], bias=EPS)
                            ACT(lnt[1][:, 0:n], lnt[0][:, 0:n], AF.Exp, ['lnt0'], ['lnt1'], scale=-0.5,
                                bias=(-0.5 * float(np.log(128.0)) if wi == 0 else 0.0))
                            TT('dve', dstT[:, t0:t0 + n], X2[:, t0:t0 + n], lnt[1][:, 0:n], ALU.mult,
                               [('X2', c), 'lnt1'], [(dk, c)])
                            if wi == 1:
                                blks = blocks_of_chunk(c)
                                bank = 4 + c % 2
                                pst = ps[bank][:].bitcast(BF16)
                                for bi_, blk in enumerate(blks):
                                    TR(pst[:, bi_ * 128:(bi_ + 1) * 128], KT[:, blk * 128:(blk + 1) * 128], ident_bf[:],
                                       [('KT', c), 'ident_bf'], pk(bank))
                                CP('dve' if c % 2 == 0 else 'act', Ktok[:, blks[0]:blks[0] + len(blks), :],
                                   pst[:, 0:len(blks) * 128].rearrange("p (b c) -> p b c", c=128), pk(bank),
                                   [('Ktok', blk) for blk in blks])

                        for i in range(9 + 4):
                            if i < 9:
                                stA(i)
                            if 0 <= i - 2 < 9:
                                stB(i - 2)
                            if 0 <= i - 3 < 9:
                                stC(i - 3)
                            if 0 <= i - 4 < 9:
                                stD(i - 4)
                    if 'qkv' in dbg_d and h == list(heads)[0]:
                        S.dma(dbg_d['qkv'][0], QT[:], [('QT', c) for c in range(9)], ['dbg_q'])
                        S.dma(dbg_d['qkv'][1], KT[:], [('KT', c) for c in range(9)], ['dbg_k'])
                        S.dma(dbg_d['vtok'], Vtok[:], [('Vtok', b) for b in range(NB)], ['dbg_v'])

                    if 'selh' in dbg_d:
                        S.dma(dbg_d['selh'], selh[:], ['selh'], ['dbg_selh'])
                    if hl.index(h) + 1 < len(hl):
                        load_qkv_w(hl[hl.index(h) + 1])
                    S.barrier()
                    for d in range(2):
                        MEMSET('pool', Sf32[d][:], 0.0, [('S', d)])
                        MEMSET('pool', Sbf[d][:], 0.0, [('Sb', d)])
                        for c_ in range(2):
                            MEMSET('pool', vnew[d][c_][:], 0.0, [('vn', d, c_)])
                    orderC = [list(range(68)), [3, 2, 1, 0] + list(range(67, 3, -1))]
                    orderB = []
                    for d in range(2):
                        ob = []
                        for c in orderC[d]:
                            if c // 2 not in ob:
                                ob.append(c // 2)
                        orderB.append(ob)
                    owritten = set()

                    def setup_micro(d, bi):
                        blk = orderB[d][bi]
                        I = ISET[d, bi % LOOK]
                        P = PSET[d, bi % (LOOK + 1)]
                        odd = bi % 2
                        col = d * 8 + h
                        kq = [('QT', chunk_of_block(blk)), ('KT', chunk_of_block(blk))]
                        tks = slice(blk * 128, (blk + 1) * 128)
                        bG, qG = PQ(d, 'G')
                        bK, qK = PQ(d, 'K')
                        bQ, qQ = PQ(d, 'Qp')
                        bL, qL = PQ(d, 'L', odd)
                        bLB, qLB = PQ(d, 'LB', odd)
                        bM, qM = PQ(d, 'M', odd)
                        ik = lambda nm: ('i', nm, d, bi % LOOK)
                        pkk = lambda nm: ('p', nm, d, bi % (LOOK + 1))
                        gcc = gc_tm[:, blk, col:col + 1]
                        bcol = beta_tm[:, blk, col:col + 1]
                        mneg = cst['mnegF'] if d == 0 else cst['mnegB']
                        md = cst['mdF'] if d == 0 else cst['mdB']
                        mo = cst['moF'] if d == 0 else cst['moB']
                        pstL = ps[bL][:].bitcast(BF16)[:, qL * 256:qL * 256 + 256]
                        pstW = ps[bM][:].bitcast(BF16)[:, qM * 256:qM * 256 + 128]
                        L0, Lo = I['LLo'][:, 0:128], I['LLo'][:, 128:256]

                        def m0():
                            ACT(P['kd'], Ktok[:, blk, :], AF.Copy, [('Ktok', blk), 'egl_tm'], [pkk('kd')],
                                scale=egl_tm[:, blk, col:col + 1])
                            TS('pool', I['Kg'], Ktok[:, blk, :], egc_tm[:, blk, col:col + 1], None, ALU.mult, None,
                               [('Ktok', blk), 'egc_tm'], [ik('Kg')])

                        def m1():
                            MM(psq(bG, qG), selh[:, d, :], gcT[:, blk, :], ['selh', 'gcT'], pk(bG, qG, 1))
                            MM(psq(bK, qK), KT[:, tks], QT[:, tks], kq, pk(bK, qK, 1))
                            MM(psq(bK, qK + 1), KT[:, tks], KT[:, tks], kq, pk(bK, qK + 1, 1))

                        def m2():
                            STT('dve', I['X'], psq(bG, qG), gcc, mneg[:], ALU.subtract, ALU.add,
                                pk(bG, qG, 1) + ['gc_tm', 'mnegF', 'mnegB'], [ik('X')])
                            ACT(P['EG'], psq(bG, qG), AF.Exp, pk(bG, qG, 1), [pkk('EG')])

                        def m3():
                            if 'EG' in dbg_d and d == 0 and bi == 0 and h == hl[0]:
                                S.dma(dbg_d['EG'], P['EG'], [pkk('EG')], ['dbg_EG'])
                                S.dma(dbg_d['Xd'], I['X'], [ik('X')], ['dbg_Xd'])
                            ACT(I['N'], I['X'], AF.Exp, [ik('X')], [ik('N')])
                            TT('dve', P['qgT'], QT[:, tks], P['EG'], ALU.mult, kq + [pkk('EG')], [pkk('qgT')])

                        def m4():
                            STT('dve', I['Bns'], psq(bK, qK + 1), bcol, I['N'], ALU.mult, ALU.mult,
                                pk(bK, qK + 1, 1) + [ik('N'), 'beta_tm'], [ik('Bns')])
                            TT('dve', P['QKT'], psq(bK, qK), I['N'], ALU.mult, pk(bK, qK, 1) + [ik('N')], [pkk('QKT')])

                        def m5():
                            TT('dve', I['Bd'], I['Bns'], md[:], ALU.mult, [ik('Bns'), 'mdF', 'mdB'], [ik('Bd')])
                            TT('pool', I['Bo'], I['Bns'], mo[:], ALU.mult, [ik('Bns'), 'moF', 'moB'], [ik('Bo')])

                        def m6():
                            TT('dve', I['Q'][0], ident_bf[:], I['Bd'], ALU.subtract, [ik('Bd'), 'ident_bf'], [ik('Q0')])
                            TR(pstL[:, 0:128], I['Bd'], ident_bf[:], [ik('Bd'), 'ident_bf'], pk(bL, qL, 1))
                            TR(pstL[:, 128:256], I['Bo'], ident_bf[:], [ik('Bo'), 'ident_bf'], pk(bL, qL, 1))

                        def m7():
                            CP('act', I['LLo'], pstL, pk(bL, qL, 1), [ik('LLo')])

                        def levA(k):
                            def f():
                                if k == 1:
                                    Lp, Bp, kLp, kBp = L0, I['Bd'], ik('LLo'), ik('Bd')
                                else:
                                    prev = I['LB'][(k - 1) % 2]
                                    Lp, Bp = prev[:, 0:128], prev[:, 128:256]
                                    kLp = kBp = ik('LB%d' % ((k - 1) % 2))
                                MM(psq(bLB, qLB), Bp, Lp, [kLp, kBp], pk(bLB, qLB, 1))
                                if k < 3:
                                    MM(psq(bLB, qLB + 1), Lp, Bp, [kLp, kBp], pk(bLB, qLB + 1, 1))
                            return f

                        def levB(k):
                            def f():
                                cur = I['LB'][k % 2]
                                if k < 3:
                                    CP('act' if k == 1 else 'dve', cur[:, 0:256], psq(bLB, qLB, 2), pk(bLB, qLB, 2),
                                       [ik('LB%d' % (k % 2))])
                                else:
                                    CP('act', cur[:, 0:128], psq(bLB, qLB), pk(bLB, qLB, 1), [ik('LB%d' % (k % 2))])
                            return f

                        def levC(k):
                            def f():
                                cur = I['LB'][k % 2]
                                MM(psq(bQ, qQ), cur[:, 0:128], I['Q'][(k - 1) % 2],
                                   [ik('LB%d' % (k % 2)), ik('Q%d' % ((k - 1) % 2))], pk(bQ, qQ, 1))
                            return f

                        def levD(k):
                            def f():
                                TT('dve', I['Q'][k % 2], psq(bQ, qQ), I['Q'][(k - 1) % 2], ALU.add,
                                   pk(bQ, qQ, 1) + [ik('Q%d' % ((k - 1) % 2))], [ik('Q%d' % (k % 2))])
                            return f

                        def m20():
                            TdT = I['Q'][1]
                            MM(psq(bM, qM), Lo, TdT, [ik('LLo'), ik('Q1')], pk(bM, qM, 1))
                            MM(psq(bLB, qLB), TdT, Vtok[:, blk, :], [ik('Q1'), ('Vtok', blk)], pk(bLB, qLB, 1))
                            MM(psq(bLB, qLB + 1), TdT, I['Kg'], [ik('Q1'), ik('Kg')], pk(bLB, qLB + 1, 1))

                        def m21():
                            ACT(I['nMT'], psq(bM, qM), AF.Copy, pk(bM, qM, 1), [ik('nMT')], scale=-1.0)
                            CP('act', I['X0'], psq(bLB, qLB, 2), pk(bLB, qLB, 2), [ik('X0')])

                        def itA(n):
                            def f():
                                prev = I['X0'] if n == 1 else I['Xit'][(n - 1) % 2]
                                kprev = ik('X0') if n == 1 else ik('Xit%d' % ((n - 1) % 2))
                                MM(psq(bLB, qLB, 2), I['nMT'], prev[:, 0:256], [ik('nMT'), kprev], pk(bLB, qLB, 2))
                            return f

                        def itB(n):
                            def f():
                                TT('dve', I['Xit'][n % 2], psq(bLB, qLB, 2), I['X0'], ALU.add,
                                   pk(bLB, qLB, 2) + [ik('X0')], [ik('Xit%d' % (n % 2))])
                            return f

                        def m28():
                            X3 = I['Xit'][1]
                            ACT(P['Ub'], X3[:, 0:128], AF.Copy, [ik('Xit1'), 'beta_tm'], [pkk('Ub')], scale=bcol)
                            TR(pstW, X3[:, 128:256], ident_bf[:], [ik('Xit1'), 'ident_bf'], pk(bM, qM, 1))

                        def m29():
                            ACT(P['nwT'], pstW, AF.Copy, pk(bM, qM, 1), [pkk('nwT')], scale=-1.0)
                        return [m0, m1, m2, m3, m4, m5, m6, m7,
                                levA(1), levB(1), levC(1), levD(1), levA(2), levB(2), levC(2), levD(2),
                                levA(3), levB(3), levC(3), levD(3), m20, m21,
                                itA(1), itB(1), itA(2), itB(2), itA(3), itB(3), m28, m29]

                    def step_micro(d, s_):
                        c = orderC[d][s_]
                        blk, ci = c // 2, c % 2
                        bi = orderB[d].index(blk)
                        P = PSET[d, bi % (LOOK + 1)]
                        col = d * 8 + h
                        cs = slice(ci * 64, ci * 64 + 64)
                        bv, qv = PQ(d, 'v')
                        bo, qo = PQ(d, 'o')
                        bs, qs = PQ(d, 'dS')
                        pkk = lambda nm: ('p', nm, d, bi % (LOOK + 1))

                        def u0():
                            MM(psq(bv, qv), P['nwT'], Sbf[d][:], [pkk('nwT'), ('Sb', d)], pk(bv, qv, 1))

                        def u1():
                            STT('dve', vnew[d][ci][cs, :], ps[bv][cs, qv * 128:(qv + 1) * 128],
                                beta_tm[cs, blk, col:col + 1], P['Ub'][cs, :], ALU.mult, ALU.add,
                                pk(bv, qv, 1) + ['beta_tm', pkk('Ub')], [('vn', d, ci)])

                        def u2():
                            MM(psq(bs, qs), P['kd'], vnew[d][ci][:], [pkk('kd'), ('vn', d, ci)], pk(bs, qs, 1))
                            if blk >= 2:
                                MM(psq(bo, qo), P['qgT'], Sbf[d][:], [pkk('qgT'), ('Sb', d)], pk(bo, qo, 1),
                                   start=True, stop=False)
                                MM(psq(bo, qo), P['QKT'], vnew[d][ci][:], [pkk('QKT'), ('vn', d, ci)], pk(bo, qo, 1),
                                   start=False, stop=True)

                        def u3():
                            lastcol = ci * 64 + (63 if d == 0 else 0)
                            STT('dve', Sf32[d][:], Sf32[d][:], P['EG'][:, lastcol:lastcol + 1], psq(bs, qs), ALU.mult,
                                ALU.add, [('S', d), pkk('EG')] + pk(bs, qs, 1), [('S', d)])
                            if blk >= 2:
                                okey = ('O', blk, ci)
                                src = ps[bo][cs, qo * 128:(qo + 1) * 128]
                                if okey not in owritten:
                                    owritten.add(okey)
                                    CP('act', Obuf[cs, blk - 2, :], src, pk(bo, qo, 1), [okey])
                                else:
                                    TT('dve', Obuf[cs, blk - 2, :], src, Obuf[cs, blk - 2, :], ALU.add,
                                       pk(bo, qo, 1) + [okey], [okey])

                        def u4():
                            CP('act', Sbf[d][:], Sf32[d][:], [('S', d)], [('Sb', d)])
                        return [u0, u1, u2, u3, u4]

                    NBK = len(orderB[0])
                    smic = {}
                    for g in range(-10 * LOOK, 5 * 68):
                        if g >= 0:
                            s_, u = g // 5, g % 5
                            for d in range(2):
                                if u == 0:
                                    smic[d] = step_micro(d, s_)
                                smic[d][u]()
                        for bi in range(NBK):
                            g0 = 10 * (bi - LOOK)
                            if g0 <= g < g0 + 30:
                                m = g - g0
                                for d in range(2):
                                    key = ('setup', d, bi)
                                    if key not in smic:
                                        smic[key] = setup_micro(d, bi)
                                    smic[key][m]()
                    S.barrier()
                    if 'O' in dbg_d and h == list(heads)[0]:
                        S.dma(dbg_d['O'], Obuf[:], [('O', b, ci) for b in range(2, 34) for ci in range(2)], ['dbg_O'])
                    if 'Sfin' in dbg_d and h == list(heads)[0]:
                        S.dma(dbg_d['Sfin'][0], Sf32[0][:], [('S', 0)], ['dbg_S0'])
                        S.dma(dbg_d['Sfin'][1], Sf32[1][:], [('S', 1)], ['dbg_S1'])

                    OK_ALL = lambda b: [('O', b + 2, 0), ('O', b + 2, 1)]
                    for b in range(32):
                        ACT(sqb[:, 0:128], Obuf[:, b, :], AF.Square, OK_ALL(b), ['sqb', 'ssq'], accum=ssq[:, b:b + 1])
                    TS('dve', ssq[:], ssq[:], 1.0 / 128, EPS, ALU.mult, ALU.add, ['ssq'], ['ssq'])
                    TT('pool', rstd_o[:], ssq[:], mhalf[:, 0:32], ALU.pow, ['ssq', 'mhalf'], ['rstd_o'])
                    for tc in range(8):
                        bank = 6 + tc % 2
                        t0 = NCTX + tc * 512
                        for kc in range(8):
                            MM(ps[bank][:, :], wdg[:, kc, :], hT[:, kc, t0:t0 + 512], [('hT', 1 + tc), 'wdg'], pk(bank),
                               start=(kc == 0), stop=(kc == 7))
                        ACT(X2[:, tc * 512:(tc + 1) * 512], ps[bank][:, :], AF.Silu, pk(bank), ['X2'])
                    for tc in range(8):
                        bank = 6 + tc % 2
                        pst = ps[bank][:].bitcast(BF16)
                        for bi in range(4):
                            b = tc * 4 + bi
                            STT('dve', Onb[bi][:], Obuf[:, b, :], rstd_o[:, b:b + 1], normw[:], ALU.mult, ALU.mult,
                                OK_ALL(b) + ['rstd_o', 'normw'], [('Onb', bi)])
                            TR(pst[:, bi * 128:(bi + 1) * 128], Onb[bi][:], ident_bf[:], [('Onb', bi), 'ident_bf'], pk(bank))
                        TT('dve', X1[:, tc * 512:(tc + 1) * 512], pst[:, 0:512], X2[:, tc * 512:(tc + 1) * 512], ALU.mult,
                           pk(bank) + ['X2'], ['X1'])
                    S.dma(og_s[h], X1[:, 0:NLAT], ['X1'], [('og_s', h)], semkey='og_s', waitall=True, eng='pool')
                    S.barrier()
            S.barrier()
        if 'fo' in phases:
            with ExitStack() as es3:
                def sb3(name, shape, dt=F32):
                    return es3.enter_context(nc.sbuf_tensor('sb_' + name, list(shape), dt))
                load_consts(('ccsc',), sb3, 'consts_fo')
                c64 = sb3('c64', [64, 128], BF16)
                rr = sb3('rr', [64, 64, 2, 128], BF16)
                S.dma(c64[:], dr['c64'], [], ['c64'])
                S.dma(rr[:], dr['rr'], [], ['rr'])
                wfv = sb3('wfv', [128, 8, 512], BF16)
                wfg = sb3('wfg', [128, 8, 512], BF16)
                wfm = sb3('wfm', [128, 4, 128], BF16)
                load_cast(wfv[:], win_d[:, :, OFF[0]:OFF[0] + 512], 8, 512, 'wfv')
                load_cast(wfg[:], win_d[:, :, OFF[1]:OFF[1] + 512], 8, 512, 'wfg')
                load_cast(wfm[:], wfm_d, 4, 128, 'wfm')
                uA = sb3('uA', [64, 64, 256], BF16)
                Y = sb3('Y', [64, 128, 2, 64], BF16)
                PT = sb3('PT', [128, 2, NLAT], BF16)
                G = sb3('G', [128, 256], BF16)
                SG = [sb3('SG0', [128, 512], BF16)] * 2
                FMc = [sb3('FMc0', [128, 512], BF16)] * 2
                LAT = [('hT', 1 + tc) for tc in range(8)]
                for g in range(4):
                    if g % 2 == 0:
                        for n0 in range(0, 64, 2):
                            bank = (n0 // 2) % 2
                            for a in range(2):
                                n2 = n0 + a
                                for kc in range(8):
                                    MM(ps[bank][0:64, a * 256:(a + 1) * 256], hT[:, kc, NCTX + n2:NCTX + NLAT:64],
                                       wfv[:, kc, g * 128:g * 128 + 256], LAT + ['wfv'], pk(bank),
                                       start=(kc == 0), stop=(kc == 7))
                            CP('act' if bank == 0 else 'dve', uA[:, n0:n0 + 2, :],
                               ps[bank][0:64, :].rearrange("p (a c) -> p a c", c=256), pk(bank), ['uA'])
                    go = (g % 2) * 128
                    for c0 in range(0, 128, 4):
                        bank = 2 + (c0 // 4) % 2
                        for a in range(4):
                            MM(ps[bank][0:64, a * 128:(a + 1) * 128], uA[:, :, go + c0 + a], c64[:, :], ['uA', 'c64'], pk(bank))
                        CP('act' if bank == 2 else 'dve', Y[:, c0:c0 + 4, :, :],
                           ps[bank][0:64, :].rearrange("p (c r k) -> p c r k", c=4, r=2), pk(bank), ['Y'])
                    if g == 0:
                        if 'uA' in dbg_d:
                            S.dma(dbg_d['uA'], uA[:, :, 0:128], ['uA'], ['dbg_uA'])
                    PTv = PT[:].rearrange("p r (k2 k1) -> p r k2 k1", k1=64)
                    for k0 in range(0, 64, 4):
                        bank = 4 + (k0 // 4) % 2
                        for a in range(4):
                            k1 = k0 + a
                            MM(ps[bank][:, a * 128:(a + 1) * 128], Y[:, :, 0, k1], rr[:, k1, 0, :], ['Y', 'rr'], pk(bank),
                               start=True, stop=False)
                            MM(ps[bank][:, a * 128:(a + 1) * 128], Y[:, :, 1, k1], rr[:, k1, 1, :], ['Y', 'rr'], pk(bank),
                               start=False, stop=True)
                        CP('act' if bank == 4 else 'dve',
                           PTv[:, :, :, k0:k0 + 4].rearrange("p r k a -> p a r k"),
                           ps[bank][:, :].rearrange("p (a r k) -> p a r k", a=4, r=2), pk(bank), ['PT'])
                    if g == 0 and 'PT' in dbg_d:
                        S.dma(dbg_d['PT'], PT[:], ['PT'], ['dbg_PT'])
                    MM(ps[6][:, 0:128], cst['ccsc'][:, 0:128], wfm[:, g, :], ['ccsc', 'wfm'], pk(6))
                    MM(ps[6][:, 128:256], cst['ccsc'][:, 128:256], wfm[:, g, :], ['ccsc', 'wfm'], pk(6))
                    CP('dve', G[:], ps[6][:, 0:256], pk(6), ['G'])
                    for tc in range(8):
                        i = tc % 2
                        t0 = NCTX + tc * 512
                        for kc in range(8):
                            MM(ps[7][:, :], wfg[:, kc, g * 128:(g + 1) * 128], hT[:, kc, t0:t0 + 512],
                               [('hT', 1 + tc), 'wfg'], pk(7), start=(kc == 0), stop=(kc == 7))
                        ACT(SG[i][:], ps[7][:, :], AF.Silu, pk(7), [('SG', 0)])
                        MM(ps[6][:, :], G[:, 0:128], PT[:, 0, tc * 512:(tc + 1) * 512], ['G', 'PT'], pk(6),
                           start=True, stop=False)
                        MM(ps[6][:, :], G[:, 128:256], PT[:, 1, tc * 512:(tc + 1) * 512], ['G', 'PT'], pk(6),
                           start=False, stop=True)
                        TT('dve', FMc[i][:], ps[6][:, :], SG[i][:], ALU.mult, pk(6) + [('SG', 0)], [('FMc', 0)])
                        if 'fm' in dbg_d:
                            S.dma(dbg_d['fm'][g, :, tc * 512:(tc + 1) * 512], FMc[i][:], [('FMc', 0)], ['dbg_fm'])
                        S.dma(fm_s[g, :, tc * 512:(tc + 1) * 512], FMc[i][:], [('FMc', 0)], [('fm_s', g, tc)], semkey='fm_s', waitall=True, eng='pool')
            S.barrier()

        if 'fin' in phases:
            wout = sb('wout', [128, 8, D], BF16)
            with ExitStack() as es4:
                def sb4(name, shape, dt=F32):
                    return es4.enter_context(nc.sbuf_tensor('sb_' + name, list(shape), dt))
                wrf = sb4('wrf', [128, 8, D], BF16)
                wrd = sb4('wrd', [128, 8, D], BF16)
                wfo = sb4('wfo', [128, 4, D], BF16)
                wdn = sb4('wdn', [128, 8, D], BF16)
                for cb in range(4):
                    cs_ = slice(cb * 256, (cb + 1) * 256)
                    S.dma(wrf[:, :, cs_], win_d[:, :, OFF[6] + cb * 256:OFF[6] + (cb + 1) * 256], [], [('wrf', cb)], eng='pool')
                    S.dma(wrd[:, :, cs_], win_d[:, :, OFF[7] + cb * 256:OFF[7] + (cb + 1) * 256], [], [('wrd', cb)], eng='pool')
                    S.dma(wfo[:, :, cs_], wfo_d[:, :, cs_], [], [('wfo', cb)], eng='pool')
                    S.dma(wdn[:, :, cs_], wdn_d[:, :, cs_], [], [('wdn', cb)], eng='pool')
                load_cast(wout[:], wout_d, 8, D, 'wout')
                if 'wrf' in dbg_d:
                    S.dma(dbg_d['wrf'], wrf[:], [('wrf', cb) for cb in range(4)], ['dbg_wrf'])
                    S.dma(dbg_d['hT2'], hT[:, :, NCTX:NCTX + 512], ALLHT, ['dbg_hT2'])
                ogc = [sb4('ogc%d' % i, [128, 8, 512], BF16) for i in range(2)]
                fmc = [sb4('fmc%d' % i, [128, 4, 512], BF16) for i in range(2)]
                sgf = [sb4('sgf%d' % i, [128, 512], BF16) for i in range(2)]
                sgd = [sb4('sgd%d' % i, [128, 512], BF16) for i in range(2)]
                m1 = [sb4('m1_%d' % i, [128, 512], BF16) for i in range(2)]
                m2 = [sb4('m2_%d' % i, [128, 512], BF16) for i in range(2)]
                mgo = [sb4('mgo%d' % i, [128, 512], BF16) for i in range(2)]
                for tc in range(8):
                    ci = tc % 2
                    t0 = NCTX + tc * 512
                    if 'dn' in phases:
                        S.dma(ogc[ci][:], og_s[:, :, tc * 512:(tc + 1) * 512].rearrange("h p t -> p h t"),
                              [('og_s', h) for h in range(H)], [('ogc', ci)])
                    else:
                        MEMSET('pool', ogc[ci][:], 0.0, [('ogc', ci)])
                    if 'fo' in phases:
                        S.dma(fmc[ci][:], fm_s[:, :, tc * 512:(tc + 1) * 512].rearrange("g p t -> p g t"),
                              [('fm_s', g, tc) for g in range(4)], [('fmc', ci)])
                    else:
                        MEMSET('pool', fmc[ci][:], 0.0, [('fmc', ci)])
                    for fc in range(8):
                        i = fc % 2
                        bb = 4 * i
                        fs = slice(fc * 128, (fc + 1) * 128)
                        for kc in range(8):
                            MM(ps[bb][:, :], wrf[:, kc, fs], hT[:, kc, t0:t0 + 512], [('hT', 1 + tc), ('wrf', fc // 2)], pk(bb),
                               start=(kc == 0), stop=(kc == 7))
                        for kc in range(8):
                            MM(ps[bb + 1][:, :], wrd[:, kc, fs], hT[:, kc, t0:t0 + 512], [('hT', 1 + tc), ('wrd', fc // 2)], pk(bb + 1),
                               start=(kc == 0), stop=(kc == 7))
                        for g in range(4):
                            MM(ps[bb + 2][:, :], wfo[:, g, fs], fmc[ci][:, g, :], [('fmc', ci), ('wfo', fc // 2)], pk(bb + 2),
                               start=(g == 0), stop=(g == 3))
                        for h in range(H):
                            MM(ps[bb + 3][:, :], wdn[:, h, fs], ogc[ci][:, h, :], [('ogc', ci), ('wdn', fc // 2)], pk(bb + 3),
                               start=(h == 0), stop=(h == 7))
                        ACT(sgf[i][:], ps[bb][:, :], AF.Sigmoid, pk(bb), [('sgf', i)])
                        ACT(sgd[i][:], ps[bb + 1][:, :], AF.Sigmoid, pk(bb + 1), [('sgd', i)])
                        TT('dve', m1[i][:], ps[bb + 2][:, :], sgf[i][:], ALU.mult, pk(bb + 2) + [('sgf', i)], [('m1', i)])
                        TT('dve', m2[i][:], ps[bb + 3][:, :], sgd[i][:], ALU.mult, pk(bb + 3) + [('sgd', i)], [('m2', i)])
                        TT('dve', mgo[i][:], m1[i][:], m2[i][:], ALU.add, [('m1', i), ('m2', i)], [('mgo', i)])
                        S.dma(mg_s[fc, :, tc * 512:(tc + 1) * 512], mgo[i][:], [('mgo', i)], [('mg_s', tc)], semkey='mg_s', waitall=True, eng='pool')
                        if 'taps' in dbg_d and tc == 0:
                            S.dma(dbg_d['taps'][0, fc], sgf[i][:], [('sgf', i)], ['dbg_t0'])
                            S.dma(dbg_d['taps'][1, fc], sgd[i][:], [('sgd', i)], ['dbg_t1'])
                            S.dma(dbg_d['taps'][2, fc], m1[i][:], [('m1', i)], ['dbg_t2'])
                            S.dma(dbg_d['taps'][3, fc], m2[i][:], [('m2', i)], ['dbg_t3'])
                        if 'merged' in dbg_d and tc == 0:
                            S.dma(dbg_d['merged'][fc], mgo[i][:], [('mgo', i)], ['dbg_mg'])
            S.barrier()
            with ExitStack() as es5:
                def sb5(name, shape, dt=F32):
                    return es5.enter_context(nc.sbuf_tensor('sb_' + name, list(shape), dt))
                lng = sb5('lng', [128, D])
                lnb = sb5('lnb', [128, D])
                S.dma(lng[:], lng_d, [], ['lng'])
                S.dma(lnb[:], lnb_d, [], ['lnb'])
                mg = [sb5('mg%d' % i, [128, 8, 512], BF16) for i in range(2)]
                NBF = 4
                xt = [sb5('fxt%d' % i, [128, D]) for i in range(NBF)]
                pet = [sb5('fpet%d' % i, [128, D]) for i in range(NBF)]
                pre = [sb5('pre%d' % i, [128, D]) for i in range(NBF)]
                sq = sb5('fsq', [128, D], BF16)
                st = [sb5('fst%d' % i, [128, 8]) for i in range(8)]

                def fA(ti):
                    tc, tt = ti // 4, ti % 4
                    ci = tc % 2
                    j = ti % NBF
                    xk, pk_, prk, sk = ('fxt', j), ('fpet', j), ('pre', j), ('fst', ti % 8)
                    stt = st[ti % 8]
                    if tt == 0:
                        S.dma(mg[ci][:], mg_s[:, :, tc * 512:(tc + 1) * 512].rearrange("f p t -> p f t"),
                              [('mg_s', tc)], [('mg', ci)])
                    S.dma(xt[j][:], x_d[ti * 128:(ti + 1) * 128, :], [], [xk])
                    S.dma(pet[j][:], dr['pe'][ti * 128:(ti + 1) * 128, :], [], [pk_])
                    TT('dve', xt[j][:], xt[j][:], pet[j][:], ALU.add, [xk, pk_], [xk])
                    for half in range(2):
                        bank = 2 * (ti % 4) + half
                        for kc in range(8):
                            MM(ps[bank][:, :], mg[ci][:, kc, tt * 128:(tt + 1) * 128],
                               wout[:, kc, half * 512:(half + 1) * 512], [('mg', ci), 'wout'], pk(bank),
                               start=(kc == 0), stop=(kc == 7))
                        TT('dve', pre[j][:, half * 512:(half + 1) * 512], ps[bank][:, :],
                           gxb[:, half * 512:(half + 1) * 512], ALU.mult, pk(bank) + ['gxb'], [prk])
                    STT('dve', pre[j][:], xt[j][:], ALPHA, pre[j][:], ALU.mult, ALU.add, [xk, prk], [prk])
                    RED(stt[:, 0:1], pre[j][:], [prk], [sk])
                    ACT(sq[:], pre[j][:], AF.Square, [prk], ['fsq', sk], accum=stt[:, 1:2])

                def fB(ti):
                    sk = ('fst', ti % 8)
                    stt = st[ti % 8]
                    TS('dve', stt[:, 2:3], stt[:, 0:1], 1.0 / D, None, ALU.mult, None, [sk], [sk])
                    TT('dve', stt[:, 3:4], stt[:, 2:3], stt[:, 2:3], ALU.mult, [sk], [sk])
                    STT('dve', stt[:, 4:5], stt[:, 1:2], 1.0 / D, stt[:, 3:4], ALU.mult, ALU.subtract, [sk], [sk])
                    TS('dve', stt[:, 7:8], stt[:, 4:5], EPS, None, ALU.add, None, [sk], [sk])
                    TT('pool', stt[:, 5:6], stt[:, 7:8], mhalf[:, 0:1], ALU.pow, [sk, 'mhalf'], [sk])

                def fC(ti):
                    j = ti % NBF
                    prk, sk = ('pre', j), ('fst', ti % 8)
                    stt = st[ti % 8]
                    STT('dve', stt[:, 6:7], stt[:, 2:3], -1.0, stt[:, 5:6], ALU.mult, ALU.mult, [sk], [sk])
                    ACT(pre[j][:], pre[j][:], AF.Identity, [prk, sk], [prk], scale=stt[:, 5:6], bias=stt[:, 6:7])
                    TT('dve', pre[j][:], pre[j][:], lng[:], ALU.mult, [prk, 'lng'], [prk])
                    TT('dve', pre[j][:], pre[j][:], lnb[:], ALU.add, [prk, 'lnb'], [prk])
                    S.dma(out_d[ti * 128:(ti + 1) * 128, :], pre[j][:], [prk], [('out', ti)], semkey='out', eng='pool')

                for n in range(32 + 2):
                    if n < 32:
                        fA(n)
                    if 0 <= n - 1 < 32:
                        fB(n - 1)
                    if 0 <= n - 2 < 32:
                        fC(n - 2)
        S.emit()
    return nc, consts


_CACHE = {}


def _prep_shared(inputs):
    f = np.float32
    w_in = np.asarray(inputs['w_in'], f)[0]
    sh = {}
    sh['w_mod'] = np.ascontiguousarray(np.asarray(inputs['w_mod'], f)[0].reshape(128, 8, 3 * D))
    sh['b_mod2'] = np.ascontiguousarray(np.tile(np.asarray(inputs['b_mod'], f)[0][None], (2, 1)))
    sh['w_in_r'] = np.ascontiguousarray(w_in.reshape(8, 128, INCOLS).transpose(1, 0, 2))
    sh['w_dn_r'] = np.ascontiguousarray(np.asarray(inputs['w_dn_out'], f)[0].reshape(8, 128, D).transpose(1, 0, 2))
    sh['w_out_r'] = np.ascontiguousarray(np.asarray(inputs['w_out'], f)[0].reshape(8, 128, D).transpose(1, 0, 2))
    sh['w_fo_r'] = np.ascontiguousarray(np.asarray(inputs['w_f_out'], f)[0].reshape(4, 128, D).transpose(1, 0, 2))
    sh['w_fmix_r'] = np.ascontiguousarray(np.asarray(inputs['w_fmix'], f)[0].transpose(1, 0, 2))
    sh['convw_r'] = np.ascontiguousarray(np.asarray(inputs['conv_w'], f)[0].T.reshape(24, 128, 3).transpose(1, 0, 2))
    sh['alog_t'] = np.ascontiguousarray(np.tile(np.asarray(inputs['a_log'], f)[0].reshape(1, 16), (128, 1)))
    sh['dtb_t'] = np.ascontiguousarray(np.tile(np.asarray(inputs['dt_bias'], f)[0].reshape(1, 16), (128, 1)))
    sh['normw_t'] = np.ascontiguousarray(np.tile(np.asarray(inputs['dn_norm_w'], f)[0].reshape(1, 128), (128, 1)))
    sh['lng_t'] = np.ascontiguousarray(np.tile(np.asarray(inputs['ln_g'], f)[0].reshape(1, D), (128, 1)))
    sh['lnb_t'] = np.ascontiguousarray(np.tile(np.asarray(inputs['ln_b'], f)[0].reshape(1, D), (128, 1)))
    return sh


def make_in_maps(inputs, consts, cores):
    f = np.float32
    sh = _prep_shared(inputs)
    x = np.asarray(inputs['x'], f)
    ctx = np.asarray(inputs['ctx'], f)
    c = np.asarray(inputs['c'], f)
    c_ctx = np.asarray(inputs['c_ctx'], f)
    maps = []
    for b in cores:
        m = dict(consts)
        m.update(sh)
        m['x'] = np.ascontiguousarray(x[b])
        m['ctx'] = np.ascontiguousarray(ctx[b])
        m['cc'] = np.ascontiguousarray(np.stack([c[b].reshape(128, 8), c_ctx.reshape(128, 8)], -1).reshape(128, 16))
        maps.append(m)
    return maps


def kernel(**inputs):
    if 'nc' not in _CACHE:
        _CACHE['nc'] = build()
    nc, consts = _CACHE['nc']
    maps = make_in_maps(inputs, consts, range(8))
    res = run_bass_kernel_spmd(nc, maps, core_ids=list(range(8)))
    out = np.stack([np.asarray(r['out'], np.float32) for r in res.results], 0)
    return out
```

```python
from contextlib import ExitStack
import numpy as np
import ml_dtypes
import concourse.bass as bass
import concourse.mybir as mybir
from concourse.bass_utils import run_bass_kernel_spmd

F32 = mybir.dt.float32
BF16 = mybir.dt.bfloat16
AF = mybir.ActivationFunctionType
ALU = mybir.AluOpType
AX = mybir.AxisListType

D = 1024
NLAT = 4096
NCTX = 256
NTOK = NCTX + NLAT
H = 8
IN_SPLITS = (512, 512, 3072, 1024, 16, 16, 1024, 1024)
OFF = [0]
for _s in IN_SPLITS:
    OFF.append(OFF[-1] + _s)
INCOLS = OFF[-1]
EPS = 1e-6
ALPHA = 2.0 ** 0.25
NEG = -30000.0
import os
NOEXCL = bool(os.environ.get('NOEXCL'))


class Sched:
    def __init__(self, nc):
        self.nc = nc
        self.ops = []
        self.barriers = []

    def barrier(self):
        self.barriers.append(len(self.ops))

    def add(self, eng, fn, reads=(), writes=(), dma=False, semkey=None, waitall=False):
        self.ops.append(dict(eng=eng, fn=fn, reads=tuple(reads), writes=tuple(writes), dma=dma,
                             semkey=semkey, waitall=waitall))

    def pe(self, fn, r, w):
        self.add('pe', fn, r, w)

    def act(self, fn, r, w):
        self.add('act', fn, r, w)

    def dve(self, fn, r, w):
        self.add('dve', fn, r, w)

    def pool(self, fn, r, w):
        self.add('pool', fn, r, w)

    def dma(self, out, in_, r, w, eng='sp', semkey=None, waitall=False):
        self.add(eng, lambda e: e.dma_start(out=out, in_=in_), r, w, dma=True, semkey=semkey, waitall=waitall)

    def emit(self):
        nc = self.nc
        ops = self.ops
        last_writer = {}
        readers = {}
        last_on_eng = {}
        last_dma = {}
        pending = {}
        bank_last = {}
        bars = set(self.barriers)
        for i, op in enumerate(ops):
            if i in bars:
                bd = set(last_on_eng.values()) | set(last_dma.values())
                for en in ('pe', 'act', 'dve', 'pool', 'sp'):
                    pending[en] = set(bd) | pending.get(en, set())
            deps = set()
            if pending.get(op['eng']):
                deps |= pending.pop(op['eng'])
            for k in op['reads']:
                if k in last_writer:
                    deps.add(last_writer[k])
            for k in op['writes']:
                if k in last_writer:
                    deps.add(last_writer[k])
                deps.update(readers.get(k, ()))
            deps.discard(i)
            for k in op['reads'] + op['writes']:
                if isinstance(k, tuple) and k and k[0] == 'ps':
                    b = k[1]
                    la = bank_last.get(b)
                    if la is not None and ops[la]['eng'] != op['eng'] and not NOEXCL:
                        deps.add(la)
                    bank_last[b] = i
            if op['eng'] == 'pe':
                deps = {d for d in deps if not (ops[d]['eng'] == 'pe' and not ops[d]['dma'])}
            op['deps'] = deps
            if op['dma']:
                last_dma[op['writes'][0]] = i
            else:
                last_on_eng[op['eng']] = i
            for k in op['reads']:
                readers.setdefault(k, []).append(i)
            for k in op['writes']:
                last_writer[k] = i
                readers[k] = []
        needed = set()
        for op in ops:
            needed.update(op['deps'])
        with ExitStack() as es:
            engsem = {}
            for en in ('pe', 'act', 'dve', 'pool'):
                engsem[en] = es.enter_context(nc.semaphore('s_' + en))
            dmasem = {}
            dmacnt = {}
            cnt = {en: 0 for en in engsem}
            for i, op in enumerate(ops):
                if op['dma']:
                    key = op['semkey'] if op['semkey'] is not None else op['writes'][0]
                    if key not in dmasem:
                        dmasem[key] = es.enter_context(nc.semaphore('d%d' % len(dmasem)))
                        dmacnt[key] = 0
                    dmacnt[key] += 16
                    op['sig'] = (dmasem[key], dmacnt[key], 16)
                    op['sigkey'] = key
                elif i in needed:
                    cnt[op['eng']] += 1
                    op['sig'] = (engsem[op['eng']], cnt[op['eng']], 1)
                else:
                    op['sig'] = None
            grp_last = {}
            for i, op in enumerate(ops):
                if op['dma']:
                    grp_last[op['sigkey']] = i
            if os.environ.get('KDEBUG'):
                print('nsem_dma', len(dmasem), 'nops', len(ops), 'cnt', cnt, 'sems', [str(v) for v in list(dmasem.values())[:3]], [str(v) for v in engsem.values()])
            block = es.enter_context(nc.Block())
            by_eng = {}
            for i, op in enumerate(ops):
                by_eng.setdefault(op['eng'], []).append(i)

            def run(e, idxs, final=False):
                waited = {}
                for i in idxs:
                    op = ops[i]
                    need = {}
                    for d in op['deps']:
                        s = ops[d]['sig']
                        if ops[d]['dma'] and ops[d]['waitall'] and i > grp_last[ops[d]['sigkey']]:
                            s = (s[0], dmacnt[ops[d]['sigkey']], 16)
                        sid = id(s[0])
                        if s[1] > need.get(sid, (None, 0))[1]:
                            need[sid] = (s[0], s[1])
                    for sid, (sem, val) in need.items():
                        if waited.get(sid, 0) < val:
                            e.wait_ge(sem, val)
                            waited[sid] = val
                    ins = op['fn'](e)
                    if op['sig'] is not None:
                        ins.then_inc(op['sig'][0], op['sig'][2])
                if final:
                    for key, sem in dmasem.items():
                        if waited.get(id(sem), 0) < dmacnt[key]:
                            e.wait_ge(sem, dmacnt[key])

            @block.sync
            def _(e):
                run(e, by_eng.get('sp', []), final=True)

            @block.tensor
            def _(e):
                run(e, by_eng.get('pe', []))

            @block.scalar
            def _(e):
                run(e, by_eng.get('act', []))

            @block.vector
            def _(e):
                run(e, by_eng.get('dve', []))

            @block.gpsimd
            def _(e):
                run(e, by_eng.get('pool', []))


def _bf(a):
    return np.ascontiguousarray(a).astype(ml_dtypes.bfloat16)


def host_consts():
    c = {}
    c['ident_bf'] = _bf(np.eye(128))
    c['ident_f'] = np.eye(128, dtype=np.float32)
    c['ones_bf'] = _bf(np.ones((128, 128)))
    selx = np.zeros((2, 128), np.float32)
    selx[0] = 1.0
    c['selx'] = selx
    q = D // 4
    om = 1.0 / (10000.0 ** (np.arange(q, dtype=np.float32) / q))
    pr = np.arange(64, dtype=np.float32)[:, None] * om
    er = np.concatenate([np.sin(pr), np.cos(pr)], -1)
    pe = np.concatenate([np.broadcast_to(er[:, None, :], (64, 64, 512)),
                         np.broadcast_to(er[None, :, :], (64, 64, 512))], -1)
    c['pe'] = np.ascontiguousarray(pe.reshape(4096, D)).astype(np.float32)
    idx = np.arange(128)
    same = (idx[:, None] // 64) == (idx[None, :] // 64)
    c['Uf'] = (same & (idx[:, None] <= idx[None, :])).astype(np.float32)
    c['Ub'] = (same & (idx[:, None] >= idx[None, :])).astype(np.float32)
    c['SCm'] = same.astype(np.float32)
    vf = same & (idx[None, :] >= idx[:, None])
    vb = same & (idx[None, :] <= idx[:, None])
    c['mnegF'] = np.where(vf, 0.0, NEG).astype(np.float32)
    c['mnegB'] = np.where(vb, 0.0, NEG).astype(np.float32)
    sub = (idx[:, None] // 16) == (idx[None, :] // 16)
    c['mdF'] = _bf(sub & (idx[None, :] > idx[:, None]))
    c['mdB'] = _bf(sub & (idx[None, :] < idx[:, None]))
    c['moF'] = _bf(same & ~sub & (idx[None, :] > idx[:, None]))
    c['moB'] = _bf(same & ~sub & (idx[None, :] < idx[:, None]))
    sel = np.zeros((128, 16, 128), np.float32)
    for c_ in range(16):
        sel[c_, c_, :] = 1.0
        sel[16 + c_, c_, :] = 1.0
    c['sel32'] = _bf(sel)
    a64 = 2 * np.pi * np.outer(np.arange(64), np.arange(64)) / 64
    c['c64'] = _bf(np.concatenate([np.cos(a64), -np.sin(a64)], 1))
    kk = np.arange(4096)
    th = 2 * np.pi * np.outer(np.arange(64), kk) / 4096
    Mc = (np.cos(th) / 64.0).reshape(64, 64, 64)
    Ms = (np.sin(th) / 64.0).reshape(64, 64, 64)
    Mc = Mc.transpose(0, 2, 1)
    Ms = Ms.transpose(0, 2, 1)
    rr = np.zeros((64, 64, 2, 2, 64))
    rr[:, :, 0, 0] = Mc
    rr[:, :, 0, 1] = -Ms
    rr[:, :, 1, 0] = Ms
    rr[:, :, 1, 1] = Mc
    c['rr'] = _bf(rr.reshape(64, 64, 2, 128))
    ac = 2 * np.pi * np.outer(np.arange(128), np.arange(128)) / 128
    c['ccsc'] = _bf(np.concatenate([np.cos(ac), np.sin(ac)], 1) / np.sqrt(128.0))
    return c


def build(dbg=(), phases=('dn', 'fo', 'fin'), heads=range(H)):
    nc = bass.Bass("TRN2", target_bir_lowering=False)
    consts = host_consts()
    dr = {}

    def din(name, shape, dt=F32):
        dr[name] = nc.dram_tensor(name, list(shape), dt, kind="ExternalInput").ap()
        return dr[name]

    x_d = din('x', [NLAT, D])
    ctx_d = din('ctx', [NCTX, D])
    cc_d = din('cc', [128, 16])
    wmod_d = din('w_mod', [128, 8, 3 * D])
    bmod_d = din('b_mod2', [2, 3 * D])
    win_d = din('w_in_r', [128, 8, INCOLS])
    wdn_d = din('w_dn_r', [128, 8, D])
    wout_d = din('w_out_r', [128, 8, D])
    wfo_d = din('w_fo_r', [128, 4, D])
    wfm_d = din('w_fmix_r', [128, 4, 128])
    convw_d = din('convw_r', [128, 24, 3])
    alog_d = din('alog_t', [128, 16])
    dtb_d = din('dtb_t', [128, 16])
    normw_d = din('normw_t', [128, 128])
    lng_d = din('lng_t', [128, D])
    lnb_d = din('lnb_t', [128, D])
    for k, v in consts.items():
        din(k, v.shape, BF16 if v.dtype == ml_dtypes.bfloat16 else F32)
    out_d = nc.dram_tensor('out', [NLAT, D], F32, kind="ExternalOutput").ap()
    og_s = nc.dram_tensor('og_s', [H, 128, NLAT], BF16).ap()
    fm_s = nc.dram_tensor('fm_s', [4, 128, NLAT], BF16).ap()
    mg_s = nc.dram_tensor('mg_s', [8, 128, NLAT], BF16).ap()
    dbg_d = {}
    for name, shape, dt in dbg:
        dbg_d[name] = nc.dram_tensor('dbg_' + name, list(shape), dt, kind="ExternalOutput").ap()

    S = Sched(nc)

    def MM(out, lhsT, rhs, r, w, start=True, stop=True):
        S.pe(lambda e: e.matmul(out, lhsT=lhsT, rhs=rhs, start=start, stop=stop), r, w)

    def TR(out, in_, ident, r, w):
        S.pe(lambda e: e.transpose(out=out, in_=in_, identity=ident), r, w)

    def TT(eng, out, in0, in1, op, r, w):
        S.add(eng, lambda e: e.tensor_tensor(out=out, in0=in0, in1=in1, op=op), r, w)

    def TS(eng, out, in0, s1, s2, op0, op1, r, w):
        if s2 is None:
            S.add(eng, lambda e: e.tensor_scalar(out=out, in0=in0, scalar1=s1, scalar2=None, op0=op0), r, w)
        else:
            S.add(eng, lambda e: e.tensor_scalar(out=out, in0=in0, scalar1=s1, scalar2=s2, op0=op0, op1=op1), r, w)

    def STT(eng, out, in0, scalar, in1, op0, op1, r, w):
        S.add(eng, lambda e: e.scalar_tensor_tensor(out=out, in0=in0, scalar=scalar, in1=in1, op0=op0, op1=op1),
              r, w)

    def ACT(out, in_, func, r, w, bias=None, scale=None, accum=None):
        kw = {}
        if bias is not None:
            kw['bias'] = bias
        if scale is not None:
            kw['scale'] = scale
        if accum is not None:
            kw['accum_out'] = accum
        S.act(lambda e: e.activation(out=out, in_=in_, func=func, **kw), r, w)

    def CP(eng, out, in_, r, w):
        if eng == 'act':
            S.act(lambda e: e.activation(out=out, in_=in_, func=AF.Copy), r, w)
        else:
            S.add(eng, lambda e: e.tensor_copy(out=out, in_=in_), r, w)

    def RED(out, in_, r, w):
        S.dve(lambda e: e.tensor_reduce(out=out, in_=in_, axis=AX.X, op=ALU.add), r, w)

    def MEMSET(eng, ap, val, w):
        S.add(eng, lambda e: e.memset(ap, val), [], w)

    with ExitStack() as es:
        def sb(name, shape, dt=F32):
            return es.enter_context(nc.sbuf_tensor('sb_' + name, list(shape), dt))

        ps = [es.enter_context(nc.psum_tensor('ps%d' % i, [128, 512], F32)) for i in range(8)]

        def pk(b, q0=0, nq=4):
            return [('ps', b, q) for q in range(q0, q0 + nq)]

        def psq(b, q, nq=1):
            return ps[b][:, q * 128:(q + nq) * 128]

        cst = {}

        def load_consts(names, alloc, grp):
            for name in names:
                v = consts[name]
                cst[name] = alloc(name, v.shape, BF16 if v.dtype == ml_dtypes.bfloat16 else F32)
                S.dma(cst[name][:], dr[name], [], [name], semkey=grp, waitall=True)

        load_consts(('ident_bf', 'ident_f', 'ones_bf', 'selx'), sb, 'consts')
        ident_bf, ident_f, selx = cst['ident_bf'], cst['ident_f'], cst['selx']
        hT = sb('hT', [128, 8, NTOK], BF16)
        mhalf = sb('mhalf', [128, 32])
        MEMSET('pool', mhalf[:], -0.5, ['mhalf'])
        modT = sb('modT', [128, 48])
        gxb = sb('gxb', [128, D])

        def load_cast(dst, src, a, b, wkey):
            bc = max(1, 2048 // a)
            for b0 in range(0, b, bc):
                bn = min(bc, b - b0)
                S.dma(dst[:, :, b0:b0 + bn], src[:, :, b0:b0 + bn], [], [wkey], eng='pool')

        def hT_keys_for_tokens(t0, t1):
            ks = set()
            for t in range(t0, t1, 128):
                ks.add(('hT', 0 if t < NCTX else 1 + (t - NCTX) // 512))
            return sorted(ks)

        with ExitStack() as es0:
            def sb0(name, shape, dt=F32):
                return es0.enter_context(nc.sbuf_tensor('sb_' + name, list(shape), dt))
            cc = sb0('cc', [128, 16])
            sc = sb0('sc', [128, 16])
            wm = [sb0('wm%d' % i, [128, 3 * D]) for i in range(2)]
            bm = sb0('bm', [2, 3 * D])
            modrow = sb0('modrow', [2, 3 * D])
            S.dma(cc[:], cc_d, [], ['cc'])
            S.dma(bm[:], bmod_d, [], ['bm'])
            ACT(sc[:], cc[:], AF.Silu, ['cc'], ['sc'])
            sc3 = sc[:].rearrange("p (k v) -> p k v", v=2)
            for k in range(8):
                S.dma(wm[k % 2][:], wmod_d[:, k, :], [], ['wm%d' % (k % 2)])
                for j in range(6):
                    MM(ps[j][0:2, :], sc3[:, k, :], wm[k % 2][:, j * 512:(j + 1) * 512],
                       ['sc', 'wm%d' % (k % 2)], pk(j), start=(k == 0), stop=(k == 7))
            for j in range(6):
                TT('dve', modrow[:, j * 512:(j + 1) * 512], ps[j][0:2, :], bm[:, j * 512:(j + 1) * 512], ALU.add,
                   pk(j) + ['bm'], ['modrow'])
            for j in range(24):
                MM(ps[6][:, 2 * j:2 * j + 2], modrow[0:2, j * 128:(j + 1) * 128], ident_f[0:2, 0:2],
                   ['modrow', 'ident_f'], pk(6))
            CP('dve', modT[:], ps[6][:, 0:48], pk(6), ['modT'])
            TS('dve', modT[:, 16:32], modT[:, 16:32], 1.0, None, ALU.add, None, ['modT'], ['modT'])
            for jj in range(2):
                MM(ps[7][:, :], selx[0:2, :], modrow[0:2, 2048 + jj * 512:2048 + (jj + 1) * 512],
                   ['modrow', 'selx'], pk(7))
                CP('act', gxb[:, jj * 512:(jj + 1) * 512], ps[7][:, :], pk(7), ['gxb'])

            NBF = 4
            xt = [sb0('xt%d' % i, [128, D]) for i in range(NBF)]
            pet = [sb0('pet%d' % i, [128, D]) for i in range(NBF)]
            sq = sb0('sq', [128, D], BF16)
            xn = [sb0('xn%d' % i, [128, 4, D], BF16) for i in range(2)]
            st = [sb0('st%d' % i, [128, 8]) for i in range(8)]
            groups = [(1, [(ctx_d, 0), (ctx_d, 1)], 0)]
            for g in range(8):
                groups.append((0, [(x_d, 4 * g + i) for i in range(4)], NCTX + g * 512))
            flat = []
            for gi, (v, tiles, tok0) in enumerate(groups):
                for i, (src, ti) in enumerate(tiles):
                    flat.append((gi, i, v, src, ti, len(tiles), tok0))

            def lnA(n):
                gi, i, v, src, ti, nt, tok0 = flat[n]
                j = n % NBF
                stt, sk, xk = st[n % 8], 'st%d' % (n % 8), 'xt%d' % j
                S.dma(xt[j][:], src[ti * 128:(ti + 1) * 128, :], [], [xk])
                if v == 0:
                    S.dma(pet[j][:], dr['pe'][ti * 128:(ti + 1) * 128, :], [], ['pet%d' % j])
                    TT('dve', xt[j][:], xt[j][:], pet[j][:], ALU.add, [xk, 'pet%d' % j], [xk])
                RED(stt[:, 0:1], xt[j][:], [xk], [sk])
                ACT(sq[:], xt[j][:], AF.Square, [xk], ['sq', sk], accum=stt[:, 1:2])

            def lnB(n):
                stt, sk = st[n % 8], 'st%d' % (n % 8)
                TS('dve', stt[:, 2:3], stt[:, 0:1], 1.0 / D, None, ALU.mult, None, [sk], [sk])
                TT('dve', stt[:, 3:4], stt[:, 2:3], stt[:, 2:3], ALU.mult, [sk], [sk])
                STT('dve', stt[:, 4:5], stt[:, 1:2], 1.0 / D, stt[:, 3:4], ALU.mult, ALU.subtract, [sk], [sk])
                TS('dve', stt[:, 7:8], stt[:, 4:5], EPS, None, ALU.add, None, [sk], [sk])
                TT('pool', stt[:, 5:6], stt[:, 7:8], mhalf[:, 0:1], ALU.pow, [sk, 'mhalf'], [sk])

            def lnC(n):
                gi, i, v, src, ti, nt, tok0 = flat[n]
                j = n % NBF
                gp = gi % 2
                stt, sk, xk = st[n % 8], 'st%d' % (n % 8), 'xt%d' % j
                STT('dve', stt[:, 6:7], stt[:, 2:3], -1.0, stt[:, 5:6], ALU.mult, ALU.mult, [sk], [sk])
                ACT(xn[gp][:, i, :], xt[j][:], AF.Identity, [xk, sk], [('xn', gp, i)], scale=stt[:, 5:6], bias=stt[:, 6:7])
                if i == nt - 1:
                    for kc in range(8):
                        pb = 6 + (kc % 2)
                        pst = ps[pb][:].bitcast(BF16)
                        for ii in range(nt):
                            TR(pst[:, ii * 128:(ii + 1) * 128], xn[gp][:, ii, kc * 128:(kc + 1) * 128], ident_bf[:],
                               [('xn', gp, ii), 'ident_bf'], pk(pb))
                        dst = hT[:, kc, tok0:tok0 + nt * 128]
                        sc_ap = modT[:, 2 * (8 + kc) + v:2 * (8 + kc) + v + 1]
                        sh_ap = modT[:, 2 * kc + v:2 * kc + v + 1]
                        if kc % 2 == 0:
                            ACT(dst, pst[:, 0:nt * 128], AF.Identity, pk(pb) + ['modT'], [('hT', gi)], scale=sc_ap, bias=sh_ap)
                        else:
                            TS('dve', dst, pst[:, 0:nt * 128], sc_ap, sh_ap, ALU.mult, ALU.add, pk(pb) + ['modT'], [('hT', gi)])

            NT0 = len(flat)
            for n in range(NT0 + 2):
                if n < NT0:
                    lnA(n)
                if 0 <= n - 1 < NT0:
                    lnB(n - 1)
                if 0 <= n - 2 < NT0:
                    lnC(n - 2)
        S.barrier()
        ALLHT = [('hT', g) for g in range(9)]
        if 'hT' in dbg_d:
            for kc in range(8):
                S.dma(dbg_d['hT'][kc], hT[:, kc, :], ALLHT, ['dbg_hT'])

        if 'dn' in phases:
            with ExitStack() as es2:
                def sb2(name, shape, dt=F32):
                    return es2.enter_context(nc.sbuf_tensor('sb_' + name, list(shape), dt))
                NB = NTOK // 128
                load_consts(('Uf', 'Ub', 'SCm', 'mnegF', 'mnegB', 'mdF', 'mdB', 'moF', 'moB'), sb2, 'consts_dn')
                wbd = sb2('wbd', [128, 8, 32], BF16)
                load_cast(wbd[:], win_d[:, :, OFF[4]:OFF[4] + 32], 8, 32, 'wbd')
                beta_tm = sb2('beta_tm', [128, NB, 16])
                gc_tm = sb2('gc_tm', [128, NB, 16])
                egc_tm = sb2('egc_tm', [128, NB, 16])
                egl_tm = sb2('egl_tm', [128, NB, 16])
                gcT = sb2('gcT', [128, NB, 128], BF16)
                selh = sb2('selh', [128, 2, 128], BF16)
                MEMSET('pool', gcT[:], 0.0, ['gcT'])
                es2p = ExitStack()
                def sb2p(name, shape, dt=F32):
                    return es2p.enter_context(nc.sbuf_tensor('sb_' + name, list(shape), dt))
                raw_tm = sb2p('raw_tm', [128, NB, 32])
                gcb_tm = sb2p('gcb_tm', [128, NB, 16], BF16)
                gch_tm = sb2p('gch_tm', [128, NB, 16])
                gcl_tm = sb2p('gcl_tm', [128, NB, 16])
                gcHL_tm = sb2p('gcHL_tm', [128, NB, 32], BF16)
                g_tm = sb2p('g_tm', [128, NB, 16])
                tot_tm = sb2p('tot_tm', [128, NB, 16])
                tmp_tm = sb2p('tmp_tm', [128, NB, 16])
                alog = sb2p('alog', [128, 16])
                dtb = sb2p('dtb', [128, 16])
                nega = sb2p('nega', [128, 16])
                S.dma(alog[:], alog_d, [], ['alog'])
                S.dma(dtb[:], dtb_d, [], ['dtb'])
                ACT(nega[:], alog[:], AF.Exp, ['alog'], ['nega'])
                TS('dve', nega[:], nega[:], -1.0, None, ALU.mult, None, ['nega'], ['nega'])
                for b0 in range(0, NB, 16):
                    nb_ = min(16, NB - b0)
                    bank = 6 + (b0 // 16) % 2
                    for bi in range(nb_):
                        blk = b0 + bi
                        for kc in range(8):
                            MM(ps[bank][:, bi * 32:(bi + 1) * 32], hT[:, kc, blk * 128:(blk + 1) * 128], wbd[:, kc, :],
                               hT_keys_for_tokens(blk * 128, blk * 128 + 128) + ['wbd'], pk(bank),
                               start=(kc == 0), stop=(kc == 7))
                    CP('act', raw_tm[:, b0:b0 + nb_, :],
                       ps[bank][:, 0:nb_ * 32].rearrange("p (b c) -> p b c", c=32), pk(bank), ['raw_tm'])
                ACT(beta_tm[:], raw_tm[:, :, 0:16], AF.Sigmoid, ['raw_tm'], ['beta_tm'])
                TT('dve', tmp_tm[:], raw_tm[:, :, 16:32], dtb[:].unsqueeze(1).to_broadcast([128, NB, 16]), ALU.add,
                   ['raw_tm', 'dtb'], ['tmp_tm'])
                ACT(tmp_tm[:], tmp_tm[:], AF.Exp, ['tmp_tm'], ['tmp_tm'])
                ACT(tmp_tm[:], tmp_tm[:], AF.Ln, ['tmp_tm'], ['tmp_tm'], bias=1.0)
                TT('dve', g_tm[:], tmp_tm[:], nega[:].unsqueeze(1).to_broadcast([128, NB, 16]), ALU.mult,
                   ['tmp_tm', 'nega'], ['g_tm'])
                for half in range(2):
                    b0 = half * 17
                    for bi in range(17):
                        blk = b0 + bi
                        MM(ps[6][:, bi * 16:bi * 16 + 8], cst['Uf'][:], g_tm[:, blk, 0:8], ['g_tm', 'Uf'], pk(6))
                        MM(ps[6][:, bi * 16 + 8:bi * 16 + 16], cst['Ub'][:], g_tm[:, blk, 8:16], ['g_tm', 'Ub'], pk(6))
                        MM(ps[7][:, bi * 16:bi * 16 + 16], cst['SCm'][:], g_tm[:, blk, :], ['g_tm', 'SCm'], pk(7))
                    CP('dve', gc_tm[:, b0:b0 + 17, :], ps[6][:, 0:272].rearrange("p (b c) -> p b c", c=16), pk(6),
                       ['gc_tm'])
                    CP('dve', tot_tm[:, b0:b0 + 17, :], ps[7][:, 0:272].rearrange("p (b c) -> p b c", c=16), pk(7),
                       ['tot_tm'])
                ACT(egc_tm[:], gc_tm[:], AF.Exp, ['gc_tm'], ['egc_tm'])
                CP('dve', gcb_tm[:], gc_tm[:], ['gc_tm'], ['gcb_tm'])
                CP('dve', gch_tm[:], gcb_tm[:], ['gcb_tm'], ['gch_tm'])
                TT('dve', gcl_tm[:], gc_tm[:], gch_tm[:], ALU.subtract, ['gc_tm', 'gch_tm'], ['gcl_tm'])
                CP('dve', gcHL_tm[:, :, 0:16], gcb_tm[:], ['gcb_tm'], ['gcHL_tm'])
                CP('dve', gcHL_tm[:, :, 16:32], gcl_tm[:], ['gcl_tm'], ['gcHL_tm'])
                for b0 in range(0, NB, 8):
                    nb_ = min(8, NB - b0)
                    pst = ps[6][:].bitcast(BF16)
                    for bi in range(nb_):
                        TR(pst[0:32, bi * 128:(bi + 1) * 128], gcHL_tm[:, b0 + bi, :], ident_bf[:], ['gcHL_tm', 'ident_bf'], pk(6))
                    CP('dve', gcT[0:32, b0:b0 + nb_, :], pst[0:32, 0:nb_ * 128].rearrange("p (b t) -> p b t", t=128),
                       pk(6), ['gcT'])
                TT('dve', tmp_tm[:], tot_tm[:], gc_tm[:], ALU.subtract, ['tot_tm', 'gc_tm'], ['tmp_tm'])
                ACT(egl_tm[:], tmp_tm[:], AF.Exp, ['tmp_tm'], ['egl_tm'])
                if 'gcT' in dbg_d:
                    S.dma(dbg_d['gcT'], gcT[:], ['gcT'], ['dbg_gcT'])
                if 'tm' in dbg_d:
                    S.dma(dbg_d['tm'][0], beta_tm[:], ['beta_tm'], ['dbg_tm0'])
                    S.dma(dbg_d['tm'][1], gc_tm[:], ['gc_tm'], ['dbg_tm1'])
                    S.dma(dbg_d['tm'][2], egl_tm[:], ['egl_tm'], ['dbg_tm2'])
                es2p.close()
                S.barrier()

                X0 = sb2('X0', [128, 4360], BF16)
                X1 = sb2('X1', [128, 4360], BF16)
                X2 = sb2('X2', [128, 4360], BF16)
                QT = sb2('QT', [128, NTOK], BF16)
                KT = sb2('KT', [128, NTOK], BF16)
                Ktok = sb2('Ktok', [128, NB, 128], BF16)
                Vtok = sb2('Vtok', [128, NB, 128], BF16)
                Obuf = sb2('Obuf', [128, 32, 128], BF16)
                wqkv = sb2('wqkv', [128, 8, 384], BF16)
                wdg = sb2('wdg', [128, 8, 128], BF16)
                convw = sb2('convw', [128, 24, 3])
                normw = sb2('normw', [128, 128])
                sqb = sb2('sqb', [128, 512], BF16)
                sqb2 = [sqb, sb2('sqb1', [128, 512], BF16)]
                Dg = [sb2('Dg%d' % j, [128, 128], BF16) for j in range(3)]
                lnt = [sb2('lnt0', [128, 512]), sb2('lnt1', [128, 512], BF16)]
                ssq = sb2('ssq', [128, 32])
                rstd_o = sb2('rstd_o', [128, 32])
                Onb = [sb2('Onb%d' % i, [128, 128], BF16) for i in range(4)]
                S.dma(convw[:], convw_d, [], ['convw'])
                S.dma(normw[:], normw_d, [], ['normw'])
                MEMSET('pool', X0[:], 0.0, ['X0'])
                Sf32 = [sb2('Sf32_%d' % d, [128, 128]) for d in range(2)]
                Sbf = [sb2('Sbf_%d' % d, [128, 128], BF16) for d in range(2)]
                vnew = [[sb2('vnew_%d%d' % (d, c_), [128, 128], BF16) for c_ in range(2)] for d in range(2)]
                LOOK = 3
                carve = [[X1, 0, 4360], [X2, 0, 4360], [X0, 260, 4352]]

                def newbuf(nm, w, dt):
                    ne = w * (2 if dt == F32 else 1)
                    for reg in carve:
                        a = (reg[1] + 1) // 2 * 2
                        if a + ne <= reg[2]:
                            reg[1] = a + ne
                            v = reg[0][:, a:a + ne]
                            return v.bitcast(F32) if dt == F32 else v
                    return sb2(nm, [128, w], dt)[:]

                INTERNAL = ('Gdh', 'Gdl', 'X', 'N', 'Bns', 'Bd', 'Bo', 'LLo', 'LB0', 'LB1', 'Q0', 'Q1', 'nMT',
                            'X0', 'Xit0', 'Xit1', 'Kg')
                ISET = {}
                PSET = {}
                for d in range(2):
                    for i in range(LOOK):
                        nb = lambda nm, w=128, dt=BF16: newbuf('%s_%d%d' % (nm, d, i), w, dt)
                        ISET[d, i] = dict(X=nb('X', 128, F32), N=nb('N'), Bns=nb('Bns'),
                                          Bd=nb('Bd'), Bo=nb('Bo'), LLo=nb('LLo', 256), LB=[nb('LB0', 256), nb('LB1', 256)],
                                          Q=[nb('Q0'), nb('Q1')], nMT=nb('nMT'), X0=nb('X0', 256),
                                          Xit=[nb('Xit0', 256), nb('Xit1', 256)], Kg=nb('Kg'))
                    for i in range(LOOK + 1):
                        nb = lambda nm, w=128, dt=BF16: newbuf('%s_%d%dp' % (nm, d, i), w, dt)
                        PSET[d, i] = dict(EG=nb('EG', 128, F32), qgT=nb('qgT'), QKT=nb('QKT'), Ub=nb('Ub'), nwT=nb('nwT'),
                                          kd=nb('kd'))

                def PQ(d, name, odd=0):
                    m = {'G': (0, 0), 'K': (0, 1), 'Qp': (0, 3), 'L': (1, 0), 'LB': (1, 1), 'X': (1, 1), 'M': (1, 3),
                         'W': (1, 3), 'v': (2, 0), 'o': (2, 1), 'dS': (2, 2)}[name]
                    if m[0] == 1 and odd:
                        return (6 + d, m[1])
                    return (3 * d + m[0], m[1])

                def chunk_of_block(blk):
                    return 0 if blk < 2 else 1 + (blk - 2) // 4

                hl = list(heads)

                def load_qkv_w(hh):
                    for wi in range(3):
                        c0 = OFF[2] + wi * 1024 + hh * 128
                        load_cast(wqkv[:, :, wi * 128:(wi + 1) * 128], win_d[:, :, c0:c0 + 128], 8, 128, ('wqkv', wi))

                for h in heads:
                    if h == hl[0]:
                        load_qkv_w(h)
                    for d_ in range(2):
                        S.dma(selh[:, d_, :], dr['sel32'][:, d_ * 8 + h, :], [], ['selh'])
                    c0 = OFF[3] + h * 128
                    load_cast(wdg[:], win_d[:, :, c0:c0 + 128], 8, 128, 'wdg')
                    def blocks_of_chunk(c):
                        return [0, 1] if c == 0 else list(range(2 + 4 * (c - 1), 6 + 4 * (c - 1)))

                    for wi in (1, 2, 0):
                        cw = convw[:, wi * 8 + h, :]
                        dstT = KT if wi == 1 else QT
                        dk = 'KT' if wi == 1 else 'QT'
                        for j in range(3):
                            TS('dve', Dg[j][:], ident_bf[:], cw[:, j:j + 1], None, ALU.mult, None, ['ident_bf', 'convw'],
                               [('Dg', j)])

                        def geom(c):
                            t0 = 0 if c == 0 else NCTX + (c - 1) * 512
                            n = 256 if c == 0 else 512
                            col0 = 1 if c == 0 else 259 + (c - 1) * 512
                            return t0, n, col0

                        def stA(c):
                            t0, n, col0 = geom(c)
                            bank = 6 + c % 2
                            for kc in range(8):
                                MM(ps[bank][:, 0:n], wqkv[:, kc, wi * 128:(wi + 1) * 128], hT[:, kc, t0:t0 + n],
                                   [('hT', c), ('wqkv', wi)], pk(bank), start=(kc == 0), stop=(kc == 7))
                            CP('dve', X0[:, col0:col0 + n], ps[bank][:, 0:n], pk(bank), [('X0', c)])

                        def stB(c):
                            t0, n, col0 = geom(c)
                            nb_keys = [('X0', cc) for cc in (c - 1, c, c + 1) if 0 <= cc <= 8] + ['X0']
                            bank = c % 2
                            for j in range(3):
                                MM(ps[bank][:, 0:n], Dg[j][:], X0[:, col0 + j - 1:col0 + j - 1 + n], nb_keys + [('Dg', j)],
                                   pk(bank), start=(j == 0), stop=(j == 2))
                            ACT(X2[:, t0:t0 + n], ps[bank][:, 0:n], AF.Silu, pk(bank), [('X2', c)])

                        def stC(c):
                            t0, n, col0 = geom(c)
                            if wi == 2:
                                blks = blocks_of_chunk(c)
                                bank = 4 + c % 2
                                pst = ps[bank][:].bitcast(BF16)
                                for bi_, blk in enumerate(blks):
                                    TR(pst[:, bi_ * 128:(bi_ + 1) * 128], X2[:, blk * 128:(blk + 1) * 128], ident_bf[:],
                                       [('X2', c), 'ident_bf'], pk(bank))
                                CP('dve' if c % 2 == 0 else 'act', Vtok[:, blks[0]:blks[0] + len(blks), :],
                                   pst[:, 0:len(blks) * 128].rearrange("p (b c) -> p b c", c=128), pk(bank),
                                   [('Vtok', blk) for blk in blks])
                            else:
                                bank = 2 + c % 2
                                sq_ = sqb2[c % 2]
                                TT('dve', sq_[:, 0:n], X2[:, t0:t0 + n], X2[:, t0:t0 + n], ALU.mult, [('X2', c)], [('sqb', c % 2)])
                                MM(ps[bank][:, 0:n], cst['ones_bf'][:], sq_[:, 0:n], [('sqb', c % 2), 'ones_bf'], pk(bank))

                        def stD(c):
                            t0, n, col0 = geom(c)
                            if wi == 2:
                                return
                            bank = 2 + c % 2
                            ACT(lnt[0][:, 0:n], ps[bank][:, 0:n], AF.Ln, pk(bank), ['lnt0'], bias=EPS)
                            ACT(lnt[1][:, 0:n], lnt[0][:, 0:n], AF.Exp, ['lnt0'], ['lnt1'], scale=-0.5,
                                bias=(-0.5 * float(np.log(128.0)) if wi == 0 else 0.0))
                            TT('dve', dstT[:, t0:t0 + n], X2[:, t0:t0 + n], lnt[1][:, 0:n], ALU.mult,
                               [('X2', c), 'lnt1'], [(dk, c)])
                            if wi == 1:
                                blks = blocks_of_chunk(c)
                                bank = 4 + c % 2
                                pst = ps[bank][:].bitcast(BF16)
                                for bi_, blk in enumerate(blks):
                                    TR(pst[:, bi_ * 128:(bi_ + 1) * 128], KT[:, blk * 128:(blk + 1) * 128], ident_bf[:],
                                       [('KT', c), 'ident_bf'], pk(bank))
                                CP('dve' if c % 2 == 0 else 'act', Ktok[:, blks[0]:blks[0] + len(blks), :],
                                   pst[:, 0:len(blks) * 128].rearrange("p (b c) -> p b c", c=128), pk(bank),
                                   [('Ktok', blk) for blk in blks])

                        for i in range(9 + 4):
                            if i < 9:
                                stA(i)
                            if 0 <= i - 2 < 9:
                                stB(i - 2)
                            if 0 <= i - 3 < 9:
                                stC(i - 3)
                            if 0 <= i - 4 < 9:
                                stD(i - 4)
                    if 'qkv' in dbg_d and h == list(heads)[0]:
                        S.dma(dbg_d['qkv'][0], QT[:], [('QT', c) for c in range(9)], ['dbg_q'])
                        S.dma(dbg_d['qkv'][1], KT[:], [('KT', c) for c in range(9)], ['dbg_k'])
                        S.dma(dbg_d['vtok'], Vtok[:], [('Vtok', b) for b in range(NB)], ['dbg_v'])

                    if 'selh' in dbg_d:
                        S.dma(dbg_d['selh'], selh[:], ['selh'], ['dbg_selh'])
                    if hl.index(h) + 1 < len(hl):
                        load_qkv_w(hl[hl.index(h) + 1])
                    S.barrier()
                    for d in range(2):
                        MEMSET('pool', Sf32[d][:], 0.0, [('S', d)])
                        MEMSET('pool', Sbf[d][:], 0.0, [('Sb', d)])
                        for c_ in range(2):
                            MEMSET('pool', vnew[d][c_][:], 0.0, [('vn', d, c_)])
                    orderC = [list(range(68)), [3, 2, 1, 0] + list(range(67, 3, -1))]
                    orderB = []
                    for d in range(2):
                        ob = []
                        for c in orderC[d]:
                            if c // 2 not in ob:
                                ob.append(c // 2)
                        orderB.append(ob)
                    owritten = set()

                    def setup_micro(d, bi):
                        blk = orderB[d][bi]
                        I = ISET[d, bi % LOOK]
                        P = PSET[d, bi % (LOOK + 1)]
                        odd = bi % 2
                        col = d * 8 + h
                        kq = [('QT', chunk_of_block(blk)), ('KT', chunk_of_block(blk))]
                        tks = slice(blk * 128, (blk + 1) * 128)
                        bG, qG = PQ(d, 'G')
                        bK, qK = PQ(d, 'K')
                        bQ, qQ = PQ(d, 'Qp')
                        bL, qL = PQ(d, 'L', odd)
                        bLB, qLB = PQ(d, 'LB', odd)
                        bM, qM = PQ(d, 'M', odd)
                        ik = lambda nm: ('i', nm, d, bi % LOOK)
                        pkk = lambda nm: ('p', nm, d, bi % (LOOK + 1))
                        gcc = gc_tm[:, blk, col:col + 1]
                        bcol = beta_tm[:, blk, col:col + 1]
                        mneg = cst['mnegF'] if d == 0 else cst['mnegB']
                        md = cst['mdF'] if d == 0 else cst['mdB']
                        mo = cst['moF'] if d == 0 else cst['moB']
                        pstL = ps[bL][:].bitcast(BF16)[:, qL * 256:qL * 256 + 256]
                        pstW = ps[bM][:].bitcast(BF16)[:, qM * 256:qM * 256 + 128]
                        L0, Lo = I['LLo'][:, 0:128], I['LLo'][:, 128:256]

                        def m0():
                            ACT(P['kd'], Ktok[:, blk, :], AF.Copy, [('Ktok', blk), 'egl_tm'], [pkk('kd')],
                                scale=egl_tm[:, blk, col:col + 1])
                            TS('pool', I['Kg'], Ktok[:, blk, :], egc_tm[:, blk, col:col + 1], None, ALU.mult, None,
                               [('Ktok', blk), 'egc_tm'], [ik('Kg')])

                        def m1():
                            MM(psq(bG, qG), selh[:, d, :], gcT[:, blk, :], ['selh', 'gcT'], pk(bG, qG, 1))
                            MM(psq(bK, qK), KT[:, tks], QT[:, tks], kq, pk(bK, qK, 1))
                            MM(psq(bK, qK + 1), KT[:, tks], KT[:, tks], kq, pk(bK, qK + 1, 1))

                        def m2():
                            STT('dve', I['X'], psq(bG, qG), gcc, mneg[:], ALU.subtract, ALU.add,
                                pk(bG, qG, 1) + ['gc_tm', 'mnegF', 'mnegB'], [ik('X')])
                            ACT(P['EG'], psq(bG, qG), AF.Exp, pk(bG, qG, 1), [pkk('EG')])

                        def m3():
                            if 'EG' in dbg_d and d == 0 and bi == 0 and h == hl[0]:
                                S.dma(dbg_d['EG'], P['EG'], [pkk('EG')], ['dbg_EG'])
                                S.dma(dbg_d['Xd'], I['X'], [ik('X')], ['dbg_Xd'])
                            ACT(I['N'], I['X'], AF.Exp, [ik('X')], [ik('N')])
                            TT('dve', P['qgT'], QT[:, tks], P['EG'], ALU.mult, kq + [pkk('EG')], [pkk('qgT')])

                        def m4():
                            STT('dve', I['Bns'], psq(bK, qK + 1), bcol, I['N'], ALU.mult, ALU.mult,
                                pk(bK, qK + 1, 1) + [ik('N'), 'beta_tm'], [ik('Bns')])
                            TT('dve', P['QKT'], psq(bK, qK), I['N'], ALU.mult, pk(bK, qK, 1) + [ik('N')], [pkk('QKT')])

                        def m5():
                            TT('dve', I['Bd'], I['Bns'], md[:], ALU.mult, [ik('Bns'), 'mdF', 'mdB'], [ik('Bd')])
                            TT('pool', I['Bo'], I['Bns'], mo[:], ALU.mult, [ik('Bns'), 'moF', 'moB'], [ik('Bo')])

                        def m6():
                            TT('dve', I['Q'][0], ident_bf[:], I['Bd'], ALU.subtract, [ik('Bd'), 'ident_bf'], [ik('Q0')])
                            TR(pstL[:, 0:128], I['Bd'], ident_bf[:], [ik('Bd'), 'ident_bf'], pk(bL, qL, 1))
                            TR(pstL[:, 128:256], I['Bo'], ident_bf[:], [ik('Bo'), 'ident_bf'], pk(bL, qL, 1))

                        def m7():
                            CP('act', I['LLo'], pstL, pk(bL, qL, 1), [ik('LLo')])

                        def levA(k):
                            def f():
                                if k == 1:
                                    Lp, Bp, kLp, kBp = L0, I['Bd'], ik('LLo'), ik('Bd')
                                else:
                                    prev = I['LB'][(k - 1) % 2]
                                    Lp, Bp = prev[:, 0:128], prev[:, 128:256]
                                    kLp = kBp = ik('LB%d' % ((k - 1) % 2))
                                MM(psq(bLB, qLB), Bp, Lp, [kLp, kBp], pk(bLB, qLB, 1))
                                if k < 3:
                                    MM(psq(bLB, qLB + 1), Lp, Bp, [kLp, kBp], pk(bLB, qLB + 1, 1))
                            return f

                        def levB(k):
                            def f():
                                cur = I['LB'][k % 2]
                                if k < 3:
                                    CP('act', cur[:, 0:256], psq(bLB, qLB, 2), pk(bLB, qLB, 2),
                                       [ik('LB%d' % (k % 2))])
                                else:
                                    CP('act', cur[:, 0:128], psq(bLB, qLB), pk(bLB, qLB, 1), [ik('LB%d' % (k % 2))])
                            return f

                        def levC(k):
                            def f():
                                cur = I['LB'][k % 2]
                                MM(psq(bQ, qQ), cur[:, 0:128], I['Q'][(k - 1) % 2],
                                   [ik('LB%d' % (k % 2)), ik('Q%d' % ((k - 1) % 2))], pk(bQ, qQ, 1))
                            return f

                        def levD(k):
                            def f():
                                TT('dve', I['Q'][k % 2], psq(bQ, qQ), I['Q'][(k - 1) % 2], ALU.add,
                                   pk(bQ, qQ, 1) + [ik('Q%d' % ((k - 1) % 2))], [ik('Q%d' % (k % 2))])
                            return f

                        def m20():
                            TdT = I['Q'][1]
                            MM(psq(bM, qM), Lo, TdT, [ik('LLo'), ik('Q1')], pk(bM, qM, 1))
                            MM(psq(bLB, qLB), TdT, Vtok[:, blk, :], [ik('Q1'), ('Vtok', blk)], pk(bLB, qLB, 1))
                            MM(psq(bLB, qLB + 1), TdT, I['Kg'], [ik('Q1'), ik('Kg')], pk(bLB, qLB + 1, 1))

                        def m21():
                            ACT(I['nMT'], psq(bM, qM), AF.Copy, pk(bM, qM, 1), [ik('nMT')], scale=-1.0)
                            CP('act', I['X0'], psq(bLB, qLB, 2), pk(bLB, qLB, 2), [ik('X0')])

                        def itA(n):
                            def f():
                                prev = I['X0'] if n == 1 else I['Xit'][(n - 1) % 2]
                                kprev = ik('X0') if n == 1 else ik('Xit%d' % ((n - 1) % 2))
                                MM(psq(bLB, qLB, 2), I['nMT'], prev[:, 0:256], [ik('nMT'), kprev], pk(bLB, qLB, 2))
                            return f

                        def itB(n):
                            def f():
                                TT('dve', I['Xit'][n % 2], psq(bLB, qLB, 2), I['X0'], ALU.add,
                                   pk(bLB, qLB, 2) + [ik('X0')], [ik('Xit%d' % (n % 2))])
                            return f

                        def m28():
                            X3 = I['Xit'][1]
                            ACT(P['Ub'], X3[:, 0:128], AF.Copy, [ik('Xit1'), 'beta_tm'], [pkk('Ub')], scale=bcol)
                            TR(pstW, X3[:, 128:256], ident_bf[:], [ik('Xit1'), 'ident_bf'], pk(bM, qM, 1))

                        def m29():
                            ACT(P['nwT'], pstW, AF.Copy, pk(bM, qM, 1), [pkk('nwT')], scale=-1.0)
                        return [m0, m1, m2, m3, m4, m5, m6, m7,
                                levA(1), levB(1), levC(1), levD(1), levA(2), levB(2), levC(2), levD(2),
                                levA(3), levB(3), levC(3), levD(3), m20, m21,
                                itA(1), itB(1), itA(2), itB(2), itA(3), itB(3), m28, m29]

                    def step_micro(d, s_):
                        c = orderC[d][s_]
                        blk, ci = c // 2, c % 2
                        bi = orderB[d].index(blk)
                        P = PSET[d, bi % (LOOK + 1)]
                        col = d * 8 + h
                        cs = slice(ci * 64, ci * 64 + 64)
                        bv, qv = PQ(d, 'v')
                        bo, qo = PQ(d, 'o')
                        bs, qs = PQ(d, 'dS')
                        pkk = lambda nm: ('p', nm, d, bi % (LOOK + 1))

                        def u0():
                            MM(psq(bv, qv), P['nwT'], Sbf[d][:], [pkk('nwT'), ('Sb', d)], pk(bv, qv, 1))

                        def u1():
                            STT('dve', vnew[d][ci][cs, :], ps[bv][cs, qv * 128:(qv + 1) * 128],
                                beta_tm[cs, blk, col:col + 1], P['Ub'][cs, :], ALU.mult, ALU.add,
                                pk(bv, qv, 1) + ['beta_tm', pkk('Ub')], [('vn', d, ci)])

                        def u2():
                            MM(psq(bs, qs), P['kd'], vnew[d][ci][:], [pkk('kd'), ('vn', d, ci)], pk(bs, qs, 1))
                            if blk >= 2:
                                MM(psq(bo, qo), P['qgT'], Sbf[d][:], [pkk('qgT'), ('Sb', d)], pk(bo, qo, 1),
                                   start=True, stop=False)
                                MM(psq(bo, qo), P['QKT'], vnew[d][ci][:], [pkk('QKT'), ('vn', d, ci)], pk(bo, qo, 1),
                                   start=False, stop=True)

                        def u3():
                            lastcol = ci * 64 + (63 if d == 0 else 0)
                            STT('dve', Sf32[d][:], Sf32[d][:], P['EG'][:, lastcol:lastcol + 1], psq(bs, qs), ALU.mult,
                                ALU.add, [('S', d), pkk('EG')] + pk(bs, qs, 1), [('S', d)])
                            if blk >= 2:
                                okey = ('O', blk, ci)
                                src = ps[bo][cs, qo * 128:(qo + 1) * 128]
                                if okey not in owritten:
                                    owritten.add(okey)
                                    CP('act', Obuf[cs, blk - 2, :], src, pk(bo, qo, 1), [okey])
                                else:
                                    TT('dve', Obuf[cs, blk - 2, :], src, Obuf[cs, blk - 2, :], ALU.add,
                                       pk(bo, qo, 1) + [okey], [okey])

                        def u4():
                            CP('act', Sbf[d][:], Sf32[d][:], [('S', d)], [('Sb', d)])
                        return [u0, u1, u2, u3, u4]

                    NBK = len(orderB[0])
                    smic = {}
                    for g in range(-10 * LOOK, 5 * 68):
                        if g >= 0:
                            s_, u = g // 5, g % 5
                            for d in range(2):
                                if u == 0:
                                    smic[d] = step_micro(d, s_)
                                smic[d][u]()
                        for bi in range(NBK):
                            g0 = 10 * (bi - LOOK)
                            if g0 <= g < g0 + 30:
                                m = g - g0
                                for d in range(2):
                                    key = ('setup', d, bi)
                                    if key not in smic:
                                        smic[key] = setup_micro(d, bi)
                                    smic[key][m]()
                    S.barrier()
                    if 'O' in dbg_d and h == list(heads)[0]:
                        S.dma(dbg_d['O'], Obuf[:], [('O', b, ci) for b in range(2, 34) for ci in range(2)], ['dbg_O'])
                    if 'Sfin' in dbg_d and h == list(heads)[0]:
                        S.dma(dbg_d['Sfin'][0], Sf32[0][:], [('S', 0)], ['dbg_S0'])
                        S.dma(dbg_d['Sfin'][1], Sf32[1][:], [('S', 1)], ['dbg_S1'])

                    OK_ALL = lambda b: [('O', b + 2, 0), ('O', b + 2, 1)]
                    for b in range(32):
                        ACT(sqb[:, 0:128], Obuf[:, b, :], AF.Square, OK_ALL(b), ['sqb', 'ssq'], accum=ssq[:, b:b + 1])
                    TS('dve', ssq[:], ssq[:], 1.0 / 128, EPS, ALU.mult, ALU.add, ['ssq'], ['ssq'])
                    TT('pool', rstd_o[:], ssq[:], mhalf[:, 0:32], ALU.pow, ['ssq', 'mhalf'], ['rstd_o'])
                    for tc in range(8):
                        bank = 6 + tc % 2
                        t0 = NCTX + tc * 512
                        for kc in range(8):
                            MM(ps[bank][:, :], wdg[:, kc, :], hT[:, kc, t0:t0 + 512], [('hT', 1 + tc), 'wdg'], pk(bank),
                               start=(kc == 0), stop=(kc == 7))
                        ACT(X2[:, tc * 512:(tc + 1) * 512], ps[bank][:, :], AF.Silu, pk(bank), ['X2'])
                    for tc in range(8):
                        bank = 6 + tc % 2
                        pst = ps[bank][:].bitcast(BF16)
                        for bi in range(4):
                            b = tc * 4 + bi
                            STT('dve', Onb[bi][:], Obuf[:, b, :], rstd_o[:, b:b + 1], normw[:], ALU.mult, ALU.mult,
                                OK_ALL(b) + ['rstd_o', 'normw'], [('Onb', bi)])
                            TR(pst[:, bi * 128:(bi + 1) * 128], Onb[bi][:], ident_bf[:], [('Onb', bi), 'ident_bf'], pk(bank))
                        TT('dve', X1[:, tc * 512:(tc + 1) * 512], pst[:, 0:512], X2[:, tc * 512:(tc + 1) * 512], ALU.mult,
                           pk(bank) + ['X2'], ['X1'])
                    S.dma(og_s[h], X1[:, 0:NLAT], ['X1'], [('og_s', h)], semkey='og_s', waitall=True, eng='pool')
                    S.barrier()
            S.barrier()
        if 'fo' in phases:
            with ExitStack() as es3:
                def sb3(name, shape, dt=F32):
                    return es3.enter_context(nc.sbuf_tensor('sb_' + name, list(shape), dt))
                load_consts(('ccsc',), sb3, 'consts_fo')
                c64 = sb3('c64', [64, 128], BF16)
                rr = sb3('rr', [64, 64, 2, 128], BF16)
                S.dma(c64[:], dr['c64'], [], ['c64'])
                S.dma(rr[:], dr['rr'], [], ['rr'])
                wfv = sb3('wfv', [128, 8, 512], BF16)
                wfg = sb3('wfg', [128, 8, 512], BF16)
                wfm = sb3('wfm', [128, 4, 128], BF16)
                load_cast(wfv[:], win_d[:, :, OFF[0]:OFF[0] + 512], 8, 512, 'wfv')
                load_cast(wfg[:], win_d[:, :, OFF[1]:OFF[1] + 512], 8, 512, 'wfg')
                load_cast(wfm[:], wfm_d, 4, 128, 'wfm')
                uA = sb3('uA', [64, 64, 256], BF16)
                Y = sb3('Y', [64, 128, 2, 64], BF16)
                PT = sb3('PT', [128, 2, NLAT], BF16)
                G = sb3('G', [128, 256], BF16)
                SG = [sb3('SG0', [128, 512], BF16)] * 2
                FMc = [sb3('FMc0', [128, 512], BF16)] * 2
                LAT = [('hT', 1 + tc) for tc in range(8)]
                for g in range(4):
                    if g % 2 == 0:
                        for n0 in range(0, 64, 2):
                            bank = (n0 // 2) % 2
                            for a in range(2):
                                n2 = n0 + a
                                for kc in range(8):
                                    MM(ps[bank][0:64, a * 256:(a + 1) * 256], hT[:, kc, NCTX + n2:NCTX + NLAT:64],
                                       wfv[:, kc, g * 128:g * 128 + 256], LAT + ['wfv'], pk(bank),
                                       start=(kc == 0), stop=(kc == 7))
                            CP('act' if bank == 0 else 'dve', uA[:, n0:n0 + 2, :],
                               ps[bank][0:64, :].rearrange("p (a c) -> p a c", c=256), pk(bank), ['uA'])
                    go = (g % 2) * 128
                    for c0 in range(0, 128, 4):
                        bank = 2 + (c0 // 4) % 2
                        for a in range(4):
                            MM(ps[bank][0:64, a * 128:(a + 1) * 128], uA[:, :, go + c0 + a], c64[:, :], ['uA', 'c64'], pk(bank))
                        CP('act' if bank == 2 else 'dve', Y[:, c0:c0 + 4, :, :],
                           ps[bank][0:64, :].rearrange("p (c r k) -> p c r k", c=4, r=2), pk(bank), ['Y'])
                    if g == 0:
                        if 'uA' in dbg_d:
                            S.dma(dbg_d['uA'], uA[:, :, 0:128], ['uA'], ['dbg_uA'])
                    PTv = PT[:].rearrange("p r (k2 k1) -> p r k2 k1", k1=64)
                    for k0 in range(0, 64, 4):
                        bank = 4 + (k0 // 4) % 2
                        for a in range(4):
                            k1 = k0 + a
                            MM(ps[bank][:, a * 128:(a + 1) * 128], Y[:, :, 0, k1], rr[:, k1, 0, :], ['Y', 'rr'], pk(bank),
                               start=True, stop=False)
                            MM(ps[bank][:, a * 128:(a + 1) * 128], Y[:, :, 1, k1], rr[:, k1, 1, :], ['Y', 'rr'], pk(bank),
                               start=False, stop=True)
                        CP('act' if bank == 4 else 'dve',
                           PTv[:, :, :, k0:k0 + 4].rearrange("p r k a -> p a r k"),
                           ps[bank][:, :].rearrange("p (a r k) -> p a r k", a=4, r=2), pk(bank), ['PT'])
                    if g == 0 and 'PT' in dbg_d:
                        S.dma(dbg_d['PT'], PT[:], ['PT'], ['dbg_PT'])
                    MM(ps[6][:, 0:128], cst['ccsc'][:, 0:128], wfm[:, g, :], ['ccsc', 'wfm'], pk(6))
                    MM(ps[6][:, 128:256], cst['ccsc'][:, 128:256], wfm[:, g, :], ['ccsc', 'wfm'], pk(6))
                    CP('dve', G[:], ps[6][:, 0:256], pk(6), ['G'])
                    for tc in range(8):
                        i = tc % 2
                        t0 = NCTX + tc * 512
                        for kc in range(8):
                            MM(ps[7][:, :], wfg[:, kc, g * 128:(g + 1) * 128], hT[:, kc, t0:t0 + 512],
                               [('hT', 1 + tc), 'wfg'], pk(7), start=(kc == 0), stop=(kc == 7))
                        ACT(SG[i][:], ps[7][:, :], AF.Silu, pk(7), [('SG', 0)])
                        MM(ps[6][:, :], G[:, 0:128], PT[:, 0, tc * 512:(tc + 1) * 512], ['G', 'PT'], pk(6),
                           start=True, stop=False)
                        MM(ps[6][:, :], G[:, 128:256], PT[:, 1, tc * 512:(tc + 1) * 512], ['G', 'PT'], pk(6),
                           start=False, stop=True)
                        TT('dve', FMc[i][:], ps[6][:, :], SG[i][:], ALU.mult, pk(6) + [('SG', 0)], [('FMc', 0)])
                        if 'fm' in dbg_d:
                            S.dma(dbg_d['fm'][g, :, tc * 512:(tc + 1) * 512], FMc[i][:], [('FMc', 0)], ['dbg_fm'])
                        S.dma(fm_s[g, :, tc * 512:(tc + 1) * 512], FMc[i][:], [('FMc', 0)], [('fm_s', g, tc)], semkey='fm_s', waitall=True, eng='pool')
            S.barrier()

        if 'fin' in phases:
            wout = sb('wout', [128, 8, D], BF16)
            with ExitStack() as es4:
                def sb4(name, shape, dt=F32):
                    return es4.enter_context(nc.sbuf_tensor('sb_' + name, list(shape), dt))
                wrf = sb4('wrf', [128, 8, D], BF16)
                wrd = sb4('wrd', [128, 8, D], BF16)
                wfo = sb4('wfo', [128, 4, D], BF16)
                wdn = sb4('wdn', [128, 8, D], BF16)
                for cb in range(4):
                    cs_ = slice(cb * 256, (cb + 1) * 256)
                    S.dma(wrf[:, :, cs_], win_d[:, :, OFF[6] + cb * 256:OFF[6] + (cb + 1) * 256], [], [('wrf', cb)], eng='pool')
                    S.dma(wrd[:, :, cs_], win_d[:, :, OFF[7] + cb * 256:OFF[7] + (cb + 1) * 256], [], [('wrd', cb)], eng='pool')
                    S.dma(wfo[:, :, cs_], wfo_d[:, :, cs_], [], [('wfo', cb)], eng='pool')
                    S.dma(wdn[:, :, cs_], wdn_d[:, :, cs_], [], [('wdn', cb)], eng='pool')
                load_cast(wout[:], wout_d, 8, D, 'wout')
                if 'wrf' in dbg_d:
                    S.dma(dbg_d['wrf'], wrf[:], [('wrf', cb) for cb in range(4)], ['dbg_wrf'])
                    S.dma(dbg_d['hT2'], hT[:, :, NCTX:NCTX + 512], ALLHT, ['dbg_hT2'])
                ogc = [sb4('ogc%d' % i, [128, 8, 512], BF16) for i in range(2)]
                fmc = [sb4('fmc%d' % i, [128, 4, 512], BF16) for i in range(2)]
                sgf = [sb4('sgf%d' % i, [128, 512], BF16) for i in range(2)]
                sgd = [sb4('sgd%d' % i, [128, 512], BF16) for i in range(2)]
                m1 = [sb4('m1_%d' % i, [128, 512], BF16) for i in range(2)]
                m2 = [sb4('m2_%d' % i, [128, 512], BF16) for i in range(2)]
                mgo = [sb4('mgo%d' % i, [128, 512], BF16) for i in range(2)]
                for tc in range(8):
                    ci = tc % 2
                    t0 = NCTX + tc * 512
                    if 'dn' in phases:
                        S.dma(ogc[ci][:], og_s[:, :, tc * 512:(tc + 1) * 512].rearrange("h p t -> p h t"),
                              [('og_s', h) for h in range(H)], [('ogc', ci)])
                    else:
                        MEMSET('pool', ogc[ci][:], 0.0, [('ogc', ci)])
                    if 'fo' in phases:
                        S.dma(fmc[ci][:], fm_s[:, :, tc * 512:(tc + 1) * 512].rearrange("g p t -> p g t"),
                              [('fm_s', g, tc) for g in range(4)], [('fmc', ci)])
                    else:
                        MEMSET('pool', fmc[ci][:], 0.0, [('fmc', ci)])
                    for fc in range(8):
                        i = fc % 2
                        bb = 4 * i
                        fs = slice(fc * 128, (fc + 1) * 128)
                        for kc in range(8):
                            MM(ps[bb][:, :], wrf[:, kc, fs], hT[:, kc, t0:t0 + 512], [('hT', 1 + tc), ('wrf', fc // 2)], pk(bb),
                               start=(kc == 0), stop=(kc == 7))
                        for kc in range(8):
                            MM(ps[bb + 1][:, :], wrd[:, kc, fs], hT[:, kc, t0:t0 + 512], [('hT', 1 + tc), ('wrd', fc // 2)], pk(bb + 1),
                               start=(kc == 0), stop=(kc == 7))
                        for g in range(4):
                            MM(ps[bb + 2][:, :], wfo[:, g, fs], fmc[ci][:, g, :], [('fmc', ci), ('wfo', fc // 2)], pk(bb + 2),
                               start=(g == 0), stop=(g == 3))
                        for h in range(H):
                            MM(ps[bb + 3][:, :], wdn[:, h, fs], ogc[ci][:, h, :], [('ogc', ci), ('wdn', fc // 2)], pk(bb + 3),
                               start=(h == 0), stop=(h == 7))
                        ACT(sgf[i][:], ps[bb][:, :], AF.Sigmoid, pk(bb), [('sgf', i)])
                        ACT(sgd[i][:], ps[bb + 1][:, :], AF.Sigmoid, pk(bb + 1), [('sgd', i)])
                        TT('dve', m1[i][:], ps[bb + 2][:, :], sgf[i][:], ALU.mult, pk(bb + 2) + [('sgf', i)], [('m1', i)])
                        TT('dve', m2[i][:], ps[bb + 3][:, :], sgd[i][:], ALU.mult, pk(bb + 3) + [('sgd', i)], [('m2', i)])
                        TT('dve', mgo[i][:], m1[i][:], m2[i][:], ALU.add, [('m1', i), ('m2', i)], [('mgo', i)])
                        S.dma(mg_s[fc, :, tc * 512:(tc + 1) * 512], mgo[i][:], [('mgo', i)], [('mg_s', tc)], semkey='mg_s', waitall=True, eng='pool')
                        if 'taps' in dbg_d and tc == 0:
                            S.dma(dbg_d['taps'][0, fc], sgf[i][:], [('sgf', i)], ['dbg_t0'])
                            S.dma(dbg_d['taps'][1, fc], sgd[i][:], [('sgd', i)], ['dbg_t1'])
                            S.dma(dbg_d['taps'][2, fc], m1[i][:], [('m1', i)], ['dbg_t2'])
                            S.dma(dbg_d['taps'][3, fc], m2[i][:], [('m2', i)], ['dbg_t3'])
                        if 'merged' in dbg_d and tc == 0:
                            S.dma(dbg_d['merged'][fc], mgo[i][:], [('mgo', i)], ['dbg_mg'])
            S.barrier()
            with ExitStack() as es5:
                def sb5(name, shape, dt=F32):
                    return es5.enter_context(nc.sbuf_tensor('sb_' + name, list(shape), dt))
                lng = sb5('lng', [128, D])
                lnb = sb5('lnb', [128, D])
                S.dma(lng[:], lng_d, [], ['lng'])
                S.dma(lnb[:], lnb_d, [], ['lnb'])
                mg = [sb5('mg%d' % i, [128, 8, 512], BF16) for i in range(2)]
                NBF = 4
                xt = [sb5('fxt%d' % i, [128, D]) for i in range(NBF)]
                pet = [sb5('fpet%d' % i, [128, D]) for i in range(NBF)]
                pre = [sb5('pre%d' % i, [128, D]) for i in range(NBF)]
                sq = sb5('fsq', [128, D], BF16)
                st = [sb5('fst%d' % i, [128, 8]) for i in range(8)]

                def fA(ti):
                    tc, tt = ti // 4, ti % 4
                    ci = tc % 2
                    j = ti % NBF
                    xk, pk_, prk, sk = ('fxt', j), ('fpet', j), ('pre', j), ('fst', ti % 8)
                    stt = st[ti % 8]
                    if tt == 0:
                        S.dma(mg[ci][:], mg_s[:, :, tc * 512:(tc + 1) * 512].rearrange("f p t -> p f t"),
                              [('mg_s', tc)], [('mg', ci)])
                    S.dma(xt[j][:], x_d[ti * 128:(ti + 1) * 128, :], [], [xk])
                    S.dma(pet[j][:], dr['pe'][ti * 128:(ti + 1) * 128, :], [], [pk_])
                    TT('dve', xt[j][:], xt[j][:], pet[j][:], ALU.add, [xk, pk_], [xk])
                    for half in range(2):
                        bank = 2 * (ti % 4) + half
                        for kc in range(8):
                            MM(ps[bank][:, :], mg[ci][:, kc, tt * 128:(tt + 1) * 128],
                               wout[:, kc, half * 512:(half + 1) * 512], [('mg', ci), 'wout'], pk(bank),
                               start=(kc == 0), stop=(kc == 7))
                        TT('dve', pre[j][:, half * 512:(half + 1) * 512], ps[bank][:, :],
                           gxb[:, half * 512:(half + 1) * 512], ALU.mult, pk(bank) + ['gxb'], [prk])
                    STT('dve', pre[j][:], xt[j][:], ALPHA, pre[j][:], ALU.mult, ALU.add, [xk, prk], [prk])
                    RED(stt[:, 0:1], pre[j][:], [prk], [sk])
                    ACT(sq[:], pre[j][:], AF.Square, [prk], ['fsq', sk], accum=stt[:, 1:2])

                def fB(ti):
                    sk = ('fst', ti % 8)
                    stt = st[ti % 8]
                    TS('dve', stt[:, 2:3], stt[:, 0:1], 1.0 / D, None, ALU.mult, None, [sk], [sk])
                    TT('dve', stt[:, 3:4], stt[:, 2:3], stt[:, 2:3], ALU.mult, [sk], [sk])
                    STT('dve', stt[:, 4:5], stt[:, 1:2], 1.0 / D, stt[:, 3:4], ALU.mult, ALU.subtract, [sk], [sk])
                    TS('dve', stt[:, 7:8], stt[:, 4:5], EPS, None, ALU.add, None, [sk], [sk])
                    TT('pool', stt[:, 5:6], stt[:, 7:8], mhalf[:, 0:1], ALU.pow, [sk, 'mhalf'], [sk])

                def fC(ti):
                    j = ti % NBF
                    prk, sk = ('pre', j), ('fst', ti % 8)
                    stt = st[ti % 8]
                    STT('dve', stt[:, 6:7], stt[:, 2:3], -1.0, stt[:, 5:6], ALU.mult, ALU.mult, [sk], [sk])
                    ACT(pre[j][:], pre[j][:], AF.Identity, [prk, sk], [prk], scale=stt[:, 5:6], bias=stt[:, 6:7])
                    TT('dve', pre[j][:], pre[j][:], lng[:], ALU.mult, [prk, 'lng'], [prk])
                    TT('dve', pre[j][:], pre[j][:], lnb[:], ALU.add, [prk, 'lnb'], [prk])
                    S.dma(out_d[ti * 128:(ti + 1) * 128, :], pre[j][:], [prk], [('out', ti)], semkey='out', eng='pool')

                for n in range(32 + 2):
                    if n < 32:
                        fA(n)
                    if 0 <= n - 1 < 32:
                        fB(n - 1)
                    if 0 <= n - 2 < 32:
                        fC(n - 2)
        S.emit()
    return nc, consts


_CACHE = {}


def _prep_shared(inputs):
    f = np.float32
    w_in = np.asarray(inputs['w_in'], f)[0]
    sh = {}
    sh['w_mod'] = np.ascontiguousarray(np.asarray(inputs['w_mod'], f)[0].reshape(128, 8, 3 * D))
    sh['b_mod2'] = np.ascontiguousarray(np.tile(np.asarray(inputs['b_mod'], f)[0][None], (2, 1)))
    sh['w_in_r'] = np.ascontiguousarray(w_in.reshape(8, 128, INCOLS).transpose(1, 0, 2))
    sh['w_dn_r'] = np.ascontiguousarray(np.asarray(inputs['w_dn_out'], f)[0].reshape(8, 128, D).transpose(1, 0, 2))
    sh['w_out_r'] = np.ascontiguousarray(np.asarray(inputs['w_out'], f)[0].reshape(8, 128, D).transpose(1, 0, 2))
    sh['w_fo_r'] = np.ascontiguousarray(np.asarray(inputs['w_f_out'], f)[0].reshape(4, 128, D).transpose(1, 0, 2))
    sh['w_fmix_r'] = np.ascontiguousarray(np.asarray(inputs['w_fmix'], f)[0].transpose(1, 0, 2))
    sh['convw_r'] = np.ascontiguousarray(np.asarray(inputs['conv_w'], f)[0].T.reshape(24, 128, 3).transpose(1, 0, 2))
    sh['alog_t'] = np.ascontiguousarray(np.tile(np.asarray(inputs['a_log'], f)[0].reshape(1, 16), (128, 1)))
    sh['dtb_t'] = np.ascontiguousarray(np.tile(np.asarray(inputs['dt_bias'], f)[0].reshape(1, 16), (128, 1)))
    sh['normw_t'] = np.ascontiguousarray(np.tile(np.asarray(inputs['dn_norm_w'], f)[0].reshape(1, 128), (128, 1)))
    sh['lng_t'] = np.ascontiguousarray(np.tile(np.asarray(inputs['ln_g'], f)[0].reshape(1, D), (128, 1)))
    sh['lnb_t'] = np.ascontiguousarray(np.tile(np.asarray(inputs['ln_b'], f)[0].reshape(1, D), (128, 1)))
    return sh


def make_in_maps(inputs, consts, cores):
    f = np.float32
    sh = _prep_shared(inputs)
    x = np.asarray(inputs['x'], f)
    ctx = np.asarray(inputs['ctx'], f)
    c = np.asarray(inputs['c'], f)
    c_ctx = np.asarray(inputs['c_ctx'], f)
    maps = []
    for b in cores:
        m = dict(consts)
        m.update(sh)
        m['x'] = np.ascontiguousarray(x[b])
        m['ctx'] = np.ascontiguousarray(ctx[b])
        m['cc'] = np.ascontiguousarray(np.stack([c[b].reshape(128, 8), c_ctx.reshape(128, 8)], -1).reshape(128, 16))
        maps.append(m)
    return maps


def kernel(**inputs):
    if 'nc' not in _CACHE:
        _CACHE['nc'] = build()
    nc, consts = _CACHE['nc']
    maps = make_in_maps(inputs, consts, range(8))
    res = run_bass_kernel_spmd(nc, maps, core_ids=list(range(8)))
    out = np.stack([np.asarray(r['out'], np.float32) for r in res.results], 0)
    return out
```

```python
from contextlib import ExitStack
import numpy as np
import ml_dtypes
import concourse.bass as bass
import concourse.mybir as mybir
from concourse.bass_utils import run_bass_kernel_spmd

F32 = mybir.dt.float32
BF16 = mybir.dt.bfloat16
AF = mybir.ActivationFunctionType
ALU = mybir.AluOpType
AX = mybir.AxisListType

D = 1024
NLAT = 4096
NCTX = 256
NTOK = NCTX + NLAT
H = 8
IN_SPLITS = (512, 512, 3072, 1024, 16, 16, 1024, 1024)
OFF = [0]
for _s in IN_SPLITS:
    OFF.append(OFF[-1] + _s)
INCOLS = OFF[-1]
EPS = 1e-6
ALPHA = 2.0 ** 0.25
NEG = -30000.0
import os
NOEXCL = bool(os.environ.get('NOEXCL'))


class Sched:
    def __init__(self, nc):
        self.nc = nc
        self.ops = []
        self.barriers = []

    def barrier(self):
        self.barriers.append(len(self.ops))

    def add(self, eng, fn, reads=(), writes=(), dma=False, semkey=None, waitall=False):
        self.ops.append(dict(eng=eng, fn=fn, reads=tuple(reads), writes=tuple(writes), dma=dma,
                             semkey=semkey, waitall=waitall))

    def pe(self, fn, r, w):
        self.add('pe', fn, r, w)

    def act(self, fn, r, w):
        self.add('act', fn, r, w)

    def dve(self, fn, r, w):
        self.add('dve', fn, r, w)

    def pool(self, fn, r, w):
        self.add('pool', fn, r, w)

    def dma(self, out, in_, r, w, eng='sp', semkey=None, waitall=False):
        self.add(eng, lambda e: e.dma_start(out=out, in_=in_), r, w, dma=True, semkey=semkey, waitall=waitall)

    def emit(self):
        nc = self.nc
        ops = self.ops
        last_writer = {}
        readers = {}
        last_on_eng = {}
        last_dma = {}
        pending = {}
        bank_last = {}
        bars = set(self.barriers)
        for i, op in enumerate(ops):
            if i in bars:
                bd = set(last_on_eng.values()) | set(last_dma.values())
                for en in ('pe', 'act', 'dve', 'pool', 'sp'):
                    pending[en] = set(bd) | pending.get(en, set())
            deps = set()
            if pending.get(op['eng']):
                deps |= pending.pop(op['eng'])
            for k in op['reads']:
                if k in last_writer:
                    deps.add(last_writer[k])
            for k in op['writes']:
                if k in last_writer:
                    deps.add(last_writer[k])
                deps.update(readers.get(k, ()))
            deps.discard(i)
            for k in op['reads'] + op['writes']:
                if isinstance(k, tuple) and k and k[0] == 'ps':
                    b = k[1]
                    la = bank_last.get(b)
                    if la is not None and ops[la]['eng'] != op['eng'] and not NOEXCL:
                        deps.add(la)
                    bank_last[b] = i
            if op['eng'] == 'pe':
                deps = {d for d in deps if not (ops[d]['eng'] == 'pe' and not ops[d]['dma'])}
            op['deps'] = deps
            if op['dma']:
                last_dma[op['writes'][0]] = i
            else:
                last_on_eng[op['eng']] = i
            for k in op['reads']:
                readers.setdefault(k, []).append(i)
            for k in op['writes']:
                last_writer[k] = i
                readers[k] = []
        needed = set()
        for op in ops:
            needed.update(op['deps'])
        with ExitStack() as es:
            engsem = {}
            for en in ('pe', 'act', 'dve', 'pool'):
                engsem[en] = es.enter_context(nc.semaphore('s_' + en))
            dmasem = {}
            dmacnt = {}
            cnt = {en: 0 for en in engsem}
            for i, op in enumerate(ops):
                if op['dma']:
                    key = op['semkey'] if op['semkey'] is not None else op['writes'][0]
                    if key not in dmasem:
                        dmasem[key] = es.enter_context(nc.semaphore('d%d' % len(dmasem)))
                        dmacnt[key] = 0
                    dmacnt[key] += 16
                    op['sig'] = (dmasem[key], dmacnt[key], 16)
                    op['sigkey'] = key
                elif i in needed:
                    cnt[op['eng']] += 1
                    op['sig'] = (engsem[op['eng']], cnt[op['eng']], 1)
                else:
                    op['sig'] = None
            grp_last = {}
            for i, op in enumerate(ops):
                if op['dma']:
                    grp_last[op['sigkey']] = i
            if os.environ.get('KDEBUG'):
                print('nsem_dma', len(dmasem), 'nops', len(ops), 'cnt', cnt, 'sems', [str(v) for v in list(dmasem.values())[:3]], [str(v) for v in engsem.values()])
            block = es.enter_context(nc.Block())
            by_eng = {}
            for i, op in enumerate(ops):
                by_eng.setdefault(op['eng'], []).append(i)

            def run(e, idxs, final=False):
                waited = {}
                for i in idxs:
                    op = ops[i]
                    need = {}
                    for d in op['deps']:
                        s = ops[d]['sig']
                        if ops[d]['dma'] and ops[d]['waitall'] and i > grp_last[ops[d]['sigkey']]:
                            s = (s[0], dmacnt[ops[d]['sigkey']], 16)
                        sid = id(s[0])
                        if s[1] > need.get(sid, (None, 0))[1]:
                            need[sid] = (s[0], s[1])
                    for sid, (sem, val) in need.items():
                        if waited.get(sid, 0) < val:
                            e.wait_ge(sem, val)
                            waited[sid] = val
                    ins = op['fn'](e)
                    if op['sig'] is not None:
                        ins.then_inc(op['sig'][0], op['sig'][2])
                if final:
                    for key, sem in dmasem.items():
                        if waited.get(id(sem), 0) < dmacnt[key]:
                            e.wait_ge(sem, dmacnt[key])

            @block.sync
            def _(e):
                run(e, by_eng.get('sp', []), final=True)

            @block.tensor
            def _(e):
                run(e, by_eng.get('pe', []))

            @block.scalar
            def _(e):
                run(e, by_eng.get('act', []))

            @block.vector
            def _(e):
                run(e, by_eng.get('dve', []))

            @block.gpsimd
            def _(e):
                run(e, by_eng.get('pool', []))


def _bf(a):
    return np.ascontiguousarray(a).astype(ml_dtypes.bfloat16)


def host_consts():
    c = {}
    c['ident_bf'] = _bf(np.eye(128))
    c['ident_f'] = np.eye(128, dtype=np.float32)
    c['ones_bf'] = _bf(np.ones((128, 128)))
    selx = np.zeros((2, 128), np.float32)
    selx[0] = 1.0
    c['selx'] = selx
    q = D // 4
    om = 1.0 / (10000.0 ** (np.arange(q, dtype=np.float32) / q))
    pr = np.arange(64, dtype=np.float32)[:, None] * om
    er = np.concatenate([np.sin(pr), np.cos(pr)], -1)
    pe = np.concatenate([np.broadcast_to(er[:, None, :], (64, 64, 512)),
                         np.broadcast_to(er[None, :, :], (64, 64, 512))], -1)
    c['pe'] = np.ascontiguousarray(pe.reshape(4096, D)).astype(np.float32)
    idx = np.arange(128)
    same = (idx[:, None] // 64) == (idx[None, :] // 64)
    c['Uf'] = (same & (idx[:, None] <= idx[None, :])).astype(np.float32)
    c['Ub'] = (same & (idx[:, None] >= idx[None, :])).astype(np.float32)
    c['SCm'] = same.astype(np.float32)
    vf = same & (idx[None, :] >= idx[:, None])
    vb = same & (idx[None, :] <= idx[:, None])
    c['mnegF'] = np.where(vf, 0.0, NEG).astype(np.float32)
    c['mnegB'] = np.where(vb, 0.0, NEG).astype(np.float32)
    sub = (idx[:, None] // 16) == (idx[None, :] // 16)
    c['mdF'] = _bf(sub & (idx[None, :] > idx[:, None]))
    c['mdB'] = _bf(sub & (idx[None, :] < idx[:, None]))
    c['moF'] = _bf(same & ~sub & (idx[None, :] > idx[:, None]))
    c['moB'] = _bf(same & ~sub & (idx[None, :] < idx[:, None]))
    sel = np.zeros((128, 16, 128), np.float32)
    for c_ in range(16):
        sel[c_, c_, :] = 1.0
        sel[16 + c_, c_, :] = 1.0
    c['sel32'] = _bf(sel)
    a64 = 2 * np.pi * np.outer(np.arange(64), np.arange(64)) / 64
    c['c64'] = _bf(np.concatenate([np.cos(a64), -np.sin(a64)], 1))
    kk = np.arange(4096)
    th = 2 * np.pi * np.outer(np.arange(64), kk) / 4096
    Mc = (np.cos(th) / 64.0).reshape(64, 64, 64)
    Ms = (np.sin(th) / 64.0).reshape(64, 64, 64)
    Mc = Mc.transpose(0, 2, 1)
    Ms = Ms.transpose(0, 2, 1)
    rr = np.zeros((64, 64, 2, 2, 64))
    rr[:, :, 0, 0] = Mc
    rr[:, :, 0, 1] = -Ms
    rr[:, :, 1, 0] = Ms
    rr[:, :, 1, 1] = Mc
    c['rr'] = _bf(rr.reshape(64, 64, 2, 128))
    ac = 2 * np.pi * np.outer(np.arange(128), np.arange(128)) / 128
    c['ccsc'] = _bf(np.concatenate([np.cos(ac), np.sin(ac)], 1) / np.sqrt(128.0))
    return c


def build(dbg=(), phases=('dn', 'fo', 'fin'), heads=range(H)):
    nc = bass.Bass("TRN2", target_bir_lowering=False)
    consts = host_consts()
    dr = {}

    def din(name, shape, dt=F32):
        dr[name] = nc.dram_tensor(name, list(shape), dt, kind="ExternalInput").ap()
        return dr[name]

    x_d = din('x', [NLAT, D])
    ctx_d = din('ctx', [NCTX, D])
    cc_d = din('cc', [128, 16])
    wmod_d = din('w_mod', [128, 8, 3 * D])
    bmod_d = din('b_mod2', [2, 3 * D])
    win_d = din('w_in_r', [128, 8, INCOLS])
    wdn_d = din('w_dn_r', [128, 8, D])
    wout_d = din('w_out_r', [128, 8, D])
    wfo_d = din('w_fo_r', [128, 4, D])
    wfm_d = din('w_fmix_r', [128, 4, 128])
    convw_d = din('convw_r', [128, 24, 3])
    alog_d = din('alog_t', [128, 16])
    dtb_d = din('dtb_t', [128, 16])
    normw_d = din('normw_t', [128, 128])
    lng_d = din('lng_t', [128, D])
    lnb_d = din('lnb_t', [128, D])
    for k, v in consts.items():
        din(k, v.shape, BF16 if v.dtype == ml_dtypes.bfloat16 else F32)
    out_d = nc.dram_tensor('out', [NLAT, D], F32, kind="ExternalOutput").ap()
    og_s = nc.dram_tensor('og_s', [H, 128, NLAT], BF16).ap()
    fm_s = nc.dram_tensor('fm_s', [4, 128, NLAT], BF16).ap()
    mg_s = nc.dram_tensor('mg_s', [8, 128, NLAT], BF16).ap()
    dbg_d = {}
    for name, shape, dt in dbg:
        dbg_d[name] = nc.dram_tensor('dbg_' + name, list(shape), dt, kind="ExternalOutput").ap()

    S = Sched(nc)

    def MM(out, lhsT, rhs, r, w, start=True, stop=True):
        S.pe(lambda e: e.matmul(out, lhsT=lhsT, rhs=rhs, start=start, stop=stop), r, w)

    def TR(out, in_, ident, r, w):
        S.pe(lambda e: e.transpose(out=out, in_=in_, identity=ident), r, w)

    def TT(eng, out, in0, in1, op, r, w):
        S.add(eng, lambda e: e.tensor_tensor(out=out, in0=in0, in1=in1, op=op), r, w)

    def TS(eng, out, in0, s1, s2, op0, op1, r, w):
        if s2 is None:
            S.add(eng, lambda e: e.tensor_scalar(out=out, in0=in0, scalar1=s1, scalar2=None, op0=op0), r, w)
        else:
            S.add(eng, lambda e: e.tensor_scalar(out=out, in0=in0, scalar1=s1, scalar2=s2, op0=op0, op1=op1), r, w)

    def STT(eng, out, in0, scalar, in1, op0, op1, r, w):
        S.add(eng, lambda e: e.scalar_tensor_tensor(out=out, in0=in0, scalar=scalar, in1=in1, op0=op0, op1=op1),
              r, w)

    def ACT(out, in_, func, r, w, bias=None, scale=None, accum=None):
        kw = {}
        if bias is not None:
            kw['bias'] = bias
        if scale is not None:
            kw['scale'] = scale
        if accum is not None:
            kw['accum_out'] = accum
        S.act(lambda e: e.activation(out=out, in_=in_, func=func, **kw), r, w)

    def CP(eng, out, in_, r, w):
        if eng == 'act':
            S.act(lambda e: e.activation(out=out, in_=in_, func=AF.Copy), r, w)
        else:
            S.add(eng, lambda e: e.tensor_copy(out=out, in_=in_), r, w)

    def RED(out, in_, r, w):
        S.dve(lambda e: e.tensor_reduce(out=out, in_=in_, axis=AX.X, op=ALU.add), r, w)

    def MEMSET(eng, ap, val, w):
        S.add(eng, lambda e: e.memset(ap, val), [], w)

    with ExitStack() as es:
        def sb(name, shape, dt=F32):
            return es.enter_context(nc.sbuf_tensor('sb_' + name, list(shape), dt))

        ps = [es.enter_context(nc.psum_tensor('ps%d' % i, [128, 512], F32)) for i in range(8)]

        def pk(b, q0=0, nq=4):
            return [('ps', b, q) for q in range(q0, q0 + nq)]

        def psq(b, q, nq=1):
            return ps[b][:, q * 128:(q + nq) * 128]

        cst = {}

        def load_consts(names, alloc, grp):
            for name in names:
                v = consts[name]
                cst[name] = alloc(name, v.shape, BF16 if v.dtype == ml_dtypes.bfloat16 else F32)
                S.dma(cst[name][:], dr[name], [], [name], semkey=grp, waitall=True)

        load_consts(('ident_bf', 'ident_f', 'ones_bf', 'selx'), sb, 'consts')
        ident_bf, ident_f, selx = cst['ident_bf'], cst['ident_f'], cst['selx']
        hT = sb('hT', [128, 8, NTOK], BF16)
        mhalf = sb('mhalf', [128, 32])
        MEMSET('pool', mhalf[:], -0.5, ['mhalf'])
        modT = sb('modT', [128, 48])
        gxb = sb('gxb', [128, D])

        def load_cast(dst, src, a, b, wkey):
            bc = max(1, 2048 // a)
            for b0 in range(0, b, bc):
                bn = min(bc, b - b0)
                S.dma(dst[:, :, b0:b0 + bn], src[:, :, b0:b0 + bn], [], [wkey], eng='pool')

        def hT_keys_for_tokens(t0, t1):
            ks = set()
            for t in range(t0, t1, 128):
                ks.add(('hT', 0 if t < NCTX else 1 + (t - NCTX) // 512))
            return sorted(ks)

        with ExitStack() as es0:
            def sb0(name, shape, dt=F32):
                return es0.enter_context(nc.sbuf_tensor('sb_' + name, list(shape), dt))
            cc = sb0('cc', [128, 16])
            sc = sb0('sc', [128, 16])
            wm = [sb0('wm%d' % i, [128, 3 * D]) for i in range(2)]
            bm = sb0('bm', [2, 3 * D])
            modrow = sb0('modrow', [2, 3 * D])
            S.dma(cc[:], cc_d, [], ['cc'])
            S.dma(bm[:], bmod_d, [], ['bm'])
            ACT(sc[:], cc[:], AF.Silu, ['cc'], ['sc'])
            sc3 = sc[:].rearrange("p (k v) -> p k v", v=2)
            for k in range(8):
                S.dma(wm[k % 2][:], wmod_d[:, k, :], [], ['wm%d' % (k % 2)])
                for j in range(6):
                    MM(ps[j][0:2, :], sc3[:, k, :], wm[k % 2][:, j * 512:(j + 1) * 512],
                       ['sc', 'wm%d' % (k % 2)], pk(j), start=(k == 0), stop=(k == 7))
            for j in range(6):
                TT('dve', modrow[:, j * 512:(j + 1) * 512], ps[j][0:2, :], bm[:, j * 512:(j + 1) * 512], ALU.add,
                   pk(j) + ['bm'], ['modrow'])
            for j in range(24):
                MM(ps[6][:, 2 * j:2 * j + 2], modrow[0:2, j * 128:(j + 1) * 128], ident_f[0:2, 0:2],
                   ['modrow', 'ident_f'], pk(6))
            CP('dve', modT[:], ps[6][:, 0:48], pk(6), ['modT'])
            TS('dve', modT[:, 16:32], modT[:, 16:32], 1.0, None, ALU.add, None, ['modT'], ['modT'])
            for jj in range(2):
                MM(ps[7][:, :], selx[0:2, :], modrow[0:2, 2048 + jj * 512:2048 + (jj + 1) * 512],
                   ['modrow', 'selx'], pk(7))
                CP('act', gxb[:, jj * 512:(jj + 1) * 512], ps[7][:, :], pk(7), ['gxb'])

            NBF = 4
            xt = [sb0('xt%d' % i, [128, D]) for i in range(NBF)]
            pet = [sb0('pet%d' % i, [128, D]) for i in range(NBF)]
            sq = sb0('sq', [128, D], BF16)
            xn = [sb0('xn%d' % i, [128, 4, D], BF16) for i in range(2)]
            st = [sb0('st%d' % i, [128, 8]) for i in range(8)]
            groups = [(1, [(ctx_d, 0), (ctx_d, 1)], 0)]
            for g in range(8):
                groups.append((0, [(x_d, 4 * g + i) for i in range(4)], NCTX + g * 512))
            flat = []
            for gi, (v, tiles, tok0) in enumerate(groups):
                for i, (src, ti) in enumerate(tiles):
                    flat.append((gi, i, v, src, ti, len(tiles), tok0))

            def lnA(n):
                gi, i, v, src, ti, nt, tok0 = flat[n]
                j = n % NBF
                stt, sk, xk = st[n % 8], 'st%d' % (n % 8), 'xt%d' % j
                S.dma(xt[j][:], src[ti * 128:(ti + 1) * 128, :], [], [xk])
                if v == 0:
                    S.dma(pet[j][:], dr['pe'][ti * 128:(ti + 1) * 128, :], [], ['pet%d' % j])
                    TT('dve', xt[j][:], xt[j][:], pet[j][:], ALU.add, [xk, 'pet%d' % j], [xk])
                RED(stt[:, 0:1], xt[j][:], [xk], [sk])
                ACT(sq[:], xt[j][:], AF.Square, [xk], ['sq', sk], accum=stt[:, 1:2])

            def lnB(n):
                stt, sk = st[n % 8], 'st%d' % (n % 8)
                TS('dve', stt[:, 2:3], stt[:, 0:1], 1.0 / D, None, ALU.mult, None, [sk], [sk])
                TT('dve', stt[:, 3:4], stt[:, 2:3], stt[:, 2:3], ALU.mult, [sk], [sk])
                STT('dve', stt[:, 4:5], stt[:, 1:2], 1.0 / D, stt[:, 3:4], ALU.mult, ALU.subtract, [sk], [sk])
                TS('dve', stt[:, 7:8], stt[:, 4:5], EPS, None, ALU.add, None, [sk], [sk])
                TT('pool', stt[:, 5:6], stt[:, 7:8], mhalf[:, 0:1], ALU.pow, [sk, 'mhalf'], [sk])

            def lnC(n):
                gi, i, v, src, ti, nt, tok0 = flat[n]
                j = n % NBF
                gp = gi % 2
                stt, sk, xk = st[n % 8], 'st%d' % (n % 8), 'xt%d' % j
                STT('dve', stt[:, 6:7], stt[:, 2:3], -1.0, stt[:, 5:6], ALU.mult, ALU.mult, [sk], [sk])
                ACT(xn[gp][:, i, :], xt[j][:], AF.Identity, [xk, sk], [('xn', gp, i)], scale=stt[:, 5:6], bias=stt[:, 6:7])
                if i == nt - 1:
                    for kc in range(8):
                        pb = 6 + (kc % 2)
                        pst = ps[pb][:].bitcast(BF16)
                        for ii in range(nt):
                            TR(pst[:, ii * 128:(ii + 1) * 128], xn[gp][:, ii, kc * 128:(kc + 1) * 128], ident_bf[:],
                               [('xn', gp, ii), 'ident_bf'], pk(pb))
                        dst = hT[:, kc, tok0:tok0 + nt * 128]
                        sc_ap = modT[:, 2 * (8 + kc) + v:2 * (8 + kc) + v + 1]
                        sh_ap = modT[:, 2 * kc + v:2 * kc + v + 1]
                        if kc % 2 == 0:
                            ACT(dst, pst[:, 0:nt * 128], AF.Identity, pk(pb) + ['modT'], [('hT', gi)], scale=sc_ap, bias=sh_ap)
                        else:
                            TS('dve', dst, pst[:, 0:nt * 128], sc_ap, sh_ap, ALU.mult, ALU.add, pk(pb) + ['modT'], [('hT', gi)])

            NT0 = len(flat)
            for n in range(NT0 + 2):
                if n < NT0:
                    lnA(n)
                if 0 <= n - 1 < NT0:
                    lnB(n - 1)
                if 0 <= n - 2 < NT0:
                    lnC(n - 2)
        S.barrier()
        ALLHT = [('hT', g) for g in range(9)]
        if 'hT' in dbg_d:
            for kc in range(8):
                S.dma(dbg_d['hT'][kc], hT[:, kc, :], ALLHT, ['dbg_hT'])

        if 'dn' in phases:
            with ExitStack() as es2:
                def sb2(name, shape, dt=F32):
                    return es2.enter_context(nc.sbuf_tensor('sb_' + name, list(shape), dt))
                NB = NTOK // 128
                load_consts(('Uf', 'Ub', 'SCm', 'mnegF', 'mnegB', 'mdF', 'mdB', 'moF', 'moB'), sb2, 'consts_dn')
                wbd = sb2('wbd', [128, 8, 32], BF16)
                load_cast(wbd[:], win_d[:, :, OFF[4]:OFF[4] + 32], 8, 32, 'wbd')
                beta_tm = sb2('beta_tm', [128, NB, 16])
                gc_tm = sb2('gc_tm', [128, NB, 16])
                egc_tm = sb2('egc_tm', [128, NB, 16])
                egl_tm = sb2('egl_tm', [128, NB, 16])
                gcT = sb2('gcT', [128, NB, 128], BF16)
                selh = sb2('selh', [128, 2, 128], BF16)
                MEMSET('pool', gcT[:], 0.0, ['gcT'])
                es2p = ExitStack()
                def sb2p(name, shape, dt=F32):
                    return es2p.enter_context(nc.sbuf_tensor('sb_' + name, list(shape), dt))
                raw_tm = sb2p('raw_tm', [128, NB, 32])
                gcb_tm = sb2p('gcb_tm', [128, NB, 16], BF16)
                gch_tm = sb2p('gch_tm', [128, NB, 16])
                gcl_tm = sb2p('gcl_tm', [128, NB, 16])
                gcHL_tm = sb2p('gcHL_tm', [128, NB, 32], BF16)
                g_tm = sb2p('g_tm', [128, NB, 16])
                tot_tm = sb2p('tot_tm', [128, NB, 16])
                tmp_tm = sb2p('tmp_tm', [128, NB, 16])
                alog = sb2p('alog', [128, 16])
                dtb = sb2p('dtb', [128, 16])
                nega = sb2p('nega', [128, 16])
                S.dma(alog[:], alog_d, [], ['alog'])
                S.dma(dtb[:], dtb_d, [], ['dtb'])
                ACT(nega[:], alog[:], AF.Exp, ['alog'], ['nega'])
                TS('dve', nega[:], nega[:], -1.0, None, ALU.mult, None, ['nega'], ['nega'])
                for b0 in range(0, NB, 16):
                    nb_ = min(16, NB - b0)
                    bank = 6 + (b0 // 16) % 2
                    for bi in range(nb_):
                        blk = b0 + bi
                        for kc in range(8):
                            MM(ps[bank][:, bi * 32:(bi + 1) * 32], hT[:, kc, blk * 128:(blk + 1) * 128], wbd[:, kc, :],
                               hT_keys_for_tokens(blk * 128, blk * 128 + 128) + ['wbd'], pk(bank),
                               start=(kc == 0), stop=(kc == 7))
                    CP('act', raw_tm[:, b0:b0 + nb_, :],
                       ps[bank][:, 0:nb_ * 32].rearrange("p (b c) -> p b c", c=32), pk(bank), ['raw_tm'])
                ACT(beta_tm[:], raw_tm[:, :, 0:16], AF.Sigmoid, ['raw_tm'], ['beta_tm'])
                TT('dve', tmp_tm[:], raw_tm[:, :, 16:32], dtb[:].unsqueeze(1).to_broadcast([128, NB, 16]), ALU.add,
                   ['raw_tm', 'dtb'], ['tmp_tm'])
                ACT(tmp_tm[:], tmp_tm[:], AF.Exp, ['tmp_tm'], ['tmp_tm'])
                ACT(tmp_tm[:], tmp_tm[:], AF.Ln, ['tmp_tm'], ['tmp_tm'], bias=1.0)
                TT('dve', g_tm[:], tmp_tm[:], nega[:].unsqueeze(1).to_broadcast([128, NB, 16]), ALU.mult,
                   ['tmp_tm', 'nega'], ['g_tm'])
                for half in range(2):
                    b0 = half * 17
                    for bi in range(17):
                        blk = b0 + bi
                        MM(ps[6][:, bi * 16:bi * 16 + 8], cst['Uf'][:], g_tm[:, blk, 0:8], ['g_tm', 'Uf'], pk(6))
                        MM(ps[6][:, bi * 16 + 8:bi * 16 + 16], cst['Ub'][:], g_tm[:, blk, 8:16], ['g_tm', 'Ub'], pk(6))
                        MM(ps[7][:, bi * 16:bi * 16 + 16], cst['SCm'][:], g_tm[:, blk, :], ['g_tm', 'SCm'], pk(7))
                    CP('dve', gc_tm[:, b0:b0 + 17, :], ps[6][:, 0:272].rearrange("p (b c) -> p b c", c=16), pk(6),
                       ['gc_tm'])
                    CP('dve', tot_tm[:, b0:b0 + 17, :], ps[7][:, 0:272].rearrange("p (b c) -> p b c", c=16), pk(7),
                       ['tot_tm'])
                ACT(egc_tm[:], gc_tm[:], AF.Exp, ['gc_tm'], ['egc_tm'])
                CP('dve', gcb_tm[:], gc_tm[:], ['gc_tm'], ['gcb_tm'])
                CP('dve', gch_tm[:], gcb_tm[:], ['gcb_tm'], ['gch_tm'])
                TT('dve', gcl_tm[:], gc_tm[:], gch_tm[:], ALU.subtract, ['gc_tm', 'gch_tm'], ['gcl_tm'])
                CP('dve', gcHL_tm[:, :, 0:16], gcb_tm[:], ['gcb_tm'], ['gcHL_tm'])
                CP('dve', gcHL_tm[:, :, 16:32], gcl_tm[:], ['gcl_tm'], ['gcHL_tm'])
                for b0 in range(0, NB, 8):
                    nb_ = min(8, NB - b0)
                    pst = ps[6][:].bitcast(BF16)
                    for bi in range(nb_):
                        TR(pst[0:32, bi * 128:(bi + 1) * 128], gcHL_tm[:, b0 + bi, :], ident_bf[:], ['gcHL_tm', 'ident_bf'], pk(6))
                    CP('dve', gcT[0:32, b0:b0 + nb_, :], pst[0:32, 0:nb_ * 128].rearrange("p (b t) -> p b t", t=128),
                       pk(6), ['gcT'])
                TT('dve', tmp_tm[:], tot_tm[:], gc_tm[:], ALU.subtract, ['tot_tm', 'gc_tm'], ['tmp_tm'])
                ACT(egl_tm[:], tmp_tm[:], AF.Exp, ['tmp_tm'], ['egl_tm'])
                if 'gcT' in dbg_d:
                    S.dma(dbg_d['gcT'], gcT[:], ['gcT'], ['dbg_gcT'])
                if 'tm' in dbg_d:
                    S.dma(dbg_d['tm'][0], beta_tm[:], ['beta_tm'], ['dbg_tm0'])
                    S.dma(dbg_d['tm'][1], gc_tm[:], ['gc_tm'], ['dbg_tm1'])
                    S.dma(dbg_d['tm'][2], egl_tm[:], ['egl_tm'], ['dbg_tm2'])
                es2p.close()
                S.barrier()

                X0 = sb2('X0', [128, 4360], BF16)
                X1 = sb2('X1', [128, 4360], BF16)
                X2 = sb2('X2', [128, 4360], BF16)
                QT = sb2('QT', [128, NTOK], BF16)
                KT = sb2('KT', [128, NTOK], BF16)
                Ktok = sb2('Ktok', [128, NB, 128], BF16)
                Vtok = sb2('Vtok', [128, NB, 128], BF16)
                Obuf = sb2('Obuf', [128, 32, 128], BF16)
                wqkv = sb2('wqkv', [128, 8, 384], BF16)
                wdg = sb2('wdg', [128, 8, 128], BF16)
                convw = sb2('convw', [128, 24, 3])
                normw = sb2('normw', [128, 128])
                sqb = sb2('sqb', [128, 512], BF16)
                sqb2 = [sqb, sb2('sqb1', [128, 512], BF16)]
                Dg = [sb2('Dg%d' % j, [128, 128], BF16) for j in range(3)]
                lnt = [sb2('lnt0', [128, 512]), sb2('lnt1', [128, 512], BF16)]
                ssq = sb2('ssq', [128, 32])
                rstd_o = sb2('rstd_o', [128, 32])
                Onb = [sb2('Onb%d' % i, [128, 128], BF16) for i in range(4)]
                S.dma(convw[:], convw_d, [], ['convw'])
                S.dma(normw[:], normw_d, [], ['normw'])
                MEMSET('pool', X0[:], 0.0, ['X0'])
                Sf32 = [sb2('Sf32_%d' % d, [128, 128]) for d in range(2)]
                Sbf = [sb2('Sbf_%d' % d, [128, 128], BF16) for d in range(2)]
                vnew = [[sb2('vnew_%d%d' % (d, c_), [128, 128], BF16) for c_ in range(2)] for d in range(2)]
                LOOK = 3
                carve = [[X1, 0, 4360], [X2, 0, 4360], [X0, 260, 4352]]

                def newbuf(nm, w, dt):
                    ne = w * (2 if dt == F32 else 1)
                    for reg in carve:
                        a = (reg[1] + 1) // 2 * 2
                        if a + ne <= reg[2]:
                            reg[1] = a + ne
                            v = reg[0][:, a:a + ne]
                            return v.bitcast(F32) if dt == F32 else v
                    return sb2(nm, [128, w], dt)[:]

                INTERNAL = ('Gdh', 'Gdl', 'X', 'N', 'Bns', 'Bd', 'Bo', 'LLo', 'LB0', 'LB1', 'Q0', 'Q1', 'nMT',
                            'X0', 'Xit0', 'Xit1', 'Kg')
                ISET = {}
                PSET = {}
                for d in range(2):
                    for i in range(LOOK):
                        nb = lambda nm, w=128, dt=BF16: newbuf('%s_%d%d' % (nm, d, i), w, dt)
                        ISET[d, i] = dict(X=nb('X', 128, F32), N=nb('N'), Bns=nb('Bns'),
                                          Bd=nb('Bd'), Bo=nb('Bo'), LLo=nb('LLo', 256), LB=[nb('LB0', 256), nb('LB1', 256)],
                                          Q=[nb('Q0'), nb('Q1')], nMT=nb('nMT'), X0=nb('X0', 256),
                                          Xit=[nb('Xit0', 256), nb('Xit1', 256)], Kg=nb('Kg'))
                    for i in range(LOOK + 1):
                        nb = lambda nm, w=128, dt=BF16: newbuf('%s_%d%dp' % (nm, d, i), w, dt)
                        PSET[d, i] = dict(EG=nb('EG', 128, F32), qgT=nb('qgT'), QKT=nb('QKT'), Ub=nb('Ub'), nwT=nb('nwT'),
                                          kd=nb('kd'))

                def PQ(d, name, odd=0):
                    m = {'G': (0, 0), 'K': (0, 1), 'Qp': (0, 3), 'L': (1, 0), 'LB': (1, 1), 'X': (1, 1), 'M': (1, 3),
                         'W': (1, 3), 'v': (2, 0), 'o': (2, 1), 'dS': (2, 2)}[name]
                    if m[0] == 1 and odd:
                        return (6 + d, m[1])
                    return (3 * d + m[0], m[1])

                def chunk_of_block(blk):
                    return 0 if blk < 2 else 1 + (blk - 2) // 4

                hl = list(heads)

                def load_qkv_w(hh):
                    for wi in range(3):
                        c0 = OFF[2] + wi * 1024 + hh * 128
                        load_cast(wqkv[:, :, wi * 128:(wi + 1) * 128], win_d[:, :, c0:c0 + 128], 8, 128, ('wqkv', wi))

                for h in heads:
                    if h == hl[0]:
                        load_qkv_w(h)
                    for d_ in range(2):
                        S.dma(selh[:, d_, :], dr['sel32'][:, d_ * 8 + h, :], [], ['selh'])
                    c0 = OFF[3] + h * 128
                    load_cast(wdg[:], win_d[:, :, c0:c0 + 128], 8, 128, 'wdg')
                    def blocks_of_chunk(c):
                        return [0, 1] if c == 0 else list(range(2 + 4 * (c - 1), 6 + 4 * (c - 1)))

                    for wi in (1, 2, 0):
                        cw = convw[:, wi * 8 + h, :]
                        dstT = KT if wi == 1 else QT
                        dk = 'KT' if wi == 1 else 'QT'
                        for j in range(3):
                            TS('dve', Dg[j][:], ident_bf[:], cw[:, j:j + 1], None, ALU.mult, None, ['ident_bf', 'convw'],
                               [('Dg', j)])

                        def geom(c):
                            t0 = 0 if c == 0 else NCTX + (c - 1) * 512
                            n = 256 if c == 0 else 512
                            col0 = 1 if c == 0 else 259 + (c - 1) * 512
                            return t0, n, col0

                        def stA(c):
                            t0, n, col0 = geom(c)
                            bank = 6 + c % 2
                            for kc in range(8):
                                MM(ps[bank][:, 0:n], wqkv[:, kc, wi * 128:(wi + 1) * 128], hT[:, kc, t0:t0 + n],
                                   [('hT', c), ('wqkv', wi)], pk(bank), start=(kc == 0), stop=(kc == 7))
                            CP('dve', X0[:, col0:col0 + n], ps[bank][:, 0:n], pk(bank), [('X0', c)])

                        def stB(c):
                            t0, n, col0 = geom(c)
                            nb_keys = [('X0', cc) for cc in (c - 1, c, c + 1) if 0 <= cc <= 8] + ['X0']
                            bank = c % 2
                            for j in range(3):
                                MM(ps[bank][:, 0:n], Dg[j][:], X0[:, col0 + j - 1:col0 + j - 1 + n], nb_keys + [('Dg', j)],
                                   pk(bank), start=(j == 0), stop=(j == 2))
                            ACT(X2[:, t0:t0 + n], ps[bank][:, 0:n], AF.Silu, pk(bank), [('X2', c)])

                        def stC(c):
                            t0, n, col0 = geom(c)
                            if wi == 2:
                                blks = blocks_of_chunk(c)
                                bank = 4 + c % 2
                                pst = ps[bank][:].bitcast(BF16)
                                for bi_, blk in enumerate(blks):
                                    TR(pst[:, bi_ * 128:(bi_ + 1) * 128], X2[:, blk * 128:(blk + 1) * 128], ident_bf[:],
                                       [('X2', c), 'ident_bf'], pk(bank))
                                CP('dve' if c % 2 == 0 else 'act', Vtok[:, blks[0]:blks[0] + len(blks), :],
                                   pst[:, 0:len(blks) * 128].rearrange("p (b c) -> p b c", c=128), pk(bank),
                                   [('Vtok', blk) for blk in blks])
                            else:
                                bank = 2 + c % 2
                                sq_ = sqb2[c % 2]
                                TT('dve', sq_[:, 0:n], X2[:, t0:t0 + n], X2[:, t0:t0 + n], ALU.mult, [('X2', c)], [('sqb', c % 2)])
                                MM(ps[bank][:, 0:n], cst['ones_bf'][:], sq_[:, 0:n], [('sqb', c % 2), 'ones_bf'], pk(bank))

                        def stD(c):
                            t0, n, col0 = geom(c)
                            if wi == 2:
                                return
                            bank = 2 + c % 2
                            ACT(lnt[0][:, 0:n], ps[bank][:, 0:n], AF.Ln, pk(bank), ['lnt0'], bias=EPS)
                            ACT(lnt[1][:, 0:n], lnt[0][:, 0:n], AF.Exp, ['lnt0'], ['lnt1'], scale=-0.5,
                                bias=(-0.5 * float(np.log(128.0)) if wi == 0 else 0.0))
                            TT('dve', dstT[:, t0:t0 + n], X2[:, t0:t0 + n], lnt[1][:, 0:n], ALU.mult,
                               [('X2', c), 'lnt1'], [(dk, c)])
                            if wi == 1:
                                blks = blocks_of_chunk(c)
                                bank = 4 + c % 2
                                pst = ps[bank][:].bitcast(BF16)
                                for bi_, blk in enumerate(blks):
                                    TR(pst[:, bi_ * 128:(bi_ + 1) * 128], KT[:, blk * 128:(blk + 1) * 128], ident_bf[:],
                                       [('KT', c), 'ident_bf'], pk(bank))
                                CP('dve' if c % 2 == 0 else 'act', Ktok[:, blks[0]:blks[0] + len(blks), :],
                                   pst[:, 0:len(blks) * 128].rearrange("p (b c) -> p b c", c=128), pk(bank),
                                   [('Ktok', blk) for blk in blks])

                        for i in range(9 + 4):
                            if i < 9:
                                stA(i)
                            if 0 <= i - 2 < 9:
                                stB(i - 2)
                            if 0 <= i - 3 < 9:
                                stC(i - 3)
                            if 0 <= i - 4 < 9:
                                stD(i - 4)
                    if 'qkv' in dbg_d and h == list(heads)[0]:
                        S.dma(dbg_d['qkv'][0], QT[:], [('QT', c) for c in range(9)], ['dbg_q'])
                        S.dma(dbg_d['qkv'][1], KT[:], [('KT', c) for c in range(9)], ['dbg_k'])
                        S.dma(dbg_d['vtok'], Vtok[:], [('Vtok', b) for b in range(NB)], ['dbg_v'])

                    if 'selh' in dbg_d:
                        S.dma(dbg_d['selh'], selh[:], ['selh'], ['dbg_selh'])
                    if hl.index(h) + 1 < len(hl):
                        load_qkv_w(hl[hl.index(h) + 1])
                    S.barrier()
                    for d in range(2):
                        MEMSET('pool', Sf32[d][:], 0.0, [('S', d)])
                        MEMSET('pool', Sbf[d][:], 0.0, [('Sb', d)])
                        for c_ in range(2):
                            MEMSET('pool', vnew[d][c_][:], 0.0, [('vn', d, c_)])
                    orderC = [list(range(68)), [3, 2, 1, 0] + list(range(67, 3, -1))]
                    orderB = []
                    for d in range(2):
                        ob = []
                        for c in orderC[d]:
                            if c // 2 not in ob:
                                ob.append(c // 2)
                        orderB.append(ob)
                    owritten = set()

                    def setup_micro(d, bi):
                        blk = orderB[d][bi]
                        I = ISET[d, bi % LOOK]
                        P = PSET[d, bi % (LOOK + 1)]
                        odd = bi % 2
                        col = d * 8 + h
                        kq = [('QT', chunk_of_block(blk)), ('KT', chunk_of_block(blk))]
                        tks = slice(blk * 128, (blk + 1) * 128)
                        bG, qG = PQ(d, 'G')
                        bK, qK = PQ(d, 'K')
                        bQ, qQ = PQ(d, 'Qp')
                        bL, qL = PQ(d, 'L', odd)
                        bLB, qLB = PQ(d, 'LB', odd)
                        bM, qM = PQ(d, 'M', odd)
                        ik = lambda nm: ('i', nm, d, bi % LOOK)
                        pkk = lambda nm: ('p', nm, d, bi % (LOOK + 1))
                        gcc = gc_tm[:, blk, col:col + 1]
                        bcol = beta_tm[:, blk, col:col + 1]
                        mneg = cst['mnegF'] if d == 0 else cst['mnegB']
                        md = cst['mdF'] if d == 0 else cst['mdB']
                        mo = cst['moF'] if d == 0 else cst['moB']
                        pstL = ps[bL][:].bitcast(BF16)[:, qL * 256:qL * 256 + 256]
                        pstW = ps[bM][:].bitcast(BF16)[:, qM * 256:qM * 256 + 128]
                        L0, Lo = I['LLo'][:, 0:128], I['LLo'][:, 128:256]

                        def m0():
                            TS('pool', P['kd'], Ktok[:, blk, :], egl_tm[:, blk, col:col + 1], None, ALU.mult, None,
                               [('Ktok', blk), 'egl_tm'], [pkk('kd')])
                            TS('pool', I['Kg'], Ktok[:, blk, :], egc_tm[:, blk, col:col + 1], None, ALU.mult, None,
                               [('Ktok', blk), 'egc_tm'], [ik('Kg')])

                        def m1():
                            MM(psq(bG, qG), selh[:, d, :], gcT[:, blk, :], ['selh', 'gcT'], pk(bG, qG, 1))
                            MM(psq(bK, qK), KT[:, tks], QT[:, tks], kq, pk(bK, qK, 1))
                            MM(psq(bK, qK + 1), KT[:, tks], KT[:, tks], kq, pk(bK, qK + 1, 1))

                        def m2():
                            STT('dve', I['X'], psq(bG, qG), gcc, mneg[:], ALU.subtract, ALU.add,
                                pk(bG, qG, 1) + ['gc_tm', 'mnegF', 'mnegB'], [ik('X')])
                            ACT(P['EG'], psq(bG, qG), AF.Exp, pk(bG, qG, 1), [pkk('EG')])

                        def m3():
                            if 'EG' in dbg_d and d == 0 and bi == 0 and h == hl[0]:
                                S.dma(dbg_d['EG'], P['EG'], [pkk('EG')], ['dbg_EG'])
                                S.dma(dbg_d['Xd'], I['X'], [ik('X')], ['dbg_Xd'])
                            ACT(I['N'], I['X'], AF.Exp, [ik('X')], [ik('N')])
                            TT('dve', P['qgT'], QT[:, tks], P['EG'], ALU.mult, kq + [pkk('EG')], [pkk('qgT')])

                        def m4():
                            STT('dve', I['Bns'], psq(bK, qK + 1), bcol, I['N'], ALU.mult, ALU.mult,
                                pk(bK, qK + 1, 1) + [ik('N'), 'beta_tm'], [ik('Bns')])
                            TT('dve', P['QKT'], psq(bK, qK), I['N'], ALU.mult, pk(bK, qK, 1) + [ik('N')], [pkk('QKT')])

                        def m5():
                            TT('dve', I['Bd'], I['Bns'], md[:], ALU.mult, [ik('Bns'), 'mdF', 'mdB'], [ik('Bd')])
                            TT('pool', I['Bo'], I['Bns'], mo[:], ALU.mult, [ik('Bns'), 'moF', 'moB'], [ik('Bo')])

                        def m6():
                            TT('dve', I['Q'][0], ident_bf[:], I['Bd'], ALU.subtract, [ik('Bd'), 'ident_bf'], [ik('Q0')])
                            TR(pstL[:, 0:128], I['Bd'], ident_bf[:], [ik('Bd'), 'ident_bf'], pk(bL, qL, 1))
                            TR(pstL[:, 128:256], I['Bo'], ident_bf[:], [ik('Bo'), 'ident_bf'], pk(bL, qL, 1))

                        def m7():
                            CP('act', I['LLo'], pstL, pk(bL, qL, 1), [ik('LLo')])

                        def levA(k):
                            def f():
                                if k == 1:
                                    Lp, Bp, kLp, kBp = L0, I['Bd'], ik('LLo'), ik('Bd')
                                else:
                                    prev = I['LB'][(k - 1) % 2]
                                    Lp, Bp = prev[:, 0:128], prev[:, 128:256]
                                    kLp = kBp = ik('LB%d' % ((k - 1) % 2))
                                MM(psq(bLB, qLB), Bp, Lp, [kLp, kBp], pk(bLB, qLB, 1))
                                if k < 3:
                                    MM(psq(bLB, qLB + 1), Lp, Bp, [kLp, kBp], pk(bLB, qLB + 1, 1))
                            return f

                        def levB(k):
                            def f():
                                cur = I['LB'][k % 2]
                                if k < 3:
                                    CP('act' if k == 1 else 'dve', cur[:, 0:256], psq(bLB, qLB, 2), pk(bLB, qLB, 2),
                                       [ik('LB%d' % (k % 2))])
                                else:
                                    CP('act', cur[:, 0:128], psq(bLB, qLB), pk(bLB, qLB, 1), [ik('LB%d' % (k % 2))])
                            return f

                        def levC(k):
                            def f():
                                cur = I['LB'][k % 2]
                                MM(psq(bQ, qQ), cur[:, 0:128], I['Q'][(k - 1) % 2],
                                   [ik('LB%d' % (k % 2)), ik('Q%d' % ((k - 1) % 2))], pk(bQ, qQ, 1))
                            return f

                        def levD(k):
                            def f():
                                TT('dve', I['Q'][k % 2], psq(bQ, qQ), I['Q'][(k - 1) % 2], ALU.add,
                                   pk(bQ, qQ, 1) + [ik('Q%d' % ((k - 1) % 2))], [ik('Q%d' % (k % 2))])
                            return f

                        def m20():
                            TdT = I['Q'][1]
                            MM(psq(bM, qM), Lo, TdT, [ik('LLo'), ik('Q1')], pk(bM, qM, 1))
                            MM(psq(bLB, qLB), TdT, Vtok[:, blk, :], [ik('Q1'), ('Vtok', blk)], pk(bLB, qLB, 1))
                            MM(psq(bLB, qLB + 1), TdT, I['Kg'], [ik('Q1'), ik('Kg')], pk(bLB, qLB + 1, 1))

                        def m21():
                            ACT(I['nMT'], psq(bM, qM), AF.Copy, pk(bM, qM, 1), [ik('nMT')], scale=-1.0)
                            CP('act', I['X0'], psq(bLB, qLB, 2), pk(bLB, qLB, 2), [ik('X0')])

                        def itA(n):
                            def f():
                                prev = I['X0'] if n == 1 else I['Xit'][(n - 1) % 2]
                                kprev = ik('X0') if n == 1 else ik('Xit%d' % ((n - 1) % 2))
                                MM(psq(bLB, qLB, 2), I['nMT'], prev[:, 0:256], [ik('nMT'), kprev], pk(bLB, qLB, 2))
                            return f

                        def itB(n):
                            def f():
                                TT('dve', I['Xit'][n % 2], psq(bLB, qLB, 2), I['X0'], ALU.add,
                                   pk(bLB, qLB, 2) + [ik('X0')], [ik('Xit%d' % (n % 2))])
                            return f

                        def m28():
                            X3 = I['Xit'][1]
                            ACT(P['Ub'], X3[:, 0:128], AF.Copy, [ik('Xit1'), 'beta_tm'], [pkk('Ub')], scale=bcol)
                            TR(pstW, X3[:, 128:256], ident_bf[:], [ik('Xit1'), 'ident_bf'], pk(bM, qM, 1))

                        def m29():
                            ACT(P['nwT'], pstW, AF.Copy, pk(bM, qM, 1), [pkk('nwT')], scale=-1.0)
                        return [m0, m1, m2, m3, m4, m5, m6, m7,
                                levA(1), levB(1), levC(1), levD(1), levA(2), levB(2), levC(2), levD(2),
                                levA(3), levB(3), levC(3), levD(3), m20, m21,
                                itA(1), itB(1), itA(2), itB(2), itA(3), itB(3), m28, m29]

                    def step_micro(d, s_):
                        c = orderC[d][s_]
                        blk, ci = c // 2, c % 2
                        bi = orderB[d].index(blk)
                        P = PSET[d, bi % (LOOK + 1)]
                        col = d * 8 + h
                        cs = slice(ci * 64, ci * 64 + 64)
                        bv, qv = PQ(d, 'v')
                        bo, qo = PQ(d, 'o')
                        bs, qs = PQ(d, 'dS')
                        pkk = lambda nm: ('p', nm, d, bi % (LOOK + 1))

                        def u0():
                            MM(psq(bv, qv), P['nwT'], Sbf[d][:], [pkk('nwT'), ('Sb', d)], pk(bv, qv, 1))

                        def u1():
                            STT('dve', vnew[d][ci][cs, :], ps[bv][cs, qv * 128:(qv + 1) * 128],
                                beta_tm[cs, blk, col:col + 1], P['Ub'][cs, :], ALU.mult, ALU.add,
                                pk(bv, qv, 1) + ['beta_tm', pkk('Ub')], [('vn', d, ci)])

                        def u2():
                            MM(psq(bs, qs), P['kd'], vnew[d][ci][:], [pkk('kd'), ('vn', d, ci)], pk(bs, qs, 1))
                            if blk >= 2:
                                MM(psq(bo, qo), P['qgT'], Sbf[d][:], [pkk('qgT'), ('Sb', d)], pk(bo, qo, 1),
                                   start=True, stop=False)
                                MM(psq(bo, qo), P['QKT'], vnew[d][ci][:], [pkk('QKT'), ('vn', d, ci)], pk(bo, qo, 1),
                                   start=False, stop=True)

                        def u3():
                            lastcol = ci * 64 + (63 if d == 0 else 0)
                            STT('dve', Sf32[d][:], Sf32[d][:], P['EG'][:, lastcol:lastcol + 1], psq(bs, qs), ALU.mult,
                                ALU.add, [('S', d), pkk('EG')] + pk(bs, qs, 1), [('S', d)])
                            if blk >= 2:
                                okey = ('O', blk, ci)
                                src = ps[bo][cs, qo * 128:(qo + 1) * 128]
                                if okey not in owritten:
                                    owritten.add(okey)
                                    CP('act', Obuf[cs, blk - 2, :], src, pk(bo, qo, 1), [okey])
                                else:
                                    TT('dve', Obuf[cs, blk - 2, :], src, Obuf[cs, blk - 2, :], ALU.add,
                                       pk(bo, qo, 1) + [okey], [okey])

                        def u4():
                            CP('act', Sbf[d][:], Sf32[d][:], [('S', d)], [('Sb', d)])
                        return [u0, u1, u2, u3, u4]

                    NBK = len(orderB[0])
                    smic = {}
                    for g in range(-10 * LOOK, 5 * 68):
                        if g >= 0:
                            s_, u = g // 5, g % 5
                            for d in range(2):
                                if u == 0:
                                    smic[d] = step_micro(d, s_)
                                smic[d][u]()
                        for bi in range(NBK):
                            g0 = 10 * (bi - LOOK)
                            if g0 <= g < g0 + 30:
                                m = g - g0
                                for d in range(2):
                                    key = ('setup', d, bi)
                                    if key not in smic:
                                        smic[key] = setup_micro(d, bi)
                                    smic[key][m]()
                    S.barrier()
                    if 'O' in dbg_d and h == list(heads)[0]:
                        S.dma(dbg_d['O'], Obuf[:], [('O', b, ci) for b in range(2, 34) for ci in range(2)], ['dbg_O'])
                    if 'Sfin' in dbg_d and h == list(heads)[0]:
                        S.dma(dbg_d['Sfin'][0], Sf32[0][:], [('S', 0)], ['dbg_S0'])
                        S.dma(dbg_d['Sfin'][1], Sf32[1][:], [('S', 1)], ['dbg_S1'])

                    OK_ALL = lambda b: [('O', b + 2, 0), ('O', b + 2, 1)]
                    for b in range(32):
                        ACT(sqb[:, 0:128], Obuf[:, b, :], AF.Square, OK_ALL(b), ['sqb', 'ssq'], accum=ssq[:, b:b + 1])
                    TS('dve', ssq[:], ssq[:], 1.0 / 128, EPS, ALU.mult, ALU.add, ['ssq'], ['ssq'])
                    TT('pool', rstd_o[:], ssq[:], mhalf[:, 0:32], ALU.pow, ['ssq', 'mhalf'], ['rstd_o'])
                    for tc in range(8):
                        bank = 6 + tc % 2
                        t0 = NCTX + tc * 512
                        for kc in range(8):
                            MM(ps[bank][:, :], wdg[:, kc, :], hT[:, kc, t0:t0 + 512], [('hT', 1 + tc), 'wdg'], pk(bank),
                               start=(kc == 0), stop=(kc == 7))
                        ACT(X2[:, tc * 512:(tc + 1) * 512], ps[bank][:, :], AF.Silu, pk(bank), ['X2'])
                    for tc in range(8):
                        bank = 6 + tc % 2
                        pst = ps[bank][:].bitcast(BF16)
                        for bi in range(4):
                            b = tc * 4 + bi
                            STT('dve', Onb[bi][:], Obuf[:, b, :], rstd_o[:, b:b + 1], normw[:], ALU.mult, ALU.mult,
                                OK_ALL(b) + ['rstd_o', 'normw'], [('Onb', bi)])
                            TR(pst[:, bi * 128:(bi + 1) * 128], Onb[bi][:], ident_bf[:], [('Onb', bi), 'ident_bf'], pk(bank))
                        TT('dve', X1[:, tc * 512:(tc + 1) * 512], pst[:, 0:512], X2[:, tc * 512:(tc + 1) * 512], ALU.mult,
                           pk(bank) + ['X2'], ['X1'])
                    S.dma(og_s[h], X1[:, 0:NLAT], ['X1'], [('og_s', h)], semkey='og_s', waitall=True, eng='pool')
                    S.barrier()
            S.barrier()
        if 'fo' in phases:
            with ExitStack() as es3:
                def sb3(name, shape, dt=F32):
                    return es3.enter_context(nc.sbuf_tensor('sb_' + name, list(shape), dt))
                load_consts(('ccsc',), sb3, 'consts_fo')
                c64 = sb3('c64', [64, 128], BF16)
                rr = sb3('rr', [64, 64, 2, 128], BF16)
                S.dma(c64[:], dr['c64'], [], ['c64'])
                S.dma(rr[:], dr['rr'], [], ['rr'])
                wfv = sb3('wfv', [128, 8, 512], BF16)
                wfg = sb3('wfg', [128, 8, 512], BF16)
                wfm = sb3('wfm', [128, 4, 128], BF16)
                load_cast(wfv[:], win_d[:, :, OFF[0]:OFF[0] + 512], 8, 512, 'wfv')
                load_cast(wfg[:], win_d[:, :, OFF[1]:OFF[1] + 512], 8, 512, 'wfg')
                load_cast(wfm[:], wfm_d, 4, 128, 'wfm')
                uA = sb3('uA', [64, 64, 256], BF16)
                Y = sb3('Y', [64, 128, 2, 64], BF16)
                PT = sb3('PT', [128, 2, NLAT], BF16)
                G = sb3('G', [128, 256], BF16)
                SG = [sb3('SG0', [128, 512], BF16)] * 2
                FMc = [sb3('FMc0', [128, 512], BF16)] * 2
                LAT = [('hT', 1 + tc) for tc in range(8)]
                for g in range(4):
                    if g % 2 == 0:
                        for n0 in range(0, 64, 2):
                            bank = (n0 // 2) % 2
                            for a in range(2):
                                n2 = n0 + a
                                for kc in range(8):
                                    MM(ps[bank][0:64, a * 256:(a + 1) * 256], hT[:, kc, NCTX + n2:NCTX + NLAT:64],
                                       wfv[:, kc, g * 128:g * 128 + 256], LAT + ['wfv'], pk(bank),
                                       start=(kc == 0), stop=(kc == 7))
                            CP('act' if bank == 0 else 'dve', uA[:, n0:n0 + 2, :],
                               ps[bank][0:64, :].rearrange("p (a c) -> p a c", c=256), pk(bank), ['uA'])
                    go = (g % 2) * 128
                    for c0 in range(0, 128, 4):
                        bank = 2 + (c0 // 4) % 2
                        for a in range(4):
                            MM(ps[bank][0:64, a * 128:(a + 1) * 128], uA[:, :, go + c0 + a], c64[:, :], ['uA', 'c64'], pk(bank))
                        CP('act' if bank == 2 else 'dve', Y[:, c0:c0 + 4, :, :],
                           ps[bank][0:64, :].rearrange("p (c r k) -> p c r k", c=4, r=2), pk(bank), ['Y'])
                    if g == 0:
                        if 'uA' in dbg_d:
                            S.dma(dbg_d['uA'], uA[:, :, 0:128], ['uA'], ['dbg_uA'])
                    PTv = PT[:].rearrange("p r (k2 k1) -> p r k2 k1", k1=64)
                    for k0 in range(0, 64, 4):
                        bank = 4 + (k0 // 4) % 2
                        for a in range(4):
                            k1 = k0 + a
                            MM(ps[bank][:, a * 128:(a + 1) * 128], Y[:, :, 0, k1], rr[:, k1, 0, :], ['Y', 'rr'], pk(bank),
                               start=True, stop=False)
                            MM(ps[bank][:, a * 128:(a + 1) * 128], Y[:, :, 1, k1], rr[:, k1, 1, :], ['Y', 'rr'], pk(bank),
                               start=False, stop=True)
                        CP('act' if bank == 4 else 'dve',
                           PTv[:, :, :, k0:k0 + 4].rearrange("p r k a -> p a r k"),
                           ps[bank][:, :].rearrange("p (a r k) -> p a r k", a=4, r=2), pk(bank), ['PT'])
                    if g == 0 and 'PT' in dbg_d:
                        S.dma(dbg_d['PT'], PT[:], ['PT'], ['dbg_PT'])
                    MM(ps[6][:, 0:128], cst['ccsc'][:, 0:128], wfm[:, g, :], ['ccsc', 'wfm'], pk(6))
                    MM(ps[6][:, 128:256], cst['ccsc'][:, 128:256], wfm[:, g, :], ['ccsc', 'wfm'], pk(6))
                    CP('dve', G[:], ps[6][:, 0:256], pk(6), ['G'])
                    for tc in range(8):
                        i = tc % 2
                        t0 = NCTX + tc * 512
                        for kc in range(8):
                            MM(ps[7][:, :], wfg[:, kc, g * 128:(g + 1) * 128], hT[:, kc, t0:t0 + 512],
                               [('hT', 1 + tc), 'wfg'], pk(7), start=(kc == 0), stop=(kc == 7))
                        ACT(SG[i][:], ps[7][:, :], AF.Silu, pk(7), [('SG', 0)])
                        MM(ps[6][:, :], G[:, 0:128], PT[:, 0, tc * 512:(tc + 1) * 512], ['G', 'PT'], pk(6),
                           start=True, stop=False)
                        MM(ps[6][:, :], G[:, 128:256], PT[:, 1, tc * 512:(tc + 1) * 512], ['G', 'PT'], pk(6),
                           start=False, stop=True)
                        TT('dve', FMc[i][:], ps[6][:, :], SG[i][:], ALU.mult, pk(6) + [('SG', 0)], [('FMc', 0)])
                        if 'fm' in dbg_d:
                            S.dma(dbg_d['fm'][g, :, tc * 512:(tc + 1) * 512], FMc[i][:], [('FMc', 0)], ['dbg_fm'])
                        S.dma(fm_s[g, :, tc * 512:(tc + 1) * 512], FMc[i][:], [('FMc', 0)], [('fm_s', g, tc)], semkey='fm_s', waitall=True, eng='pool')
            S.barrier()

        if 'fin' in phases:
            wout = sb('wout', [128, 8, D], BF16)
            with ExitStack() as es4:
                def sb4(name, shape, dt=F32):
                    return es4.enter_context(nc.sbuf_tensor('sb_' + name, list(shape), dt))
                wrf = sb4('wrf', [128, 8, D], BF16)
                wrd = sb4('wrd', [128, 8, D], BF16)
                wfo = sb4('wfo', [128, 4, D], BF16)
                wdn = sb4('wdn', [128, 8, D], BF16)
                for cb in range(4):
                    cs_ = slice(cb * 256, (cb + 1) * 256)
                    S.dma(wrf[:, :, cs_], win_d[:, :, OFF[6] + cb * 256:OFF[6] + (cb + 1) * 256], [], [('wrf', cb)], eng='pool')
                    S.dma(wrd[:, :, cs_], win_d[:, :, OFF[7] + cb * 256:OFF[7] + (cb + 1) * 256], [], [('wrd', cb)], eng='pool')
                    S.dma(wfo[:, :, cs_], wfo_d[:, :, cs_], [], [('wfo', cb)], eng='pool')
                    S.dma(wdn[:, :, cs_], wdn_d[:, :, cs_], [], [('wdn', cb)], eng='pool')
                load_cast(wout[:], wout_d, 8, D, 'wout')
                if 'wrf' in dbg_d:
                    S.dma(dbg_d['wrf'], wrf[:], [('wrf', cb) for cb in range(4)], ['dbg_wrf'])
                    S.dma(dbg_d['hT2'], hT[:, :, NCTX:NCTX + 512], ALLHT, ['dbg_hT2'])
                ogc = [sb4('ogc%d' % i, [128, 8, 512], BF16) for i in range(2)]
                fmc = [sb4('fmc%d' % i, [128, 4, 512], BF16) for i in range(2)]
                sgf = [sb4('sgf%d' % i, [128, 512], BF16) for i in range(2)]
                sgd = [sb4('sgd%d' % i, [128, 512], BF16) for i in range(2)]
                m1 = [sb4('m1_%d' % i, [128, 512], BF16) for i in range(2)]
                m2 = [sb4('m2_%d' % i, [128, 512], BF16) for i in range(2)]
                mgo = [sb4('mgo%d' % i, [128, 512], BF16) for i in range(2)]
                for tc in range(8):
                    ci = tc % 2
                    t0 = NCTX + tc * 512
                    if 'dn' in phases:
                        S.dma(ogc[ci][:], og_s[:, :, tc * 512:(tc + 1) * 512].rearrange("h p t -> p h t"),
                              [('og_s', h) for h in range(H)], [('ogc', ci)])
                    else:
                        MEMSET('pool', ogc[ci][:], 0.0, [('ogc', ci)])
                    if 'fo' in phases:
                        S.dma(fmc[ci][:], fm_s[:, :, tc * 512:(tc + 1) * 512].rearrange("g p t -> p g t"),
                              [('fm_s', g, tc) for g in range(4)], [('fmc', ci)])
                    else:
                        MEMSET('pool', fmc[ci][:], 0.0, [('fmc', ci)])
                    for fc in range(8):
                        i = fc % 2
                        bb = 4 * i
                        fs = slice(fc * 128, (fc + 1) * 128)
                        for kc in range(8):
                            MM(ps[bb][:, :], wrf[:, kc, fs], hT[:, kc, t0:t0 + 512], [('hT', 1 + tc), ('wrf', fc // 2)], pk(bb),
                               start=(kc == 0), stop=(kc == 7))
                        for kc in range(8):
                            MM(ps[bb + 1][:, :], wrd[:, kc, fs], hT[:, kc, t0:t0 + 512], [('hT', 1 + tc), ('wrd', fc // 2)], pk(bb + 1),
                               start=(kc == 0), stop=(kc == 7))
                        for g in range(4):
                            MM(ps[bb + 2][:, :], wfo[:, g, fs], fmc[ci][:, g, :], [('fmc', ci), ('wfo', fc // 2)], pk(bb + 2),
                               start=(g == 0), stop=(g == 3))
                        for h in range(H):
                            MM(ps[bb + 3][:, :], wdn[:, h, fs], ogc[ci][:, h, :], [('ogc', ci), ('wdn', fc // 2)], pk(bb + 3),
                               start=(h == 0), stop=(h == 7))
                        ACT(sgf[i][:], ps[bb][:, :], AF.Sigmoid, pk(bb), [('sgf', i)])
                        ACT(sgd[i][:], ps[bb + 1][:, :], AF.Sigmoid, pk(bb + 1), [('sgd', i)])
                        TT('dve', m1[i][:], ps[bb + 2][:, :], sgf[i][:], ALU.mult, pk(bb + 2) + [('sgf', i)], [('m1', i)])
                        TT('dve', m2[i][:], ps[bb + 3][:, :], sgd[i][:], ALU.mult, pk(bb + 3) + [('sgd', i)], [('m2', i)])
                        TT('dve', mgo[i][:], m1[i][:], m2[i][:], ALU.add, [('m1', i), ('m2', i)], [('mgo', i)])
                        S.dma(mg_s[fc, :, tc * 512:(tc + 1) * 512], mgo[i][:], [('mgo', i)], [('mg_s', tc)], semkey='mg_s', waitall=True, eng='pool')
                        if 'taps' in dbg_d and tc == 0:
                            S.dma(dbg_d['taps'][0, fc], sgf[i][:], [('sgf', i)], ['dbg_t0'])
                            S.dma(dbg_d['taps'][1, fc], sgd[i][:], [('sgd', i)], ['dbg_t1'])
                            S.dma(dbg_d['taps'][2, fc], m1[i][:], [('m1', i)], ['dbg_t2'])
                            S.dma(dbg_d['taps'][3, fc], m2[i][:], [('m2', i)], ['dbg_t3'])
                        if 'merged' in dbg_d and tc == 0:
                            S.dma(dbg_d['merged'][fc], mgo[i][:], [('mgo', i)], ['dbg_mg'])
            S.barrier()
            with ExitStack() as es5:
                def sb5(name, shape, dt=F32):
                    return es5.enter_context(nc.sbuf_tensor('sb_' + name, list(shape), dt))
                lng = sb5('lng', [128, D])
                lnb = sb5('lnb', [128, D])
                S.dma(lng[:], lng_d, [], ['lng'])
                S.dma(lnb[:], lnb_d, [], ['lnb'])
                mg = [sb5('mg%d' % i, [128, 8, 512], BF16) for i in range(2)]
                NBF = 4
                xt = [sb5('fxt%d' % i, [128, D]) for i in range(NBF)]
                pet = [sb5('fpet%d' % i, [128, D]) for i in range(NBF)]
                pre = [sb5('pre%d' % i, [128, D]) for i in range(NBF)]
                sq = sb5('fsq', [128, D], BF16)
                st = [sb5('fst%d' % i, [128, 8]) for i in range(8)]

                def fA(ti):
                    tc, tt = ti // 4, ti % 4
                    ci = tc % 2
                    j = ti % NBF
                    xk, pk_, prk, sk = ('fxt', j), ('fpet', j), ('pre', j), ('fst', ti % 8)
                    stt = st[ti % 8]
                    if tt == 0:
                        S.dma(mg[ci][:], mg_s[:, :, tc * 512:(tc + 1) * 512].rearrange("f p t -> p f t"),
                              [('mg_s', tc)], [('mg', ci)])
                    S.dma(xt[j][:], x_d[ti * 128:(ti + 1) * 128, :], [], [xk])
                    S.dma(pet[j][:], dr['pe'][ti * 128:(ti + 1) * 128, :], [], [pk_])
                    TT('dve', xt[j][:], xt[j][:], pet[j][:], ALU.add, [xk, pk_], [xk])
                    for half in range(2):
                        bank = 2 * (ti % 4) + half
                        for kc in range(8):
                            MM(ps[bank][:, :], mg[ci][:, kc, tt * 128:(tt + 1) * 128],
                               wout[:, kc, half * 512:(half + 1) * 512], [('mg', ci), 'wout'], pk(bank),
                               start=(kc == 0), stop=(kc == 7))
                        TT('dve', pre[j][:, half * 512:(half + 1) * 512], ps[bank][:, :],
                           gxb[:, half * 512:(half + 1) * 512], ALU.mult, pk(bank) + ['gxb'], [prk])
                    STT('dve', pre[j][:], xt[j][:], ALPHA, pre[j][:], ALU.mult, ALU.add, [xk, prk], [prk])
                    RED(stt[:, 0:1], pre[j][:], [prk], [sk])
                    ACT(sq[:], pre[j][:], AF.Square, [prk], ['fsq', sk], accum=stt[:, 1:2])

                def fB(ti):
                    sk = ('fst', ti % 8)
                    stt = st[ti % 8]
                    TS('dve', stt[:, 2:3], stt[:, 0:1], 1.0 / D, None, ALU.mult, None, [sk], [sk])
                    TT('dve', stt[:, 3:4], stt[:, 2:3], stt[:, 2:3], ALU.mult, [sk], [sk])
                    STT('dve', stt[:, 4:5], stt[:, 1:2], 1.0 / D, stt[:, 3:4], ALU.mult, ALU.subtract, [sk], [sk])
                    TS('dve', stt[:, 7:8], stt[:, 4:5], EPS, None, ALU.add, None, [sk], [sk])
                    TT('pool', stt[:, 5:6], stt[:, 7:8], mhalf[:, 0:1], ALU.pow, [sk, 'mhalf'], [sk])

                def fC(ti):
                    j = ti % NBF
                    prk, sk = ('pre', j), ('fst', ti % 8)
                    stt = st[ti % 8]
                    STT('dve', stt[:, 6:7], stt[:, 2:3], -1.0, stt[:, 5:6], ALU.mult, ALU.mult, [sk], [sk])
                    ACT(pre[j][:], pre[j][:], AF.Identity, [prk, sk], [prk], scale=stt[:, 5:6], bias=stt[:, 6:7])
                    TT('dve', pre[j][:], pre[j][:], lng[:], ALU.mult, [prk, 'lng'], [prk])
                    TT('dve', pre[j][:], pre[j][:], lnb[:], ALU.add, [prk, 'lnb'], [prk])
                    S.dma(out_d[ti * 128:(ti + 1) * 128, :], pre[j][:], [prk], [('out', ti)], semkey='out', eng='pool')

                for n in range(32 + 2):
                    if n < 32:
                        fA(n)
                    if 0 <= n - 1 < 32:
                        fB(n - 1)
                    if 0 <= n - 2 < 32:
                        fC(n - 2)
        S.emit()
    return nc, consts


_CACHE = {}


def _prep_shared(inputs):
    f = np.float32
    w_in = np.asarray(inputs['w_in'], f)[0]
    sh = {}
    sh['w_mod'] = np.ascontiguousarray(np.asarray(inputs['w_mod'], f)[0].reshape(128, 8, 3 * D))
    sh['b_mod2'] = np.ascontiguousarray(np.tile(np.asarray(inputs['b_mod'], f)[0][None], (2, 1)))
    sh['w_in_r'] = np.ascontiguousarray(w_in.reshape(8, 128, INCOLS).transpose(1, 0, 2))
    sh['w_dn_r'] = np.ascontiguousarray(np.asarray(inputs['w_dn_out'], f)[0].reshape(8, 128, D).transpose(1, 0, 2))
    sh['w_out_r'] = np.ascontiguousarray(np.asarray(inputs['w_out'], f)[0].reshape(8, 128, D).transpose(1, 0, 2))
    sh['w_fo_r'] = np.ascontiguousarray(np.asarray(inputs['w_f_out'], f)[0].reshape(4, 128, D).transpose(1, 0, 2))
    sh['w_fmix_r'] = np.ascontiguousarray(np.asarray(inputs['w_fmix'], f)[0].transpose(1, 0, 2))
    sh['convw_r'] = np.ascontiguousarray(np.asarray(inputs['conv_w'], f)[0].T.reshape(24, 128, 3).transpose(1, 0, 2))
    sh['alog_t'] = np.ascontiguousarray(np.tile(np.asarray(inputs['a_log'], f)[0].reshape(1, 16), (128, 1)))
    sh['dtb_t'] = np.ascontiguousarray(np.tile(np.asarray(inputs['dt_bias'], f)[0].reshape(1, 16), (128, 1)))
    sh['normw_t'] = np.ascontiguousarray(np.tile(np.asarray(inputs['dn_norm_w'], f)[0].reshape(1, 128), (128, 1)))
    sh['lng_t'] = np.ascontiguousarray(np.tile(np.asarray(inputs['ln_g'], f)[0].reshape(1, D), (128, 1)))
    sh['lnb_t'] = np.ascontiguousarray(np.tile(np.asarray(inputs['ln_b'], f)[0].reshape(1, D), (128, 1)))
    return sh


def make_in_maps(inputs, consts, cores):
    f = np.float32
    sh = _prep_shared(inputs)
    x = np.asarray(inputs['x'], f)
    ctx = np.asarray(inputs['ctx'], f)
    c = np.asarray(inputs['c'], f)
    c_ctx = np.asarray(inputs['c_ctx'], f)
    maps = []
    for b in cores:
        m = dict(consts)
        m.update(sh)
        m['x'] = np.ascontiguousarray(x[b])
        m['ctx'] = np.ascontiguousarray(ctx[b])
        m['cc'] = np.ascontiguousarray(np.stack([c[b].reshape(128, 8), c_ctx.reshape(128, 8)], -1).reshape(128, 16))
        maps.append(m)
    return maps


def kernel(**inputs):
    if 'nc' not in _CACHE:
        _CACHE['nc'] = build()
    nc, consts = _CACHE['nc']
    maps = make_in_maps(inputs, consts, range(8))
    res = run_bass_kernel_spmd(nc, maps, core_ids=list(range(8)))
    out = np.stack([np.asarray(r['out'], np.float32) for r in res.results], 0)
    return out
```
